# Optimizing a Trainium2 kernel written in Bass

```python
import math
import jax
import jax.numpy as jnp
from jax import lax
import numpy as np

D_MODEL = 4096
BATCH = 4
SEQ = 2048
DEPTH = 2
DEC_BATCH = 32
DEC_SEQ = 4
PAST_LEN = 16384
PAGE_SIZE = 128

HEAD_DIM = 128
H_A = 20
KV_A = 4
H_B = 12
H_ATT = H_A + H_B
WINDOW_A = 128
DILATED_BRANCHES = ((128, 1), (512, 4), (2048, 16))
WINDOW_B_MAX = 2048
BAND = 128
N_BUCKETS = 32
RELPOS_MAX_DIST = 2048
ATT_COLS = (H_A * HEAD_DIM, KV_A * HEAD_DIM, KV_A * HEAD_DIM, H_B * HEAD_DIM, H_B * HEAD_DIM, H_B * HEAD_DIM)

RW_HEAD = 64
H_C = D_MODEL // RW_HEAD
LORA_DECAY = 128
LORA_AAA = 128
LORA_GATE = 480
GN_EPS = 64e-5

PEER_HEADS = 8
PEER_DKEY = 256
PEER_NKEYS = 128
PEER_EXPERTS = PEER_NKEYS * PEER_NKEYS
PEER_TOPK = 16
PEER_BLOCK = 128

LN_EPS = 1e-5
DEEPNORM_ALPHA = (2.0 * DEPTH) ** 0.25
DEEPNORM_BETA = (8.0 * DEPTH) ** -0.25

kernel_name = 'hybrid_swa_dilated_rwkv7_peer_step'


def layer_norm(x, g, b):
    xf = x.astype(jnp.float32)
    mu = jnp.mean(xf, axis=-1, keepdims=True)
    var = jnp.mean(jnp.square(xf - mu), axis=-1, keepdims=True)
    return ((xf - mu) * lax.rsqrt(var + LN_EPS) * g + b).astype(x.dtype)


def relpos_bucket(dist):
    n = jnp.maximum(jnp.asarray(dist, jnp.int32), 0)
    exact = N_BUCKETS // 2
    nf = jnp.maximum(n, 1).astype(jnp.float32)
    large = exact + (jnp.log(nf / exact) / math.log(RELPOS_MAX_DIST / exact) * (N_BUCKETS - exact)).astype(jnp.int32)
    return jnp.where(n < exact, n, jnp.minimum(large, N_BUCKETS - 1))


def relpos_bias(tab, dist):
    return jnp.moveaxis(tab[relpos_bucket(dist)].astype(jnp.float32), -1, 0)


def softmax_lse(s, sink=None):
    if sink is not None:
        s = jnp.concatenate([s, jnp.broadcast_to(sink.astype(jnp.float32), s.shape[:-1] + (1,))], axis=-1)
    lse = jax.nn.logsumexp(s, axis=-1)
    p = jnp.exp(s - lse[..., None])
    if sink is not None:
        p = p[..., :-1]
    return p, lse


def band_rows(x):
    b, l = x.shape[:2]
    nb = l // BAND
    pad = [(0, 0), (BAND, 0)] + [(0, 0)] * (x.ndim - 2)
    xb = jnp.pad(x, pad).reshape((b, nb + 1, BAND) + x.shape[2:])
    return jnp.concatenate([xb[:, :-1], xb[:, 1:]], axis=2)


def banded_attention(q, k, v, tab, dil, sink):
    b, l, h, hd = q.shape
    kvh = k.shape[2]
    rep = h // kvh
    lp = -(-l // BAND) * BAND
    nb = lp // BAND

    def pad_tail(z):
        return jnp.pad(z, [(0, 0), (0, lp - l)] + [(0, 0)] * (z.ndim - 2))

    qb = pad_tail(q).reshape(b, nb, BAND, kvh, rep, hd)
    kb = band_rows(pad_tail(k))
    vb = band_rows(pad_tail(v))
    sub = np.arange(BAND)[:, None] + BAND - np.arange(2 * BAND)[None, :]
    in_win = (sub >= 0) & (sub <= BAND)
    mask = in_win[None] & ((np.arange(nb)[:, None, None] > 0) | (np.arange(2 * BAND) >= BAND)[None, None, :])
    s = jnp.einsum('bnqgrd,bnkgd->bngrqk', qb, kb, preferred_element_type=jnp.float32) * hd ** -0.5
    s = s + relpos_bias(tab, dil * sub).reshape(kvh, rep, BAND, 2 * BAND)
    s = jnp.where(mask[None, :, None, None], s, -jnp.inf)
    p, lse = softmax_lse(s, None if sink is None else sink.reshape(kvh, rep, 1, 1))
    o = jnp.einsum('bngrqk,bnkgd->bnqgrd', p.astype(v.dtype), vb).reshape(b, lp, h, hd)[:, :l]
    lse = lse.transpose(0, 1, 4, 2, 3).reshape(b, lp, h)[:, :l]
    return o, lse


def merge_branches(outs, lses):
    wts = jax.nn.softmax(jnp.stack(lses), axis=0)
    o = jnp.sum(wts[..., None] * jnp.stack(outs).astype(jnp.float32), axis=0)
    return o.astype(outs[0].dtype)


def dilated_prompt(q, k, v, tab):
    b, l = q.shape[:2]
    outs, lses = [], []
    for (_, d) in DILATED_BRANCHES:
        def to_sub(a):
            return a.reshape((b, l // d, d) + a.shape[2:]).swapaxes(1, 2).reshape((b * d, l // d) + a.shape[2:])

        def from_sub(a):
            return a.reshape((b, d, l // d) + a.shape[2:]).swapaxes(1, 2).reshape((b, l) + a.shape[2:])

        o, lse = banded_attention(to_sub(q), to_sub(k), to_sub(v), tab, d, None)
        outs.append(from_sub(o))
        lses.append(from_sub(lse))
    return merge_branches(outs, lses)


def window_attention_sample(q, k_new, v_new, k_buf, v_buf, tab, sink):
    b, s_len, h, hd = q.shape
    l = k_buf.shape[1]
    kvh = k_new.shape[2]
    rep = h // kvh
    kall = jnp.concatenate([k_buf.astype(k_new.dtype), k_new], axis=1)
    vall = jnp.concatenate([v_buf.astype(v_new.dtype), v_new], axis=1)
    dist = (l + np.arange(s_len))[:, None] - np.arange(l + s_len)[None, :]
    mask = (dist >= 0) & (dist <= WINDOW_A)
    qg = q.reshape(b, s_len, kvh, rep, hd)
    s = jnp.einsum('bqgrd,bkgd->bgrqk', qg, kall, preferred_element_type=jnp.float32) * hd ** -0.5
    s = s + relpos_bias(tab, dist).reshape(kvh, rep, s_len, l + s_len)
    s = jnp.where(mask, s, -jnp.inf)
    p, _ = softmax_lse(s, sink.reshape(kvh, rep, 1, 1))
    o = jnp.einsum('bgrqk,bkgd->bqgrd', p.astype(vall.dtype), vall).reshape(b, s_len, h, hd)
    return o, kall[:, -l:], vall[:, -l:]


def dilated_sample(q, k_new, v_new, k_buf, v_buf, tab):
    b, s_len, h, hd = q.shape
    l = k_buf.shape[1]
    kall = jnp.concatenate([k_buf.astype(k_new.dtype), k_new], axis=1)
    vall = jnp.concatenate([v_buf.astype(v_new.dtype), v_new], axis=1)
    outs, lses = [], []
    for (w, d) in DILATED_BRANCHES:
        j = np.arange(w // d + 1)
        idx = (l + np.arange(s_len))[:, None] - d * j[None, :]
        valid = idx >= 0
        idxc = np.maximum(idx, 0)
        kg = kall[:, idxc]
        vg = vall[:, idxc]
        s = jnp.einsum('bqhd,bqjhd->bhqj', q, kg, preferred_element_type=jnp.float32) * hd ** -0.5
        s = s + relpos_bias(tab, d * j)[:, None, :]
        s = jnp.where(valid[None, None], s, -jnp.inf)
        p, lse = softmax_lse(s, None)
        outs.append(jnp.einsum('bhqj,bqjhd->bqhd', p.astype(vg.dtype), vg))
        lses.append(lse.transpose(0, 2, 1))
    return merge_branches(outs, lses), kall[:, -l:], vall[:, -l:]


def attn_projections(x, w_in_att):
    b, t, _ = x.shape
    p = jnp.einsum('btd,de->bte', x, w_in_att)
    parts = jnp.split(p, np.cumsum(ATT_COLS)[:-1].tolist(), axis=-1)
    return tuple(z.reshape(b, t, -1, HEAD_DIM) for z in parts)


def attn_output(oa, ob, w_out_att):
    b, t = oa.shape[:2]
    o = jnp.concatenate([oa, ob], axis=2).reshape(b, t, H_ATT * HEAD_DIM)
    return jnp.einsum('bte,ed->btd', o, w_out_att)


def wkv7_scan(s0, r, w, k, v, a, bvec):
    def step(s, inp):
        r_t, w_t, k_t, v_t, a_t, b_t = inp
        sa = jnp.einsum('bhvk,bhk->bhv', s, a_t)
        s = s * w_t[:, :, None, :] + sa[..., None] * b_t[:, :, None, :] + v_t[..., None] * k_t[:, :, None, :]
        return s, jnp.einsum('bhvk,bhk->bhv', s, r_t)
    seq = tuple(jnp.swapaxes(z, 0, 1) for z in (r, w, k, v, a, bvec))
    s_fin, y = lax.scan(step, s0, seq)
    return jnp.swapaxes(y, 0, 1), s_fin


def rwkv7_mixer(x, x_prev, s0, mu, w_rkv, w0, w1, w2, a0, a1, a2, g1, g2, k_k, k_a, r_k, gn_g, gn_b, w_o):
    b, t, d = x.shape
    f32 = jnp.float32
    xx = jnp.concatenate([x_prev[:, None, :].astype(x.dtype), x[:, :-1]], axis=1) - x
    xr, xw, xk, xv, xa, xg = (x + xx * mu[i] for i in range(6))
    r, k, v = jnp.einsum('cbtd,cde->cbte', jnp.stack([xr, xk, xv]), w_rkv)
    w_log = -jax.nn.softplus(-(w0 + jnp.tanh(xw @ w1) @ w2).astype(f32)) - 0.5
    decay = jnp.exp(-jnp.exp(w_log))
    a = jax.nn.sigmoid((a0 + (xa @ a1) @ a2).astype(f32))
    g = jax.nn.sigmoid(xg @ g1) @ g2

    def heads(z):
        return z.reshape(b, t, H_C, RW_HEAD)

    kf = k.astype(f32)
    kk = heads(kf * k_k)
    kk = kk / jnp.maximum(jnp.sqrt(jnp.sum(kk * kk, axis=-1, keepdims=True)), 1e-12)
    kf = heads(kf * (1.0 + (a - 1.0) * k_a))
    rf = heads(r.astype(f32))
    vf = heads(v.astype(f32))
    y, s_fin = wkv7_scan(s0.astype(f32), rf, heads(decay), kf, vf, -kk, kk * heads(a))
    mean = jnp.mean(y, axis=-1, keepdims=True)
    var = jnp.mean(jnp.square(y - mean), axis=-1, keepdims=True)
    y = (y - mean) * lax.rsqrt(var + GN_EPS) * gn_g.reshape(H_C, RW_HEAD) + gn_b.reshape(H_C, RW_HEAD)
    y = y + jnp.sum(rf * kf * r_k, axis=-1, keepdims=True) * vf
    y = y.reshape(b, t, d).astype(x.dtype)
    return (y * g) @ w_o, x[:, -1], s_fin.astype(x.dtype)


def peer_ffn(x, w_q, sub_keys, u_tab, v_tab):
    b, t, d = x.shape
    n = b * t
    npad = -(-n // PEER_BLOCK) * PEER_BLOCK
    xf = jnp.pad(x.reshape(n, d), ((0, npad - n), (0, 0)))

    def block(xb):
        q = (xb @ w_q).reshape(PEER_BLOCK, PEER_HEADS, 2, PEER_DKEY // 2)
        s = jnp.einsum('thcd,hcnd->thcn', q, sub_keys, preferred_element_type=jnp.float32)
        sv, si = lax.top_k(s, PEER_TOPK)
        cand = sv[:, :, 0, :, None] + sv[:, :, 1, None, :]
        cv, ci = lax.top_k(cand.reshape(PEER_BLOCK, PEER_HEADS, PEER_TOPK * PEER_TOPK), PEER_TOPK)
        i1 = jnp.take_along_axis(si[:, :, 0], ci // PEER_TOPK, axis=-1)
        i2 = jnp.take_along_axis(si[:, :, 1], ci % PEER_TOPK, axis=-1)
        idx = i1 * PEER_NKEYS + i2
        gate = jax.nn.softmax(cv, axis=-1)
        hid = jnp.einsum('thkd,td->thk', u_tab[idx], xb, preferred_element_type=jnp.float32)
        coef = (gate * jax.nn.gelu(hid, approximate=False)).astype(xb.dtype)
        return jnp.einsum('thk,thkd->td', coef, v_tab[idx])

    y = lax.map(block, xf.reshape(npad // PEER_BLOCK, PEER_BLOCK, d))
    return y.reshape(npad, d)[:n].reshape(b, t, d)


def setup_inputs(seed: int = 0) -> dict:
    key = jax.random.key(seed)
    ks = iter(jax.random.split(key, 48))

    def nrm(shape, scale):
        return jax.random.normal(next(ks), shape, jnp.float32) * scale

    def unif(shape, lo, hi):
        return jax.random.uniform(next(ks), shape, jnp.float32, lo, hi)

    d = D_MODEL
    sc = d ** -0.5
    la = min(WINDOW_A, PAST_LEN)
    lb = min(WINDOW_B_MAX, PAST_LEN)
    w_in_att = jnp.concatenate([
        nrm((d, ATT_COLS[0]), sc), nrm((d, ATT_COLS[1]), sc), nrm((d, ATT_COLS[2]), sc * DEEPNORM_BETA),
        nrm((d, ATT_COLS[3]), sc), nrm((d, ATT_COLS[4]), sc), nrm((d, ATT_COLS[5]), sc * DEEPNORM_BETA)], axis=1)
    w_rkv = nrm((3, d, d), sc) * jnp.array([1.0, 1.0, DEEPNORM_BETA], jnp.float32)[:, None, None]
    return {
        'x_prompt': nrm((BATCH, SEQ, d), 1.0),
        'x_sample': nrm((DEC_BATCH, DEC_SEQ, d), 1.0),
        'cache_a_k': nrm((DEC_BATCH, la, KV_A, HEAD_DIM), 1.0),
        'cache_a_v': nrm((DEC_BATCH, la, KV_A, HEAD_DIM), DEEPNORM_BETA),
        'cache_b_k': nrm((DEC_BATCH, lb, H_B, HEAD_DIM), 1.0),
        'cache_b_v': nrm((DEC_BATCH, lb, H_B, HEAD_DIM), DEEPNORM_BETA),
        'state_shift': nrm((DEC_BATCH, d), 1.0),
        'state_wkv': nrm((DEC_BATCH, H_C, RW_HEAD, RW_HEAD), 0.5),
        'w_in_att': w_in_att,
        'w_out_att': nrm((H_ATT * HEAD_DIM, d), (H_ATT * HEAD_DIM) ** -0.5 * DEEPNORM_BETA),
        'att_sinks': nrm((H_A,), 0.5),
        'rel_bias': nrm((N_BUCKETS, H_ATT), 0.5),
        'rw_mu': unif((6, d), 0.0, 1.0),
        'rw_w_rkv': w_rkv,
        'rw_w0': unif((d,), -6.0, -1.0),
        'rw_w1': nrm((d, LORA_DECAY), sc),
        'rw_w2': nrm((LORA_DECAY, d), 0.1 * LORA_DECAY ** -0.5),
        'rw_a0': nrm((d,), 0.1),
        'rw_a1': nrm((d, LORA_AAA), sc),
        'rw_a2': nrm((LORA_AAA, d), 0.1 * LORA_AAA ** -0.5),
        'rw_g1': nrm((d, LORA_GATE), sc),
        'rw_g2': nrm((LORA_GATE, d), LORA_GATE ** -0.5),
        'rw_k_k': 0.85 + nrm((d,), 0.05),
        'rw_k_a': 1.0 + nrm((d,), 0.05),
        'rw_r_k': nrm((H_C, RW_HEAD), 0.1),
        'rw_gn_g': 1.0 + nrm((d,), 0.05),
        'rw_gn_b': nrm((d,), 0.02),
        'rw_w_o': nrm((d, d), sc * DEEPNORM_BETA),
        'peer_w_q': nrm((DEPTH, d, PEER_HEADS * PEER_DKEY), sc),
        'peer_sub_keys': nrm((DEPTH, PEER_HEADS, 2, PEER_NKEYS, PEER_DKEY // 2), (PEER_DKEY // 2) ** -0.5),
        'peer_u': nrm((DEPTH, PEER_EXPERTS, d), sc),
        'peer_v': nrm((DEPTH, PEER_EXPERTS, d), DEEPNORM_BETA * PEER_HEADS ** -0.5),
        'ln_g': 1.0 + nrm((DEPTH, 2, d), 0.05),
        'ln_b': nrm((DEPTH, 2, d), 0.02),
    }


def reference(x_prompt, x_sample, cache_a_k, cache_a_v, cache_b_k, cache_b_v, state_shift, state_wkv,
              w_in_att, w_out_att, att_sinks, rel_bias,
              rw_mu, rw_w_rkv, rw_w0, rw_w1, rw_w2, rw_a0, rw_a1, rw_a2, rw_g1, rw_g2,
              rw_k_k, rw_k_a, rw_r_k, rw_gn_g, rw_gn_b, rw_w_o,
              peer_w_q, peer_sub_keys, peer_u, peer_v, ln_g, ln_b):
    xp, xs = x_prompt, x_sample
    tab_a, tab_b = rel_bias[:, :H_A], rel_bias[:, H_A:]
    for layer in range(DEPTH):
        if layer % 2 == 0:
            qa, ka, va, qb, kb, vb = attn_projections(xp, w_in_att)
            oa = banded_attention(qa, ka, va, tab_a, 1, att_sinks)[0]
            ob = dilated_prompt(qb, kb, vb, tab_b)
            mix_p = attn_output(oa, ob, w_out_att)
            la_p = min(WINDOW_A, xp.shape[1])
            lb_p = min(WINDOW_B_MAX, xp.shape[1])
            a_k_p, a_v_p = ka[:, -la_p:], va[:, -la_p:]
            b_k_p, b_v_p = kb[:, -lb_p:], vb[:, -lb_p:]
            qa, ka, va, qb, kb, vb = attn_projections(xs, w_in_att)
            oa, a_k_s, a_v_s = window_attention_sample(qa, ka, va, cache_a_k, cache_a_v, tab_a, att_sinks)
            ob, b_k_s, b_v_s = dilated_sample(qb, kb, vb, cache_b_k, cache_b_v, tab_b)
            mix_s = attn_output(oa, ob, w_out_att)
        else:
            rw = (rw_mu, rw_w_rkv, rw_w0, rw_w1, rw_w2, rw_a0, rw_a1, rw_a2, rw_g1, rw_g2,
                  rw_k_k, rw_k_a, rw_r_k, rw_gn_g, rw_gn_b, rw_w_o)
            zero_state = jnp.zeros((xp.shape[0], H_C, RW_HEAD, RW_HEAD), jnp.float32)
            mix_p, shift_p, wkv_p = rwkv7_mixer(xp, jnp.zeros_like(xp[:, 0]), zero_state, *rw)
            mix_s, shift_s, wkv_s = rwkv7_mixer(xs, state_shift, state_wkv, *rw)
        xp = layer_norm(DEEPNORM_ALPHA * xp + mix_p, ln_g[layer, 0], ln_b[layer, 0])
        xs = layer_norm(DEEPNORM_ALPHA * xs + mix_s, ln_g[layer, 0], ln_b[layer, 0])
        xp = layer_norm(DEEPNORM_ALPHA * xp + peer_ffn(xp, peer_w_q[layer], peer_sub_keys[layer], peer_u[layer], peer_v[layer]),
                        ln_g[layer, 1], ln_b[layer, 1])
        xs = layer_norm(DEEPNORM_ALPHA * xs + peer_ffn(xs, peer_w_q[layer], peer_sub_keys[layer], peer_u[layer], peer_v[layer]),
                        ln_g[layer, 1], ln_b[layer, 1])
    return (xp, xs, a_k_p, a_v_p, b_k_p, b_v_p, shift_p, wkv_p, a_k_s, a_v_s, b_k_s, b_v_s, shift_s, wkv_s)
```

```python
import numpy as np
from contextlib import ExitStack
import concourse.bass as bass
import concourse.mybir as mybir
from concourse.bass_utils import run_bass_kernel_spmd

F32 = mybir.dt.float32
BF16 = mybir.dt.bfloat16
I32 = mybir.dt.int32
U32 = mybir.dt.uint32
ALU = mybir.AluOpType
AF = mybir.ActivationFunctionType
AX = mybir.AxisListType

D = 4096
KC = 32
SEQ = 2048
NS = 16
HD = 128
SCALE = HD ** -0.5
ALPHA = 4.0 ** 0.25
LN_EPS = 1e-5


class Buf:
    __slots__ = ("name", "w", "r", "t")

    def __init__(self, name, t=None):
        self.name = name
        self.w = {}
        self.r = {}
        self.t = t

    def __getitem__(self, key):
        return self.t[key]


class Eng:
    def __init__(self, k, name, e, is_pe=False):
        self.k = k
        self.name = name
        self.e = e
        self.is_pe = is_pe
        self.sem = None
        self.cnt = 0
        self.waited = {}
        self.dma_sems = []
        self.dma_vals = []
        self.dma_i = 0

    def new_sem(self):
        self.sem = self.k.alloc_sem(self.name)
        self.cnt = 0

    def wait_tok(self, tok):
        if tok is None:
            return
        sem, val = tok
        if self.is_pe and sem is self.sem:
            return
        key = sem.name
        if self.waited.get(key, 0) >= val:
            return
        self.e.wait_ge(sem, val)
        self.waited[key] = val
        self.k.n_wait += 1


class KB:
    def __init__(self):
        self.nc = bass.Bass("TRN2", target_bir_lowering=False)
        nc = self.nc
        self._sem_i = 0
        self.n_wait = 0
        self.n_ins = 0
        self.pe = Eng(self, "pe", nc.tensor, is_pe=True)
        self.dve = Eng(self, "dve", nc.vector)
        self.act = Eng(self, "act", nc.scalar)
        self.pool = Eng(self, "pool", nc.gpsimd)
        self.sp = Eng(self, "sp", nc.sync)
        self.engs = [self.pe, self.dve, self.act, self.pool, self.sp]
        for e in self.engs:
            e.new_sem()
        for e in (self.sp, self.pool):
            for i in range(16):
                e.dma_sems.append(self.alloc_sem(e.name + "_dma%d" % i))
                e.dma_vals.append(0)

    def alloc_sem(self, name):
        self._sem_i += 1
        cm = self.nc.semaphore("%s_s%d" % (name, self._sem_i))
        return cm.__enter__()

    def _deps(self, eng, reads, writes, acc=False):
        toks = []
        for b in reads:
            toks.extend(b.w.values())
        if not acc:
            for b in writes:
                toks.extend(b.w.values())
                toks.extend(b.r.values())
        for t in toks:
            eng.wait_tok(t)

    def _commit(self, tok, reads, writes, acc=False):
        for b in reads:
            old = b.r.get(tok[0].name)
            if old is None or old[1] < tok[1]:
                b.r[tok[0].name] = tok
        for b in writes:
            if acc:
                old = b.w.get(tok[0].name)
                if old is None or old[1] < tok[1]:
                    b.w[tok[0].name] = tok
            else:
                b.w = {tok[0].name: tok}
                b.r = {}

    def op(self, eng, fn, reads=(), writes=()):
        self._deps(eng, reads, writes)
        if eng.cnt >= 30000:
            eng.new_sem()
        ins = fn()
        eng.cnt += 1
        ins.then_inc(eng.sem, 1)
        tok = (eng.sem, eng.cnt)
        self._commit(tok, reads, writes)
        self.n_ins += 1
        return tok

    def dma(self, eng, fn, reads=(), writes=(), acc=False):
        self._deps(eng, reads, writes, acc)
        i = eng.dma_i % len(eng.dma_sems)
        eng.dma_i += 1
        sem = eng.dma_sems[i]
        prev = eng.dma_vals[i]
        if prev > 0:
            eng.wait_tok((sem, prev))
        ins = fn()
        ins.then_inc(sem, 16)
        eng.dma_vals[i] = prev + 16
        tok = (sem, prev + 16)
        self._commit(tok, reads, writes, acc)
        self.n_ins += 1
        return tok

    def barrier(self):
        toks = []
        for eng in (self.sp, self.pool):
            for sem, v in zip(eng.dma_sems, eng.dma_vals):
                if v > 0:
                    toks.append((sem, v))
        for eng in self.engs:
            if eng.cnt > 0:
                toks.append((eng.sem, eng.cnt))
        for e in self.engs:
            for t in toks:
                if t[0] is e.sem:
                    continue
                e.wait_tok(t)

    def finish(self):
        e = self.sp
        for eng in (self.sp, self.pool):
            for sem, v in zip(eng.dma_sems, eng.dma_vals):
                if v > 0:
                    e.wait_tok((sem, v))
        for eng in self.engs:
            if eng is not e and eng.cnt > 0:
                e.wait_tok((eng.sem, eng.cnt))


class Ring:
    def __init__(self, bufs):
        self.bufs = bufs
        self.i = 0

    def next(self):
        b = self.bufs[self.i % len(self.bufs)]
        self.i += 1
        return b


class PhaseStack(ExitStack):
    kb = None

    def __exit__(self, *a):
        if a[0] is None and PhaseStack.kb is not None:
            PhaseStack.kb.barrier()
        return super().__exit__(*a)


class Prog:
    def __init__(self, debug=False):
        self.k = KB()
        PhaseStack.kb = self.k
        self.nc = self.k.nc
        self.debug = debug
        self.ins = {}
        self.outs = {}
        self.scr = {}
        self._evac_i = 0
        self.bg = []

    def inp(self, name, shape, dt=F32):
        t = self.nc.dram_tensor(name, list(shape), dt, kind="ExternalInput")
        self.ins[name] = Buf(name, t.ap())
        return self.ins[name]

    def out(self, name, shape, dt=F32):
        t = self.nc.dram_tensor(name, list(shape), dt, kind="ExternalOutput")
        self.outs[name] = Buf(name, t.ap())
        return self.outs[name]

    def scratch(self, name, shape, dt=F32, dbg=False):
        kind = "ExternalOutput" if (dbg and self.debug) else "Internal"
        t = self.nc.dram_tensor(name, list(shape), dt, kind=kind)
        self.scr[name] = Buf(name, t.ap())
        return self.scr[name]

    def sb(self, es, name, shape, dt=F32):
        self._sb_i = getattr(self, "_sb_i", 0) + 1
        nm = "sb%d_%s" % (self._sb_i, name)
        return Buf(nm, es.enter_context(self.nc.sbuf_tensor(nm, list(shape), dt)))

    def sbring(self, es, name, shape, dt, n):
        return Ring([self.sb(es, "%s%d" % (name, i), shape, dt) for i in range(n)])

    def mm(self, out_b, out_ap, lhsT_b, lhsT_ap, rhs_b, rhs_ap, start, stop):
        nc = self.nc
        return self.k.op(self.k.pe, lambda: nc.tensor.matmul(out_ap, lhsT=lhsT_ap, rhs=rhs_ap, start=start, stop=stop),
                         reads=[lhsT_b, rhs_b], writes=[out_b])

    def tr(self, out_b, out_ap, in_b, in_ap, ident_b, ident_ap):
        nc = self.nc
        return self.k.op(self.k.pe, lambda: nc.tensor.transpose(out=out_ap, in_=in_ap, identity=ident_ap),
                         reads=[in_b, ident_b], writes=[out_b])

    def ld(self, out_b, out_ap, in_b, in_ap, eng=None):
        eng = eng or self.k.sp
        return self.k.dma(eng, lambda: eng.e.dma_start(out=out_ap, in_=in_ap), reads=[in_b], writes=[out_b])

    def st(self, out_b, out_ap, in_b, in_ap, eng=None):
        eng = eng or self.k.sp
        return self.k.dma(eng, lambda: eng.e.dma_start(out=out_ap, in_=in_ap), reads=[in_b], writes=[out_b], acc=True)

    def evac(self, out_b, out_ap, in_b, in_ap, scale=None, extra_reads=()):
        nc = self.nc
        self._evac_i += 1
        if self._evac_i % 2 == 0:
            if scale is None:
                f = lambda: nc.scalar.copy(out=out_ap, in_=in_ap)
            else:
                f = lambda: nc.scalar.mul(out=out_ap, in_=in_ap, mul=scale)
            return self.k.op(self.k.act, f, reads=[in_b, *extra_reads], writes=[out_b])
        else:
            if scale is None:
                f = lambda: nc.vector.tensor_copy(out=out_ap, in_=in_ap)
            else:
                f = lambda: nc.vector.tensor_scalar_mul(out=out_ap, in0=in_ap, scalar1=scale)
            return self.k.op(self.k.dve, f, reads=[in_b, *extra_reads], writes=[out_b])


def phase_l0a(P):
    nc, k = P.nc, P.k
    xp, xs, w_in, ident = P.ins["xp"], P.ins["xs"], P.ins["w_in_att"], P.ins["ident"]
    QK, VA, PS = P.scr["QK"], P.scr["VA"], P.scr["PS"]
    wv = w_in.t.rearrange("(kc p) n -> p kc n", p=128)
    with PhaseStack() as es:
        NT = 1024 + NS
        xT = P.sb(es, "xT", [128, KC, NT], BF16)
        xring = P.sbring(es, "xtile", [128, D], F32, 2)
        wring = P.sbring(es, "wb", [128, KC, 256], BF16, 3)
        stA = P.sbring(es, "stA", [128, 512], BF16, 3)
        stB = P.sbring(es, "stB", [128, 256], F32, 3)
        idt = P.sb(es, "idt", [128, 128], F32)
        P.ld(idt, idt[:], ident, ident.t)
        pbanks = Ring(P.psum)

        def load_xT(src_b, src_ap, ntok, col0):
            xt = xring.next()
            P.ld(xt, xt[0:ntok, :], src_b, src_ap)
            for g in range(8):
                pb = pbanks.next()
                for j in range(4):
                    kc = g * 4 + j
                    P.tr(pb, pb[:, j * 128:j * 128 + ntok], xt, xt[0:ntok, kc * 128:(kc + 1) * 128], idt, idt[0:ntok, 0:ntok])
                P.evac(xT, xT[:, g * 4:(g + 1) * 4, col0:col0 + ntok], pb,
                       pb[:, 0:512].rearrange("p (j t) -> p j t", j=4)[:, :, 0:ntok])

        for ps_i in range(2):
            tok0 = ps_i * 1024
            for tt in range(8):
                load_xT(xp, xp.t[tok0 + tt * 128: tok0 + (tt + 1) * 128, :], 128, tt * 128)
            has_s = (ps_i == 1)
            if has_s:
                load_xT(xs, xs.t[:, :], NS, 1024)
            for j in range(32):
                wb = wring.next()
                P.ld(wb, wb[:], w_in, wv[:, :, j * 256:(j + 1) * 256], eng=k.pool)
                drain_bg(P)
                if j < 10:
                    kind, chunk0 = "q", 2 * j
                elif j < 12:
                    kind, chunk0 = "kA", 20 + 2 * (j - 10)
                elif j < 14:
                    kind, chunk0 = "vA", None
                elif j < 20:
                    kind, chunk0 = "q", 24 + 2 * (j - 14)
                elif j < 26:
                    kind, chunk0 = "kB", 36 + 2 * (j - 20)
                else:
                    kind, chunk0 = "vB", None
                if kind in ("q", "kA", "kB"):
                    for h in range(2):
                        for tc in range(2):
                            pb = pbanks.next()
                            for kc in range(KC):
                                P.mm(pb, pb[:, 0:512], wb, wb[:, kc, h * 128:(h + 1) * 128],
                                     xT, xT[:, kc, tc * 512:(tc + 1) * 512], kc == 0, kc == KC - 1)
                            s = stA.next()
                            P.evac(s, s[:], pb, pb[:, 0:512], scale=(SCALE if kind == "q" else None))
                            P.st(QK, QK.t[chunk0 + h, :, tok0 + tc * 512: tok0 + (tc + 1) * 512], s, s[:])
                if kind in ("kA", "vA", "kB", "vB"):
                    for tt in range(8):
                        gt = tok0 + tt * 128
                        if kind == "kA" and gt != SEQ - 128:
                            continue
                        pb = pbanks.next()
                        for kc in range(KC):
                            P.mm(pb, pb[:, 0:256], xT, xT[:, kc, tt * 128:(tt + 1) * 128], wb, wb[:, kc, :], kc == 0, kc == KC - 1)
                        s = stB.next()
                        P.evac(s, s[:], pb, pb[:, 0:256])
                        if kind == "kA":
                            o = P.outs["a_k_p"]
                            P.st(o, o.t[:, (j - 10) * 256:(j - 9) * 256], s, s[:])
                        elif kind == "vA":
                            P.st(VA, VA.t[gt:gt + 128, (j - 12) * 256:(j - 11) * 256], s, s[:])
                            if gt == SEQ - 128:
                                o = P.outs["a_v_p"]
                                P.st(o, o.t[:, (j - 12) * 256:(j - 11) * 256], s, s[:])
                        elif kind == "kB":
                            o = P.outs["b_k_p"]
                            P.st(o, o.t[gt:gt + 128, (j - 20) * 256:(j - 19) * 256], s, s[:])
                        else:
                            o = P.outs["b_v_p"]
                            P.st(o, o.t[gt:gt + 128, (j - 26) * 256:(j - 25) * 256], s, s[:])
                if has_s:
                    pb = pbanks.next()
                    for kc in range(KC):
                        P.mm(pb, pb[0:NS, 0:256], xT, xT[:, kc, 1024:1024 + NS], wb, wb[:, kc, :], kc == 0, kc == KC - 1)
                    s = stB.next()
                    P.evac(s, s[0:NS, :], pb, pb[0:NS, 0:256])
                    P.st(PS, PS.t[:, j * 256:(j + 1) * 256], s, s[0:NS, :])


DILS = (1, 4, 16)


def phase_bias_setup(P):
    nc, k = P.nc, P.k
    GGD = P.scr["GGD"]
    with PhaseStack() as es:
        tab = P.sb(es, "tab", [32, 32], F32)
        P.ld(tab, tab[:], P.ins["rel_bias"], P.ins["rel_bias"].t)
        sel = P.sb(es, "sel", [32, 3, 384], F32)
        P.ld(sel, sel[:], P.ins["sel"], P.ins["sel"].t)
        msk = P.sb(es, "msk", [32, 384], F32)
        P.ld(msk, msk[:], P.ins["selmask"], P.ins["selmask"].t)
        gg = P.sb(es, "gg", [32, 3, 384], F32)
        for di in range(3):
            pb = P.psum[di]
            P.mm(pb, pb[0:32, 0:384], tab, tab[:, :], sel, sel[:, di, :], True, True)
            k.op(k.act, lambda: nc.scalar.activation(out=gg[:, di, :], in_=pb[0:32, 0:384], func=AF.Exp), reads=[pb], writes=[gg])
            k.op(k.dve, lambda: nc.vector.tensor_tensor(out=gg[:, di, :], in0=gg[:, di, :], in1=msk[:, :], op=ALU.mult), reads=[gg, msk], writes=[gg])
        P.st(GGD, GGD.t.rearrange("d h j -> h d j"), gg, gg[:])


def phase_l0b(P):
    nc, k = P.nc, P.k
    QK, VA, OT, GGD = P.scr["QK"], P.scr["VA"], P.scr["OT"], P.scr["GGD"]
    bvp = P.outs["b_v_p"]
    with PhaseStack() as es:
        qT = P.sbring(es, "qT", [128, SEQ], BF16, 2)
        kT = P.sbring(es, "kT", [128, SEQ], BF16, 2)
        Vt = P.sbring(es, "Vt", [128, 16, 128], BF16, 2)
        Et = P.sbring(es, "Et", [128, 2, 128], F32, 2)
        p1r = P.sbring(es, "p1", [128, 128], F32, 3)
        ptr = P.sbring(es, "pt", [128, 128], BF16, 3)
        rzr = P.sbring(es, "rz", [128, 128], F32, 2)
        ost = P.sbring(es, "ost", [128, SEQ], BF16, 2)
        oacc = P.sb(es, "oacc", [128, SEQ], F32)
        zacc = P.sb(es, "zacc", [128, SEQ], F32)
        ones = P.sb(es, "ones", [128, 128], BF16)
        esk = P.sb(es, "esk", [128, 20], F32)
        k.op(k.dve, lambda: nc.vector.memset(ones[:], 1.0), writes=[ones])
        sinks = P.ins["att_sinks"]
        P.ld(esk, esk[:], sinks, sinks.t.partition_broadcast(128))
        k.op(k.act, lambda: nc.scalar.activation(out=esk[:], in_=esk[:], func=AF.Exp), reads=[esk], writes=[esk])
        pbanks = Ring(P.psum)

        def hankel(eb, slot, di, h, base):
            src = bass.AP(tensor=GGD.t.tensor, offset=(di * 32 + h) * 384 + base, ap=[[1, 128], [1, 128]])
            P.ld(eb, eb[:, slot, :], GGD, src)

        def attend(qb, q_ap, pairs, eb, ob, zb):
            np_ = len(pairs)
            for i, (kb_, k_ap, vb_, v_ap, slot) in enumerate(pairs):
                sb_ = pbanks.next()
                P.mm(sb_, sb_[:, 0:128], kb_, k_ap, qb, q_ap, True, True)
                p1 = p1r.next()
                k.op(k.act, lambda: nc.scalar.activation(out=p1[:], in_=sb_[:, 0:128], func=AF.Exp), reads=[sb_], writes=[p1])
                pt = ptr.next()
                k.op(k.dve, lambda: nc.vector.tensor_tensor(out=pt[:], in0=p1[:], in1=eb[:, slot, ::-1], op=ALU.mult),
                     reads=[p1, eb], writes=[pt])
                P.mm(ob, ob[:, 0:128], vb_, v_ap, pt, pt[:], i == 0, i == np_ - 1)
                P.mm(zb, zb[:, 0:128], ones, ones[:], pt, pt[:], i == 0, i == np_ - 1)

        cur_g = -1
        for h in range(20):
            g = h // 5
            q = qT.next()
            P.ld(q, q[:], QK, QK.t[h])
            if g != cur_g:
                kk = kT.next()
                P.ld(kk, kk[:], QK, QK.t[20 + g])
                vv = Vt.next()
                P.ld(vv, vv[:], VA, VA.t[:, g * 128:(g + 1) * 128].rearrange("(n p) d -> p n d", p=128), eng=k.pool)
                cur_g = g
            eb = Et.next()
            hankel(eb, 0, 0, h, 128)
            hankel(eb, 1, 0, h, 0)
            o_s = ost.next()
            for n in range(16):
                ob, zb = pbanks.next(), pbanks.next()
                pairs = []
                if n > 0:
                    pairs.append((kk, kk[:, (n - 1) * 128:n * 128], vv, vv[:, n - 1, :], 1))
                pairs.append((kk, kk[:, n * 128:(n + 1) * 128], vv, vv[:, n, :], 0))
                attend(q, q[:, n * 128:(n + 1) * 128], pairs, eb, ob, zb)
                rz = rzr.next()
                k.op(k.dve, lambda: nc.vector.tensor_scalar(out=rz[:], in0=zb[:, 0:128], scalar1=esk[:, h:h + 1], scalar2=None, op0=ALU.add),
                     reads=[zb, esk], writes=[rz])
                k.op(k.dve, lambda: nc.vector.reciprocal(out=rz[:], in_=rz[:]), reads=[rz], writes=[rz])
                k.op(k.dve, lambda: nc.vector.tensor_tensor(out=o_s[:, n * 128:(n + 1) * 128], in0=ob[:, 0:128], in1=rz[:], op=ALU.mult),
                     reads=[ob, rz], writes=[o_s])
            P.st(OT, OT.t[h, :, 0:SEQ], o_s, o_s[:])

        for hb in range(12):
            h = 20 + hb
            q = qT.next()
            P.ld(q, q[:], QK, QK.t[24 + hb])
            kk = kT.next()
            P.ld(kk, kk[:], QK, QK.t[36 + hb])
            for di, d in enumerate(DILS):
                nb = SEQ // (128 * d)
                vv = Vt.next()
                P.ld(vv, vv[:].rearrange("p (r n) c -> p r n c", r=d),
                     bvp, bvp.t[:, hb * 128:(hb + 1) * 128].rearrange("(n i r) c -> i r n c", i=128, r=d), eng=k.pool)
                eb = Et.next()
                hankel(eb, 0, di, h, 128)
                hankel(eb, 1, di, h, 0)
                qv = q[:].rearrange("p (n i r) -> p r n i", i=128, r=d)
                kv = kk[:].rearrange("p (n i r) -> p r n i", i=128, r=d)
                ov = oacc[:].rearrange("p (n i r) -> p r n i", i=128, r=d)
                zv = zacc[:].rearrange("p (n i r) -> p r n i", i=128, r=d)
                for r in range(d):
                    for n in range(nb):
                        ob, zb = pbanks.next(), pbanks.next()
                        pairs = []
                        if n > 0:
                            pairs.append((kk, kv[:, r, n - 1, :], vv, vv[:, r * nb + n - 1, :], 1))
                        pairs.append((kk, kv[:, r, n, :], vv, vv[:, r * nb + n, :], 0))
                        attend(q, qv[:, r, n, :], pairs, eb, ob, zb)
                        if di == 0:
                            k.op(k.act, lambda: nc.scalar.copy(out=ov[:, r, n, :], in_=ob[:, 0:128]), reads=[ob], writes=[oacc])
                            k.op(k.dve, lambda: nc.vector.tensor_copy(out=zv[:, r, n, :], in_=zb[:, 0:128]), reads=[zb], writes=[zacc])
                        else:
                            k.op(k.dve, lambda: nc.vector.tensor_tensor(out=ov[:, r, n, :], in0=ov[:, r, n, :], in1=ob[:, 0:128], op=ALU.add),
                                 reads=[ob, oacc], writes=[oacc])
                            k.op(k.dve, lambda: nc.vector.tensor_tensor(out=zv[:, r, n, :], in0=zv[:, r, n, :], in1=zb[:, 0:128], op=ALU.add),
                                 reads=[zb, zacc], writes=[zacc])
            k.op(k.dve, lambda: nc.vector.reciprocal(out=zacc[:], in_=zacc[:]), reads=[zacc], writes=[zacc])
            o_s = ost.next()
            k.op(k.dve, lambda: nc.vector.tensor_tensor(out=o_s[:], in0=oacc[:], in1=zacc[:], op=ALU.mult), reads=[oacc, zacc], writes=[o_s])
            P.st(OT, OT.t[h, :, 0:SEQ], o_s, o_s[:])


T_ALL = SEQ + NS
TOK_TILES = [(i * 128, 128) for i in range(16)] + [(SEQ, NS)]
TOK_CHUNKS = [(i * 512, 512) for i in range(4)] + [(SEQ, NS)]


def load_actT(P, es, src, name="actT"):
    a = P.sb(es, name, [128, KC, T_ALL], BF16)
    for c0 in range(0, KC, 8):
        P.ld(a, a[:, c0:c0 + 8, :], src, src.t[c0:c0 + 8].rearrange("c p t -> p c t"))
    return a


def proj_pass(P, actT, W_ap, ncols, form, emit, kc_n=KC, blk=256, chunks=None, tiles=None):
    k = P.k
    wv = W_ap.rearrange("(kc p) n -> p kc n", p=128)
    with PhaseStack() as es:
        wring = P.sbring(es, "wblk", [128, kc_n, blk], BF16, 3)
        pbanks = Ring(P.psum)
        for c0 in range(0, ncols, blk):
            cn = min(blk, ncols - c0)
            wb = wring.next()
            P.ld(wb, wb[:, :, 0:cn], P.wsrc, wv[:, :, c0:c0 + cn], eng=k.pool)
            drain_bg(P)
            if form == "A":
                for h in range((cn + 127) // 128):
                    m_ = min(128, cn - h * 128)
                    for (t0, tn) in (chunks or TOK_CHUNKS):
                        pb = pbanks.next()
                        for kc in range(kc_n):
                            P.mm(pb, pb[0:m_, 0:tn], wb, wb[:, kc, h * 128:h * 128 + m_], actT, actT[:, kc, t0:t0 + tn], kc == 0, kc == kc_n - 1)
                        emit((c0 // 128) + h, t0, tn, pb)
            else:
                for (t0, tn) in (tiles or TOK_TILES):
                    pb = pbanks.next()
                    for kc in range(kc_n):
                        P.mm(pb, pb[0:tn, 0:cn], actT, actT[:, kc, t0:t0 + tn], wb, wb[:, kc, 0:cn], kc == 0, kc == kc_n - 1)
                    emit(c0, cn, t0, tn, pb)


def ln_rows(P, es_bufs, h, nt, gbc, bbc):
    nc, k = P.nc, P.k
    junk, st = es_bufs
    k.op(k.act, lambda: nc.scalar.activation(out=junk[0:nt, :], in_=h[0:nt, :], func=AF.Copy, accum_out=st[0:nt, 0:1]), reads=[h], writes=[junk, st])
    k.op(k.act, lambda: nc.scalar.activation(out=junk[0:nt, :], in_=h[0:nt, :], func=AF.Square, accum_out=st[0:nt, 1:2]), reads=[h], writes=[junk, st])
    k.op(k.dve, lambda: nc.vector.tensor_scalar(out=st[0:nt, 0:2], in0=st[0:nt, 0:2], scalar1=1.0 / D, scalar2=None, op0=ALU.mult), reads=[st], writes=[st])
    k.op(k.dve, lambda: nc.vector.tensor_tensor(out=st[0:nt, 2:3], in0=st[0:nt, 0:1], in1=st[0:nt, 0:1], op=ALU.mult), reads=[st], writes=[st])
    k.op(k.dve, lambda: nc.vector.tensor_tensor(out=st[0:nt, 3:4], in0=st[0:nt, 1:2], in1=st[0:nt, 2:3], op=ALU.subtract), reads=[st], writes=[st])
    k.op(k.dve, lambda: nc.vector.tensor_scalar(out=st[0:nt, 3:4], in0=st[0:nt, 3:4], scalar1=LN_EPS, scalar2=None, op0=ALU.add), reads=[st], writes=[st])
    k.op(k.act, lambda: nc.scalar.activation(out=st[0:nt, 3:4], in_=st[0:nt, 3:4], func=AF.Sqrt), reads=[st], writes=[st])
    k.op(k.dve, lambda: nc.vector.reciprocal(out=st[0:nt, 3:4], in_=st[0:nt, 3:4]), reads=[st], writes=[st])
    k.op(k.dve, lambda: nc.vector.tensor_scalar(out=h[0:nt, :], in0=h[0:nt, :], scalar1=st[0:nt, 0:1], scalar2=st[0:nt, 3:4], op0=ALU.subtract, op1=ALU.mult), reads=[h, st], writes=[h])
    k.op(k.dve, lambda: nc.vector.tensor_tensor(out=h[0:nt, :], in0=h[0:nt, :], in1=gbc[0:nt, :], op=ALU.mult), reads=[h, gbc], writes=[h])
    k.op(k.pool, lambda: nc.gpsimd.tensor_tensor(out=h[0:nt, :], in0=h[0:nt, :], in1=bbc[0:nt, :], op=ALU.add), reads=[h, bbc], writes=[h])


def to_featT(P, pbanks, h, nt, idt, dstT_b, dst_scr, t0, stg_ring):
    stg = stg_ring.next()
    for g in range(8):
        pb = pbanks.next()
        for j in range(4):
            kc = g * 4 + j
            P.tr(pb, pb[:, j * 128:j * 128 + nt], h, h[0:nt, kc * 128:(kc + 1) * 128], idt, idt[0:nt, 0:nt])
        P.evac(stg, stg[:, g * 4:(g + 1) * 4, 0:nt], pb, pb[:, 0:512].rearrange("p (j t) -> p j t", j=4)[:, :, 0:nt])
    P.st(dst_scr, dst_scr.t[:, :, t0:t0 + nt].rearrange("c p t -> p c t"), stg, stg[:, :, 0:nt])


def phase_outproj(P, srcT, W_b, dst):
    with PhaseStack() as es:
        actT = load_actT(P, es, srcT)
        stg = P.sbring(es, "stgB", [128, 256], F32, 4)
        P.wsrc = W_b

        def emit(c0, cn, t0, tn, pb):
            s_ = stg.next()
            P.evac(s_, s_[0:tn, 0:cn], pb, pb[0:tn, 0:cn])
            P.st(dst, dst.t[t0:t0 + tn, c0:c0 + cn], s_, s_[0:tn, 0:cn])
        proj_pass(P, actT, W_b.t, D, "B", emit)


def phase_res_ln(P, xres_fn, mix, ln_row, dst_tok, dst_T, dst_bf=None, alpha=ALPHA):
    nc, k = P.nc, P.k
    with PhaseStack() as es:
        gbc = P.sb(es, "gbc", [128, D], F32); bbc = P.sb(es, "bbc", [128, D], F32)
        lg, lb = P.ins["ln_g"], P.ins["ln_b"]
        P.ld(gbc, gbc[:], lg, lg.t[ln_row:ln_row + 1, :].partition_broadcast(128))
        P.ld(bbc, bbc[:], lb, lb.t[ln_row:ln_row + 1, :].partition_broadcast(128))
        idt = P.sb(es, "idt", [128, 128], F32)
        P.ld(idt, idt[:], P.ins["ident"], P.ins["ident"].t)
        xr = P.sbring(es, "xr", [128, D], F32, 2)
        mr = P.sbring(es, "mr", [128, D], F32, 2)
        junk = P.sb(es, "junk", [128, D], F32); st = P.sb(es, "lnst", [128, 4], F32)
        stgT = P.sbring(es, "stgT", [128, KC, 128], BF16, 2)
        mbf = P.sbring(es, "mbf", [128, D], BF16, 2)
        pbanks = Ring(P.psum)
        for (t0, tn) in TOK_TILES:
            x = xr.next(); m = mr.next()
            xb_, xap = xres_fn(t0, tn)
            P.ld(x, x[0:tn, :], xb_, xap)
            P.ld(m, m[0:tn, :], mix, mix.t[t0:t0 + tn, :])
            k.op(k.dve, lambda: nc.vector.scalar_tensor_tensor(out=m[0:tn, :], in0=x[0:tn, :], scalar=alpha, in1=m[0:tn, :], op0=ALU.mult, op1=ALU.add),
                 reads=[x, m], writes=[m])
            ln_rows(P, (junk, st), m, tn, gbc, bbc)
            P.st(dst_tok, dst_tok.t[t0:t0 + tn, :], m, m[0:tn, :])
            if dst_bf is not None:
                mb = mbf.next()
                k.op(k.act, lambda: nc.scalar.copy(out=mb[0:tn, :], in_=m[0:tn, :]), reads=[m], writes=[mb])
                P.st(dst_bf, dst_bf.t[t0:t0 + tn, :], mb, mb[0:tn, :])
            if dst_T is not None:
                to_featT(P, pbanks, m, tn, idt, None, dst_T, t0, stgT)


def phase_tables_bf16(P, layer):
    k = P.k
    for nm, dst in (("peer_u", P.scr["UB%d" % layer]), ("peer_v", P.scr["VB%d" % layer])):
        src = P.ins[nm]
        for r0 in range(0, 16384, 512):
            def job(src=src, dst=dst, r0=r0):
                k.dma(k.pool, lambda: P.nc.gpsimd.dma_start(out=dst.t[r0:r0 + 512, :], in_=src.t[layer, r0:r0 + 512, :]),
                      reads=[src], writes=[dst], acc=True)
            P.bg.append(job)


def drain_bg(P, n=1):
    for _ in range(n):
        if P.bg:
            P.bg.pop(0)()


def phase_peer_q(P, layer, XT, QPT):
    with PhaseStack() as es:
        actT = load_actT(P, es, XT)
        stg = P.sbring(es, "stgA", [128, 512], BF16, 4)
        P.wsrc = P.ins["peer_w_q"]

        def emit(ci, t0, tn, pb):
            s_ = stg.next()
            P.evac(s_, s_[:, 0:tn], pb, pb[:, 0:tn])
            P.st(QPT, QPT.t[ci, :, t0:t0 + tn], s_, s_[:, 0:tn])
        proj_pass(P, actT, P.ins["peer_w_q"].t[layer], 2048, "A", emit)


def phase_peer_main(P, layer, XM, XMB, QPT, ln_row, dst_tok, dst_T):
    nc, k = P.nc, P.k
    UB, VB = P.scr["UB%d" % layer], P.scr["VB%d" % layer]
    drain_bg(P, len(P.bg))
    with PhaseStack() as es:
        gbc = P.sb(es, "gbc", [128, D], F32); bbc = P.sb(es, "bbc", [128, D], F32)
        lg, lb = P.ins["ln_g"], P.ins["ln_b"]
        P.ld(gbc, gbc[:], lg, lg.t[ln_row:ln_row + 1, :].partition_broadcast(128))
        P.ld(bbc, bbc[:], lb, lb.t[ln_row:ln_row + 1, :].partition_broadcast(128))
        idt = P.sb(es, "idt", [128, 128], F32)
        P.ld(idt, idt[:], P.ins["ident"], P.ins["ident"].t)
        io16 = P.sb(es, "io16", [128, 16], F32)
        P.ld(io16, io16[:], P.ins["iota16"], P.ins["iota16"].t)
        ohw = P.sb(es, "ohw", [128, 255], F32)
        P.ld(ohw, ohw[:], P.ins["ohw"], P.ins["ohw"].t)
        skT = P.sb(es, "skT", [128, 16, 128], BF16)
        skn = P.sb(es, "skn", [128, 16, 128], F32)
        sk = P.ins["peer_sub_keys"]
        P.ld(skn, skn[:], sk, sk.t[layer].rearrange("h c n d -> n (h c) d"))
        pbanks = Ring(P.psum)
        for g in range(4):
            pb = pbanks.next()
            for j in range(4):
                P.tr(pb, pb[:, j * 128:(j + 1) * 128], skn, skn[:, g * 4 + j, :], idt, idt[:])
            P.evac(skT, skT[:, g * 4:(g + 1) * 4, :], pb, pb[:, 0:512].rearrange("p (j t) -> p j t", j=4))
        qTr = P.sbring(es, "qpT", [128, 16, 128], BF16, 2)
        sc = P.sb(es, "sc", [128, 16, 128], F32)
        wk = P.sb(es, "wk", [128, 256], F32)
        sv = P.sb(es, "sv", [128, 16, 16], F32)
        si = P.sb(es, "si", [128, 16, 16], U32)
        sif = P.sb(es, "sif", [128, 16, 16], F32)
        cand = P.sb(es, "cand", [128, 8, 256], F32)
        big = P.sb(es, "big", [128, 8, 256], F32)
        cv = P.sb(es, "cv", [128, 8, 16], F32)
        ci = P.sb(es, "ci", [128, 8, 16], U32)
        cii = P.sb(es, "cii", [128, 8, 16], U32)
        cif = P.sb(es, "cif", [128, 2, 8, 16], F32)
        i12 = P.sb(es, "i12", [128, 2, 8, 16], F32)
        idxf = P.sb(es, "idxf", [128, 128], F32)
        idxT = P.sb(es, "idxT", [128, 128], I32)
        gate = P.sb(es, "gate", [128, 8, 16], F32)
        gsum = P.sb(es, "gsum", [128, 8], F32)
        hidT = P.sb(es, "hidT", [128, 128], F32)
        hid = P.sb(es, "hid", [128, 128], F32)
        coefT = P.sb(es, "coefT", [128, 128], F32)
        Ur = P.sbring(es, "Ug", [128, D], BF16, 3)
        Vr = P.sbring(es, "Vg", [128, D], BF16, 3)
        xmb = P.sb(es, "xmb", [128, D], BF16)
        onesb = P.sb(es, "onesb", [128, 128], BF16)
        k.op(k.dve, lambda: nc.vector.memset(onesb[:], 1.0), writes=[onesb])
        selr = P.sbring(es, "selr", [128, 128], BF16, 3)
        hid2 = P.sb(es, "hid2", [128, 2, 128], F32)
        Zr = P.sbring(es, "Zt", [128, 128], BF16, 3)
        junkb = P.sb(es, "junkb", [128, D], BF16)
        xm = P.sb(es, "xm", [128, D], F32)
        junk = junkb; st = P.sb(es, "lnst", [128, 4], F32)
        stgT = P.sbring(es, "stgT", [128, KC, 128], BF16, 1)

        def top16(src_b, src_ap, n, val_b, val_ap, idx_b, idx_ap):
            k.op(k.dve, lambda: nc.vector.max(out=val_ap[:, 0:8], in_=src_ap), reads=[src_b], writes=[val_b])
            k.op(k.dve, lambda: nc.vector.max_index(out=idx_ap[:, 0:8], in_max=val_ap[:, 0:8], in_values=src_ap), reads=[src_b, val_b], writes=[idx_b])
            k.op(k.dve, lambda: nc.vector.match_replace(out=wk[0:src_ap.shape[0], 0:n], in_to_replace=val_ap[:, 0:8], in_values=src_ap, imm_value=-1e30),
                 reads=[src_b, val_b], writes=[wk])
            wv_ = wk[0:src_ap.shape[0], 0:n]
            k.op(k.dve, lambda: nc.vector.max(out=val_ap[:, 8:16], in_=wv_), reads=[wk], writes=[val_b])
            k.op(k.dve, lambda: nc.vector.max_index(out=idx_ap[:, 8:16], in_max=val_ap[:, 8:16], in_values=wv_), reads=[wk, val_b], writes=[idx_b])

        for (t0, tn) in TOK_TILES:
            qT = qTr.next()
            P.ld(qT, qT[:, :, 0:tn], QPT, QPT.t[:, :, t0:t0 + tn].rearrange("c p t -> p c t"))
            P.ld(xm, xm[0:tn, :], XM, XM.t[t0:t0 + tn, :])
            for g in range(4):
                pb = pbanks.next()
                for j in range(4):
                    ch = g * 4 + j
                    P.mm(pb, pb[0:tn, j * 128:(j + 1) * 128], qT, qT[:, ch, 0:tn], skT, skT[:, ch, :], True, True)
                P.evac(sc, sc[0:tn, g * 4:(g + 1) * 4, :], pb, pb[0:tn, 0:512].rearrange("p (j n) -> p j n", j=4))
            for ch in range(16):
                top16(sc, sc[0:tn, ch, :], 128, sv, sv[0:tn, ch, :], si, si[0:tn, ch, :])
            k.op(k.dve, lambda: nc.vector.tensor_copy(out=sif[0:tn], in_=si[0:tn]), reads=[si], writes=[sif])
            svv = sv[0:tn].rearrange("p (h c) k -> p h c k", c=2)
            k.op(k.dve, lambda: nc.vector.tensor_tensor(out=cand[0:tn].rearrange("p h (i j) -> p h i j", i=16),
                                                          in0=svv[:, :, 0, :].unsqueeze(3).to_broadcast([tn, 8, 16, 16]),
                                                          in1=svv[:, :, 1, :].unsqueeze(2).to_broadcast([tn, 8, 16, 16]), op=ALU.add),
                 reads=[sv], writes=[cand])
            for h in range(8):
                top16(cand, cand[0:tn, h, :], 256, cv, cv[0:tn, h, :], ci, ci[0:tn, h, :])
            k.op(k.dve, lambda: nc.vector.tensor_single_scalar(out=cii[0:tn], in_=ci[0:tn], scalar=4, op=ALU.logical_shift_right), reads=[ci], writes=[cii])
            k.op(k.dve, lambda: nc.vector.tensor_copy(out=cif[0:tn, 0], in_=cii[0:tn]), reads=[cii], writes=[cif])
            k.op(k.dve, lambda: nc.vector.tensor_single_scalar(out=cii[0:tn], in_=ci[0:tn], scalar=15, op=ALU.bitwise_and), reads=[ci], writes=[cii])
            k.op(k.dve, lambda: nc.vector.tensor_copy(out=cif[0:tn, 1], in_=cii[0:tn]), reads=[cii], writes=[cif])
            sifv = sif[0:tn].rearrange("p (h c) k -> p h c k", c=2)
            for c in range(2):
                bv = big[0:tn].rearrange("p h (k i) -> p h k i", k=16)
                k.op(k.dve, lambda: nc.vector.tensor_tensor(out=bv, in0=cif[0:tn, c].unsqueeze(3).to_broadcast([tn, 8, 16, 16]),
                                                              in1=io16[0:tn].unsqueeze(1).unsqueeze(1).to_broadcast([tn, 8, 16, 16]), op=ALU.is_equal),
                     reads=[cif, io16], writes=[big])
                k.op(k.dve, lambda: nc.vector.tensor_tensor(out=bv, in0=bv, in1=sifv[:, :, c, :].unsqueeze(2).to_broadcast([tn, 8, 16, 16]), op=ALU.mult),
                     reads=[big, sif], writes=[big])
                k.op(k.dve, lambda: nc.vector.tensor_reduce(out=i12[0:tn, c], in_=bv, axis=AX.X, op=ALU.add), reads=[big], writes=[i12])
            k.op(k.dve, lambda: nc.vector.scalar_tensor_tensor(out=idxf[0:tn].rearrange("p (h k) -> p h k", h=8), in0=i12[0:tn, 0], scalar=128.0, in1=i12[0:tn, 1],
                                                                 op0=ALU.mult, op1=ALU.add), reads=[i12], writes=[idxf])
            k.op(k.dve, lambda: nc.vector.tensor_tensor(out=gate[0:tn], in0=cv[0:tn], in1=cv[0:tn, :, 0:1].to_broadcast([tn, 8, 16]), op=ALU.subtract), reads=[cv], writes=[gate])
            k.op(k.act, lambda: nc.scalar.activation(out=gate[0:tn], in_=gate[0:tn], func=AF.Exp), reads=[gate], writes=[gate])
            k.op(k.dve, lambda: nc.vector.tensor_reduce(out=gsum[0:tn], in_=gate[0:tn], axis=AX.X, op=ALU.add), reads=[gate], writes=[gsum])
            k.op(k.dve, lambda: nc.vector.reciprocal(out=gsum[0:tn], in_=gsum[0:tn]), reads=[gsum], writes=[gsum])
            k.op(k.dve, lambda: nc.vector.tensor_tensor(out=gate[0:tn], in0=gate[0:tn], in1=gsum[0:tn].unsqueeze(2).to_broadcast([tn, 8, 16]), op=ALU.mult), reads=[gate, gsum], writes=[gate])
            pb = pbanks.next()
            P.tr(pb, pb[:, 0:tn], idxf, idxf[0:tn, :], idt, idt[0:tn, 0:tn])
            k.op(k.dve, lambda: nc.vector.tensor_copy(out=idxT[:, 0:tn], in_=pb[:, 0:tn]), reads=[pb], writes=[idxT])
            k.op(k.dve, lambda: nc.vector.memset(hid2[:], 0.0), writes=[hid2])
            k.op(k.act, lambda: nc.scalar.copy(out=xmb[0:tn, :], in_=xm[0:tn, :]), reads=[xm], writes=[xmb])
            for t in range(tn):
                ug = Ur.next()
                k.dma(k.pool, lambda: nc.gpsimd.indirect_dma_start(out=ug[:, :], out_offset=None, in_=UB.t,
                                                                     in_offset=bass.IndirectOffsetOnAxis(ap=idxT[:, t:t + 1], axis=0)),
                      reads=[UB, idxT], writes=[ug])
                sel = selr.next()
                k.op(k.act, lambda: nc.scalar.activation(out=sel[0:tn, :], in_=onesb[0:tn, :], func=AF.Copy, scale=idt[0:tn, t:t + 1]),
                     reads=[onesb, idt], writes=[sel])
                for hh in range(2):
                    for j in range(4):
                        pbj = P.psum[hh * 4 + j]
                        P.mm(pbj, pbj[:, 0:512], sel, sel[0:tn, :], xmb, xmb[0:tn, (hh * 4 + j) * 512:(hh * 4 + j + 1) * 512], True, True)
                    pgrp = [P.psum[hh * 4 + j] for j in range(4)]
                    k.op(k.dve, lambda: nc.vector.scalar_tensor_tensor(out=junkb[:, hh * 2048:(hh + 1) * 2048], in0=ug[:, hh * 2048:(hh + 1) * 2048], scalar=1.0,
                                                                         in1=P.psum_all[:, hh * 4:(hh + 1) * 4, :].rearrange("p b c -> p (b c)"),
                                                                         op0=ALU.mult, op1=ALU.mult, accum_out=hid2[:, hh, t:t + 1]),
                         reads=[ug] + pgrp, writes=[junkb, hid2])
            k.op(k.dve, lambda: nc.vector.tensor_tensor(out=hidT[:], in0=hid2[:, 0, :], in1=hid2[:, 1, :], op=ALU.add), reads=[hid2], writes=[hidT])
            pb = pbanks.next()
            P.tr(pb, pb[0:tn, 0:128], hidT, hidT[:, 0:tn], idt, idt[:])
            k.op(k.act, lambda: nc.scalar.activation(out=hid[0:tn], in_=pb[0:tn, 0:128], func=AF.Gelu), reads=[pb], writes=[hid])
            k.op(k.dve, lambda: nc.vector.tensor_tensor(out=hid[0:tn], in0=hid[0:tn], in1=gate[0:tn].rearrange("p h k -> p (h k)"), op=ALU.mult), reads=[hid, gate], writes=[hid])
            pb = pbanks.next()
            P.tr(pb, pb[:, 0:tn], hid, hid[0:tn, :], idt, idt[0:tn, 0:tn])
            k.op(k.dve, lambda: nc.vector.tensor_copy(out=coefT[:, 0:tn], in_=pb[:, 0:tn]), reads=[pb], writes=[coefT])
            for t in range(tn):
                vg = Vr.next(); zt = Zr.next()
                k.dma(k.pool, lambda: nc.gpsimd.indirect_dma_start(out=vg[:, :], out_offset=None, in_=VB.t,
                                                                     in_offset=bass.IndirectOffsetOnAxis(ap=idxT[:, t:t + 1], axis=0)),
                      reads=[VB, idxT], writes=[vg])
                k.op(k.dve, lambda: nc.vector.tensor_scalar(out=zt[:], in0=ohw[:, 127 - t:255 - t], scalar1=coefT[:, t:t + 1], scalar2=None, op0=ALU.mult),
                     reads=[ohw, coefT], writes=[zt])
                for j in range(8):
                    pbj = P.psum[j]
                    P.mm(pbj, pbj[:, 0:512], zt, zt[:], vg, vg[:, j * 512:(j + 1) * 512], t == 0, t == tn - 1)
            for j in range(8):
                pbj = P.psum[j]
                k.op(k.dve, lambda: nc.vector.scalar_tensor_tensor(out=xm[0:tn, j * 512:(j + 1) * 512], in0=xm[0:tn, j * 512:(j + 1) * 512], scalar=ALPHA,
                                                                     in1=pbj[0:tn, 0:512], op0=ALU.mult, op1=ALU.add), reads=[xm, pbj], writes=[xm])
            ln_rows(P, (junk, st), xm, tn, gbc, bbc)
            if callable(dst_tok):
                db, dap = dst_tok(t0, tn)
                P.st(db, dap, xm, xm[0:tn, :])
            else:
                P.st(dst_tok, dst_tok.t[t0:t0 + tn, :], xm, xm[0:tn, :])
            if dst_T is not None:
                to_featT(P, pbanks, xm, tn, idt, None, dst_T, t0, stgT)


def phase_sample_caches(P):
    k, nc = P.k, P.nc
    PS = P.scr["PS"]
    specs = [("ca_k", "a_k_s", 128, 2560, 512), ("ca_v", "a_v_s", 128, 3072, 512),
             ("cb_k", "b_k_s", 2048, 5120, 1536), ("cb_v", "b_v_s", 2048, 6656, 1536)]
    for cin, cout, L, c0, w in specs:
        ci, co = P.ins[cin], P.outs[cout]
        for b in range(4):
            k.dma(k.sp, lambda: nc.sync.dma_start(out=co.t[b, 0:L - 4, :], in_=ci.t[b, 4:L, :]), reads=[ci], writes=[co], acc=True)
            k.dma(k.sp, lambda: nc.sync.dma_start(out=co.t[b, L - 4:L, :], in_=PS.t[4 * b:4 * b + 4, c0:c0 + w]), reads=[PS], writes=[co], acc=True)


def phase_sample_attn(P):
    nc, k = P.nc, P.k
    PS, GGD, OT = P.scr["PS"], P.scr["GGD"], P.scr["OT"]
    with PhaseStack() as es:
        idt = P.sb(es, "idt", [128, 128], F32)
        P.ld(idt, idt[:], P.ins["ident"], P.ins["ident"].t)
        ohs = P.sb(es, "ohs", [128, 31], F32)
        P.ld(ohs, ohs[:], P.ins["ohs"], P.ins["ohs"].t)
        eb = P.sb(es, "ebS", [128, 3, 32], F32)
        for di in range(3):
            src = bass.AP(tensor=GGD.t.tensor, offset=di * 32 * 384 + 127, ap=[[1, 128], [384, 32]])
            k.dma(k.sp, lambda: nc.sync.dma_start(out=eb[:, di, :], in_=src, allow_slow_non_contiguous=True), reads=[GGD], writes=[eb])
        es0 = P.sb(es, "es0", [16, 32], F32)
        src = bass.AP(tensor=GGD.t.tensor, offset=255, ap=[[0, 16], [384, 32]])
        k.dma(k.sp, lambda: nc.sync.dma_start(out=es0[:], in_=src, allow_slow_non_contiguous=True), reads=[GGD], writes=[es0])
        esk = P.sb(es, "eskS", [16, 20], F32)
        sinks = P.ins["att_sinks"]
        P.ld(esk, esk[:], sinks, sinks.t.partition_broadcast(16))
        k.op(k.act, lambda: nc.scalar.activation(out=esk[:], in_=esk[:], func=AF.Exp), reads=[esk], writes=[esk])
        pst = P.sb(es, "pst", [16, 8192], F32)
        P.ld(pst, pst[:], PS, PS.t)
        qbc = P.sbring(es, "qbcS", [128, 2560], F32, 2)
        Kt = P.sbring(es, "KtS", [128, 1536], F32, 2)
        Vt = P.sbring(es, "VtS", [128, 1536], F32, 2)
        prod = P.sb(es, "prodS", [128, 2560], F32)
        Sx = P.sbring(es, "SxS", [128, 20], F32, 2)
        osb = P.sb(es, "osb", [16, D], F32)
        zsb = P.sb(es, "zsb", [16, 32], F32)
        pself = P.sb(es, "pself", [16, 32], F32)
        ptmp = P.sb(es, "ptmp", [16, 2560], F32)
        stgT = P.sbring(es, "stgTS", [128, KC, 128], BF16, 1)

        def variant(mix, di, d, b, sidx, first, last):
            bs = 4 * b + sidx
            H, rep, nkv = (20, 5, 4) if mix == "A" else (12, 1, 12)
            w = nkv * 128
            qc0, kc0, vc0 = (0, 2560, 3072) if mix == "A" else (3584, 5120, 6656)
            ck, cv = (P.ins["ca_k"], P.ins["ca_v"]) if mix == "A" else (P.ins["cb_k"], P.ins["cb_v"])
            L = 128 if mix == "A" else 2048
            q = qbc.next()
            P.ld(q, q[:, 0:H * 128], PS, PS.t[bs:bs + 1, qc0:qc0 + H * 128].partition_broadcast(128))
            kt, vt = Kt.next(), Vt.next()
            base = L + sidx - 128 * d
            ncache = 128 - sidx if d == 1 else 128
            for (tile, cin, c0) in ((kt, ck, kc0), (vt, cv, vc0)):
                src = cin.t[b, base:base + d * (ncache - 1) + 1:d, :] if d > 1 else cin.t[b, base:base + ncache, :]
                P.ld(tile, tile[0:ncache, 0:w], cin, src)
                if ncache < 128:
                    P.ld(tile, tile[ncache:128, 0:w], PS, PS.t[4 * b:4 * b + sidx, c0:c0 + w])
            pv = prod[:, 0:H * 128].rearrange("p (g r c) -> p g r c", g=nkv, r=rep)
            kv = kt[:, 0:w].rearrange("p (g c) -> p g c", g=nkv).unsqueeze(2).to_broadcast([128, nkv, rep, 128])
            qv = q[:, 0:H * 128].rearrange("p (g r c) -> p g r c", g=nkv, r=rep)
            k.op(k.dve, lambda: nc.vector.tensor_tensor(out=pv, in0=kv, in1=qv, op=ALU.mult), reads=[kt, q], writes=[prod])
            sx = Sx.next()
            k.op(k.dve, lambda: nc.vector.tensor_reduce(out=sx[:, 0:H], in_=prod[:, 0:H * 128].rearrange("p (h c) -> p h c", h=H), axis=AX.X, op=ALU.add),
                 reads=[prod], writes=[sx])
            k.op(k.act, lambda: nc.scalar.activation(out=sx[:, 0:H], in_=sx[:, 0:H], func=AF.Exp, scale=SCALE), reads=[sx], writes=[sx])
            h0 = 0 if mix == "A" else 20
            k.op(k.dve, lambda: nc.vector.tensor_tensor(out=sx[:, 0:H], in0=sx[:, 0:H], in1=eb[:, di, h0:h0 + H], op=ALU.mult), reads=[sx, eb], writes=[sx])
            vv = vt[:, 0:w].rearrange("p (g c) -> p g c", g=nkv).unsqueeze(2).to_broadcast([128, nkv, rep, 128])
            sv_ = sx[:, 0:H].rearrange("p (g r) -> p g r", g=nkv).unsqueeze(3).to_broadcast([128, nkv, rep, 128])
            k.op(k.dve, lambda: nc.vector.tensor_tensor(out=pv, in0=vv, in1=sv_, op=ALU.mult), reads=[vt, sx], writes=[prod])
            lh = ohs[:, 15 - bs:31 - bs]
            nb = (H * 128) // 512
            for j in range(nb):
                pb = P.psum[j]
                P.mm(pb, pb[0:16, 0:512], ohs, lh, prod, prod[:, j * 512:(j + 1) * 512], first, last)
            pb = P.psum[7]
            P.mm(pb, pb[0:16, 0:H], ohs, lh, sx, sx[:, 0:H], first, last)

        def self_and_norm(mix, mult):
            H, rep, nkv = (20, 5, 4) if mix == "A" else (12, 1, 12)
            qc0, kc0, vc0 = (0, 2560, 3072) if mix == "A" else (3584, 5120, 6656)
            h0 = 0 if mix == "A" else 20
            oc0 = 0 if mix == "A" else 2560
            pt = ptmp[:, 0:H * 128].rearrange("p (g r c) -> p g r c", g=nkv, r=rep)
            qv = pst[:, qc0:qc0 + H * 128].rearrange("p (g r c) -> p g r c", g=nkv, r=rep)
            kv = pst[:, kc0:kc0 + nkv * 128].rearrange("p (g c) -> p g c", g=nkv).unsqueeze(2).to_broadcast([16, nkv, rep, 128])
            vv = pst[:, vc0:vc0 + nkv * 128].rearrange("p (g c) -> p g c", g=nkv).unsqueeze(2).to_broadcast([16, nkv, rep, 128])
            k.op(k.dve, lambda: nc.vector.tensor_tensor(out=pt, in0=kv, in1=qv, op=ALU.mult), reads=[pst], writes=[ptmp])
            ps_ = pself[:, h0:h0 + H]
            k.op(k.dve, lambda: nc.vector.tensor_reduce(out=ps_, in_=ptmp[:, 0:H * 128].rearrange("p (h c) -> p h c", h=H), axis=AX.X, op=ALU.add), reads=[ptmp], writes=[pself])
            k.op(k.act, lambda: nc.scalar.activation(out=ps_, in_=ps_, func=AF.Exp, scale=SCALE), reads=[pself], writes=[pself])
            k.op(k.dve, lambda: nc.vector.scalar_tensor_tensor(out=ps_, in0=ps_, scalar=float(mult), in1=es0[:, h0:h0 + H], op0=ALU.mult, op1=ALU.mult), reads=[pself, es0], writes=[pself])
            zt = zsb[:, h0:h0 + H]
            pb7 = P.psum[7]
            k.op(k.dve, lambda: nc.vector.tensor_tensor(out=zt, in0=pb7[0:16, 0:H], in1=ps_, op=ALU.add), reads=[pb7, pself], writes=[zsb])
            if mix == "A":
                k.op(k.dve, lambda: nc.vector.tensor_tensor(out=zt, in0=zt, in1=esk[:, 0:20], op=ALU.add), reads=[zsb, esk], writes=[zsb])
            k.op(k.dve, lambda: nc.vector.reciprocal(out=zt, in_=zt), reads=[zsb], writes=[zsb])
            k.op(k.dve, lambda: nc.vector.tensor_tensor(out=pt, in0=vv, in1=ps_.rearrange("p (g r) -> p g r", g=nkv).unsqueeze(3).to_broadcast([16, nkv, rep, 128]), op=ALU.mult),
                 reads=[pst, pself], writes=[ptmp])
            for j in range((H * 128) // 512):
                pb = P.psum[j]
                k.op(k.dve, lambda: nc.vector.tensor_tensor(out=ptmp[:, j * 512:(j + 1) * 512], in0=ptmp[:, j * 512:(j + 1) * 512], in1=pb[0:16, 0:512], op=ALU.add),
                     reads=[ptmp, pb], writes=[ptmp])
            k.op(k.dve, lambda: nc.vector.tensor_tensor(out=osb[:, oc0:oc0 + H * 128].rearrange("p (h c) -> p h c", h=H),
                                                          in0=ptmp[:, 0:H * 128].rearrange("p (h c) -> p h c", h=H),
                                                          in1=zt.unsqueeze(2).to_broadcast([16, H, 128]), op=ALU.mult), reads=[ptmp, zsb], writes=[osb])

        for b in range(4):
            for sidx in range(4):
                variant("A", 0, 1, b, sidx, b == 0 and sidx == 0, b == 3 and sidx == 3)
        self_and_norm("A", 1)
        n = 0
        for b in range(4):
            for sidx in range(4):
                for di, d in enumerate(DILS):
                    variant("B", di, d, b, sidx, n == 0, n == 47)
                    n += 1
        self_and_norm("B", 3)
        to_featT(P, Ring(P.psum), osb, NS, idt, None, OT, SEQ, stgT)


EXPM05 = float(np.exp(-0.5))


def phase_rwkv_prep(P):
    nc, k = P.nc, P.k
    X1T = P.scr["X1T"]
    RC, WC, AC, BC, KFC, VC, GC, A1C = (P.scr[n] for n in ("RC", "WC", "AC", "BC", "KFC", "VC", "GC", "A1C"))
    with PhaseStack() as es:
        idt = P.sb(es, "idt", [128, 128], F32)
        P.ld(idt, idt[:], P.ins["ident"], P.ins["ident"].t)
        bones = P.sb(es, "bones", [128, 128], F32)
        P.ld(bones, bones[:], P.ins["bones"], P.ins["bones"].t)
        PF = P.sb(es, "PF", [128, KC, 10], F32)
        OMM = P.sb(es, "OMM", [128, KC, 6], F32)
        shT = P.sb(es, "shT", [128, KC, 4], BF16)
        es_setup = PhaseStack()
        es_setup.__enter__()
        prm = P.sb(es_setup, "prm", [10, D], F32)
        P.ld(prm, prm[0:6, :], P.ins["rw_mu"], P.ins["rw_mu"].t)
        for i, nm in enumerate(("rw_w0", "rw_a0", "rw_k_k", "rw_k_a")):
            P.ld(prm, prm[6 + i:7 + i, :], P.ins[nm], P.ins[nm].t)
        pb = P.psum[0]
        for kc in range(KC):
            P.tr(pb, pb[:, kc * 10:kc * 10 + 10], prm, prm[0:10, kc * 128:(kc + 1) * 128], idt, idt[0:10, 0:10])
        k.op(k.dve, lambda: nc.vector.tensor_copy(out=PF[:], in_=pb[:, 0:320].rearrange("p (c i) -> p c i", i=10)), reads=[pb], writes=[PF])
        k.op(k.dve, lambda: nc.vector.tensor_scalar(out=OMM[:], in0=PF[:, :, 0:6], scalar1=-1.0, scalar2=1.0, op0=ALU.mult, op1=ALU.add), reads=[PF], writes=[OMM])
        shs = P.sb(es_setup, "shs", [4, D], F32)
        P.ld(shs, shs[:], P.ins["st_shift"], P.ins["st_shift"].t)
        pb = P.psum[1]
        for kc in range(KC):
            P.tr(pb, pb[:, kc * 4:kc * 4 + 4], shs, shs[0:4, kc * 128:(kc + 1) * 128], idt, idt[0:4, 0:4])
        k.op(k.dve, lambda: nc.vector.tensor_copy(out=shT[:], in_=pb[:, 0:128].rearrange("p (c i) -> p c i", i=4)), reads=[pb], writes=[shT])
        es_setup.__exit__(None, None, None)

        NH = 1024
        xh = P.sb(es, "xh", [128, KC, 1 + NH + NS], BF16)
        pvs = P.sb(es, "pvs", [128, KC, NS], BF16)
        xi = P.sb(es, "xi", [128, KC, NH + NS], BF16)
        hw = P.sb(es, "hw", [128, NH + NS], BF16)
        ha = P.sb(es, "ha", [128, NH + NS], BF16)
        hg = P.sb(es, "hg", [128, 4, NH + NS], BF16)
        stg = P.sbring(es, "stgR", [128, 512], F32, 4)
        ew = P.sbring(es, "ewR", [128, 512], F32, 6)

        for hf in range(2):
            c0 = hf * NH
            nsamp = NS if hf == 1 else 0
            ncol = NH + nsamp
            chunks = [(0, 512), (512, 512)] + ([(NH, NS)] if hf == 1 else [])
            gcol = lambda t0: (c0 + t0) if t0 < NH else SEQ
            if hf == 0:
                k.op(k.dve, lambda: nc.vector.memset(xh[:, :, 0:1], 0.0), writes=[xh])
                for q0 in range(0, KC, 8):
                    P.ld(xh, xh[:, q0:q0 + 8, 1:1 + NH], X1T, X1T.t[q0:q0 + 8, :, 0:NH].rearrange("c p t -> p c t"))
            else:
                for q0 in range(0, KC, 8):
                    P.ld(xh, xh[:, q0:q0 + 8, 0:1 + NH], X1T, X1T.t[q0:q0 + 8, :, NH - 1:2 * NH].rearrange("c p t -> p c t"))
                    P.ld(xh, xh[:, q0:q0 + 8, 1 + NH:1 + NH + NS], X1T, X1T.t[q0:q0 + 8, :, SEQ:SEQ + NS].rearrange("c p t -> p c t"))
                xs4 = xh[:, :, 1 + NH:1 + NH + NS].rearrange("p c (b s) -> p c b s", s=4)
                pv4 = pvs[:].rearrange("p c (b s) -> p c b s", s=4)
                k.op(k.dve, lambda: nc.vector.tensor_copy(out=pv4[:, :, :, 1:4], in_=xs4[:, :, :, 0:3]), reads=[xh], writes=[pvs])
                k.op(k.dve, lambda: nc.vector.tensor_copy(out=pv4[:, :, :, 0], in_=shT[:]), reads=[shT], writes=[pvs])

            def mix(i):
                for kc in range(KC):
                    eng = k.dve
                    e = nc.vector
                    k.op(eng, lambda: e.tensor_scalar(out=xi[:, kc, 0:NH], in0=xh[:, kc, 1:1 + NH], scalar1=OMM[:, kc, i:i + 1], scalar2=None, op0=ALU.mult),
                         reads=[xh, OMM], writes=[xi])
                    k.op(eng, lambda: e.scalar_tensor_tensor(out=xi[:, kc, 0:NH], in0=xh[:, kc, 0:NH], scalar=PF[:, kc, i:i + 1], in1=xi[:, kc, 0:NH], op0=ALU.mult, op1=ALU.add),
                         reads=[xh, PF, xi], writes=[xi])
                    if nsamp:
                        k.op(eng, lambda: e.tensor_scalar(out=xi[:, kc, NH:NH + NS], in0=xh[:, kc, 1 + NH:1 + NH + NS], scalar1=OMM[:, kc, i:i + 1], scalar2=None, op0=ALU.mult),
                             reads=[xh, OMM], writes=[xi])
                        k.op(eng, lambda: e.scalar_tensor_tensor(out=xi[:, kc, NH:NH + NS], in0=pvs[:, kc, :], scalar=PF[:, kc, i:i + 1], in1=xi[:, kc, NH:NH + NS], op0=ALU.mult, op1=ALU.add),
                             reads=[pvs, PF, xi], writes=[xi])

            mix(1)
            P.wsrc = P.ins["rw_w1"]
            proj_pass(P, xi, P.ins["rw_w1"].t, 128, "A", lambda ci, t0, tn, pb: k.op(
                k.act, lambda: nc.scalar.activation(out=hw[:, t0:t0 + tn], in_=pb[:, 0:tn], func=AF.Tanh), reads=[pb], writes=[hw]), chunks=chunks, blk=128)
            mix(4)
            P.wsrc = P.ins["rw_a1"]
            proj_pass(P, xi, P.ins["rw_a1"].t, 128, "A", lambda ci, t0, tn, pb: k.op(
                k.act, lambda: nc.scalar.copy(out=ha[:, t0:t0 + tn], in_=pb[:, 0:tn]), reads=[pb], writes=[ha]), chunks=chunks, blk=128)
            mix(5)
            P.wsrc = P.ins["rw_g1"]
            for ci in range(4):
                cw = 128 if ci < 3 else 96
                proj_pass(P, xi, P.ins["rw_g1"].t[:, ci * 128:ci * 128 + cw], cw, "A", lambda _c, t0, tn, pb, ci=ci, cw=cw: k.op(
                    k.act, lambda: nc.scalar.activation(out=hg[0:cw, ci, t0:t0 + tn], in_=pb[0:cw, 0:tn], func=AF.Sigmoid), reads=[pb], writes=[hg]), chunks=chunks, blk=cw)

            def second(Wb, hsrc, hview, kparts, emit2):
                P.wsrc = Wb
                with PhaseStack() as es2:
                    wr = P.sbring(es2, "w2blk", [128, 4, 256], BF16, 2)
                    pbk = Ring(P.psum)
                    for cb in range(0, D, 256):
                        wb = wr.next()
                        for ki, (r0, rn) in enumerate(kparts):
                            P.ld(wb, wb[0:rn, ki, :], Wb, Wb.t[r0:r0 + rn, cb:cb + 256], eng=k.pool)
                        for h in range(2):
                            gch = cb // 128 + h
                            for (t0, tn) in chunks:
                                pb = pbk.next()
                                for ki, (r0, rn) in enumerate(kparts):
                                    P.mm(pb, pb[:, 0:tn], wb, wb[0:rn, ki, h * 128:(h + 1) * 128], hsrc, hview(ki, rn, t0, tn), ki == 0, ki == len(kparts) - 1)
                                emit2(gch, t0, tn, pb)

            def emit_a(gch, t0, tn, pb):
                s_ = stg.next()
                k.op(k.act, lambda: nc.scalar.activation(out=s_[:, 0:tn], in_=pb[:, 0:tn], func=AF.Sigmoid, bias=PF[:, gch, 7:8]), reads=[pb, PF], writes=[s_])
                P.st(A1C, A1C.t[gch * 128:(gch + 1) * 128, gcol(t0):gcol(t0) + tn], s_, s_[:, 0:tn])

            def emit_w(gch, t0, tn, pb):
                s_ = stg.next()
                k.op(k.act, lambda: nc.scalar.activation(out=s_[:, 0:tn], in_=pb[:, 0:tn], func=AF.Sigmoid, bias=PF[:, gch, 6:7]), reads=[pb, PF], writes=[s_])
                k.op(k.act, lambda: nc.scalar.activation(out=s_[:, 0:tn], in_=s_[:, 0:tn], func=AF.Exp, scale=-EXPM05), reads=[s_], writes=[s_])
                P.st(WC, WC.t[gch * 128:(gch + 1) * 128, gcol(t0):gcol(t0) + tn], s_, s_[:, 0:tn])

            def emit_g(gch, t0, tn, pb):
                s_ = stg.next()
                P.evac(s_, s_[:, 0:tn], pb, pb[:, 0:tn])
                P.st(GC, GC.t[gch * 128:(gch + 1) * 128, gcol(t0):gcol(t0) + tn], s_, s_[:, 0:tn])

            second(P.ins["rw_a2"], ha, lambda ki, rn, t0, tn: ha[0:rn, t0:t0 + tn], [(0, 128)], emit_a)
            second(P.ins["rw_w2"], hw, lambda ki, rn, t0, tn: hw[0:rn, t0:t0 + tn], [(0, 128)], emit_w)
            second(P.ins["rw_g2"], hg, lambda ki, rn, t0, tn: hg[0:rn, ki, t0:t0 + tn], [(0, 128), (128, 128), (256, 128), (384, 96)], emit_g)

            def emit_plain(dst):
                def f(ci, t0, tn, pb):
                    s_ = stg.next()
                    P.evac(s_, s_[:, 0:tn], pb, pb[:, 0:tn])
                    P.st(dst, dst.t[ci * 128:(ci + 1) * 128, gcol(t0):gcol(t0) + tn], s_, s_[:, 0:tn])
                return f
            P.wsrc = P.ins["rw_w_rkv"]
            mix(0)
            proj_pass(P, xi, P.ins["rw_w_rkv"].t[0], D, "A", emit_plain(RC), chunks=chunks, blk=128)
            mix(3)
            proj_pass(P, xi, P.ins["rw_w_rkv"].t[2], D, "A", emit_plain(VC), chunks=chunks, blk=128)
            mix(2)

            def emit_k(gch, t0, tn, pb):
                kraw, a_t, kkn, sq, t1 = ew.next(), ew.next(), ew.next(), ew.next(), ew.next()
                gc_ = gcol(t0)
                P.evac(kraw, kraw[:, 0:tn], pb, pb[:, 0:tn])
                P.ld(a_t, a_t[:, 0:tn], A1C, A1C.t[gch * 128:(gch + 1) * 128, gc_:gc_ + tn])
                k.op(k.dve, lambda: nc.vector.tensor_scalar(out=kkn[:, 0:tn], in0=kraw[:, 0:tn], scalar1=PF[:, gch, 8:9], scalar2=None, op0=ALU.mult), reads=[kraw, PF], writes=[kkn])
                k.op(k.pool, lambda: nc.gpsimd.tensor_tensor(out=sq[:, 0:tn], in0=kkn[:, 0:tn], in1=kkn[:, 0:tn], op=ALU.mult), reads=[kkn], writes=[sq])
                pb2 = P.psum[7]
                P.mm(pb2, pb2[:, 0:tn], bones, bones[:], sq, sq[:, 0:tn], True, True)
                k.op(k.act, lambda: nc.scalar.activation(out=sq[:, 0:tn], in_=pb2[:, 0:tn], func=AF.Sqrt), reads=[pb2], writes=[sq])
                k.op(k.dve, lambda: nc.vector.tensor_scalar_max(out=sq[:, 0:tn], in0=sq[:, 0:tn], scalar1=1e-12), reads=[sq], writes=[sq])
                k.op(k.dve, lambda: nc.vector.reciprocal(out=sq[:, 0:tn], in_=sq[:, 0:tn]), reads=[sq], writes=[sq])
                k.op(k.dve, lambda: nc.vector.tensor_tensor(out=kkn[:, 0:tn], in0=kkn[:, 0:tn], in1=sq[:, 0:tn], op=ALU.mult), reads=[kkn, sq], writes=[kkn])
                k.op(k.pool, lambda: nc.gpsimd.tensor_scalar(out=sq[:, 0:tn], in0=kkn[:, 0:tn], scalar1=-1.0, scalar2=None, op0=ALU.mult), reads=[kkn], writes=[sq])
                P.st(AC, AC.t[gch * 128:(gch + 1) * 128, gc_:gc_ + tn], sq, sq[:, 0:tn])
                k.op(k.dve, lambda: nc.vector.tensor_tensor(out=kkn[:, 0:tn], in0=kkn[:, 0:tn], in1=a_t[:, 0:tn], op=ALU.mult), reads=[kkn, a_t], writes=[kkn])
                P.st(BC, BC.t[gch * 128:(gch + 1) * 128, gc_:gc_ + tn], kkn, kkn[:, 0:tn])
                k.op(k.dve, lambda: nc.vector.tensor_scalar(out=t1[:, 0:tn], in0=a_t[:, 0:tn], scalar1=-1.0, scalar2=PF[:, gch, 9:10], op0=ALU.add, op1=ALU.mult), reads=[a_t, PF], writes=[t1])
                k.op(k.dve, lambda: nc.vector.scalar_tensor_tensor(out=t1[:, 0:tn], in0=t1[:, 0:tn], scalar=1.0, in1=kraw[:, 0:tn], op0=ALU.add, op1=ALU.mult), reads=[t1, kraw], writes=[t1])
                P.st(KFC, KFC.t[gch * 128:(gch + 1) * 128, gc_:gc_ + tn], t1, t1[:, 0:tn])
            proj_pass(P, xi, P.ins["rw_w_rkv"].t[1], D, "A", emit_k, chunks=chunks, blk=128)


GN_EPS = 64e-5
TCH = 32


def phase_rwkv_scan(P):
    nc, k = P.nc, P.k
    RC, WC, AC, BC, KFC, VC, YC = (P.scr[n] for n in ("RC", "WC", "AC", "BC", "KFC", "VC", "YC"))
    with PhaseStack() as es:
        idt = P.sb(es, "idt", [128, 128], F32)
        P.ld(idt, idt[:], P.ins["ident"], P.ins["ident"].t)
        maskp = P.sb(es, "maskp", [128, 2], F32); P.ld(maskp, maskp[:], P.ins["maskp"], P.ins["maskp"].t)
        maskh = P.sb(es, "maskh", [128, 2], F32); P.ld(maskh, maskh[:], P.ins["maskh"], P.ins["maskh"].t)
        maskg = P.sb(es, "maskg", [128, 32], F32); P.ld(maskg, maskg[:], P.ins["maskg"], P.ins["maskg"].t)
        rk = P.sb(es, "rk", [64, 64], F32); P.ld(rk, rk[:], P.ins["rw_r_k"], P.ins["rw_r_k"].t)
        gng = P.sb(es, "gng", [64, 64], F32); P.ld(gng, gng[:], P.ins["rw_gn_g"], P.ins["rw_gn_g"].t.rearrange("o (r v) -> (o r) v", v=64))
        gnb = P.sb(es, "gnb", [64, 64], F32); P.ld(gnb, gnb[:], P.ins["rw_gn_b"], P.ins["rw_gn_b"].t.rearrange("o (r v) -> (o r) v", v=64))
        MM = [P.sb(es, "MA", [128, 2048], F32), P.sb(es, "MB", [128, 2048], F32)]
        Mw = P.sb(es, "Mw", [128, 2048], F32)
        RVr = P.sbring(es, "RV", [128, 2048], F32, 2)
        ytmp = P.sb(es, "ytmp", [64, 2048], F32)
        aF = P.sb(es, "aF", [128, 32, TCH], F32); wF = P.sb(es, "wF", [128, 32, TCH], F32); rF = P.sb(es, "rF", [128, 32, TCH], F32)
        LU = P.sb(es, "LU", [128, TCH, 64], F32); LRb = P.sb(es, "LRb", [128, TCH, 64], BF16)
        Mb = P.sb(es, "Mb", [128, 2048], BF16)
        PU = Buf("psU", P.psum_all[:, 0:4, :]); PO = Buf("psO", P.psum_all[:, 4:8, :])
        LBK = P.sb(es, "LBK", [128, TCH, 128], F32)
        BK = P.sb(es, "BK", [128, 64, TCH], F32); VH = P.sb(es, "VH", [128, 64, TCH], F32)
        RH = P.sb(es, "RH", [64, 64, TCH], F32); KH = P.sb(es, "KH", [64, 64, TCH], F32)
        yacc = P.sb(es, "yacc", [64, 64, TCH], F32)
        t1 = P.sb(es, "gt1", [64, 64, TCH], F32); t2 = P.sb(es, "gt2", [64, 64, TCH], F32)
        st = P.sb(es, "gst", [64, 6, TCH], F32)
        Sio = P.sb(es, "Sio", [64, 32, 128], F32)
        cur = [0]

        def hm(src, c, n):
            return src.t[:, c:c + n].rearrange("(r q) t -> r q t", q=64)

        def load_state(b):
            sw = P.ins["st_wkv"]
            P.ld(Sio, Sio[:].rearrange("v g (h q) -> v g h q", h=2), sw, sw.t[b].rearrange("(g h) v q -> v g h q", h=2))
            M = MM[cur[0]]
            for g4 in range(8):
                pb = P.psum[g4]
                for j in range(4):
                    g = g4 * 4 + j
                    P.tr(pb, pb[:, j * 64:(j + 1) * 64], Sio, Sio[:, g, :], idt, idt[0:64, 0:64])
                P.evac(M, M[:, g4 * 256:(g4 + 1) * 256], pb, pb[:, 0:256])
            k.barrier()

        def store_state(dst_b, dst_ap):
            k.barrier()
            M = MM[cur[0]]
            for g4 in range(8):
                pb = P.psum[g4]
                for j in range(4):
                    g = g4 * 4 + j
                    P.tr(pb, pb[0:64, j * 128:(j + 1) * 128], M, M[:, g * 64:(g + 1) * 64], idt, idt[:])
                P.evac(Sio, Sio[:, g4 * 4:(g4 + 1) * 4, :], pb, pb[0:64, 0:512].rearrange("v (j q) -> v j q", j=4))
            k.dma(k.sp, lambda: nc.sync.dma_start(out=dst_ap.rearrange("(g h) v q -> v g h q", h=2), in_=Sio[:].rearrange("v g (h q) -> v g h q", h=2)),
                  reads=[Sio], writes=[dst_b], acc=True)
            k.barrier()

        def chunk(c, n):
            for (tile, src) in ((aF, AC), (wF, WC), (rF, RC)):
                P.ld(tile, tile[:, :, 0:n], src, src.t[:, c:c + n].rearrange("(g p) t -> p g t", p=128))
            P.ld(BK, BK[0:64, :, 0:n], BC, hm(BC, c, n))
            P.ld(BK, BK[64:128, :, 0:n], KFC, hm(KFC, c, n))
            P.ld(VH, VH[0:64, :, 0:n], VC, hm(VC, c, n))
            P.ld(VH, VH[64:128, :, 0:n], VC, hm(VC, c, n))
            P.ld(RH, RH[:, :, 0:n], RC, hm(RC, c, n))
            P.ld(KH, KH[:, :, 0:n], KFC, hm(KFC, c, n))
            for (L_, F_) in ((LU, aF), (LRb, rF)):
                k.op(k.pool, lambda: nc.gpsimd.tensor_tensor(out=L_[:, 0:n, :].rearrange("p t (g h) -> p t g h", h=2),
                                                             in0=F_[:, :, 0:n].rearrange("p g t -> p t g").unsqueeze(3).to_broadcast([128, n, 32, 2]),
                                                             in1=maskp[:].unsqueeze(1).unsqueeze(1).to_broadcast([128, n, 32, 2]), op=ALU.mult),
                     reads=[F_, maskp], writes=[L_])
            k.op(k.pool, lambda: nc.gpsimd.tensor_tensor(out=LBK[:, 0:n, :].rearrange("p t (h q) -> p t h q", h=2),
                                                         in0=BK[:, :, 0:n].rearrange("p q t -> p t q").unsqueeze(2).to_broadcast([128, n, 2, 64]),
                                                         in1=maskh[:].unsqueeze(1).unsqueeze(3).to_broadcast([128, n, 2, 64]), op=ALU.mult),
                 reads=[BK, maskh], writes=[LBK])
            for i in range(n):
                M = MM[cur[0]]; Mn = MM[1 - cur[0]]
                RV = RVr.next()
                k.op(k.pool, lambda: nc.gpsimd.tensor_tensor(out=RV[64:128, :].rearrange("p (g v) -> p g v", v=64),
                                                             in0=VH[64:128, :, i].unsqueeze(1).to_broadcast([64, 32, 64]),
                                                             in1=maskg[64:128, :].unsqueeze(2).to_broadcast([64, 32, 64]), op=ALU.mult),
                     reads=[VH, maskg], writes=[RV])
                for j in range(4):
                    P.mm(PU, PU.t[0:64, j, :], LU, LU[:, i, :], M, M[:, j * 512:(j + 1) * 512], True, True)
                k.op(k.dve, lambda: nc.vector.tensor_tensor(out=Mw[:].rearrange("p (g v) -> p g v", v=64), in0=M[:].rearrange("p (g v) -> p g v", v=64),
                                                            in1=wF[:, :, i].unsqueeze(2).to_broadcast([128, 32, 64]), op=ALU.mult),
                     reads=[M, wF], writes=[Mw])
                k.op(k.dve, lambda: nc.vector.tensor_tensor(out=RV[0:64, :].rearrange("p (g v) -> p g v", v=64),
                                                            in0=PU.t[0:64, :, :].rearrange("p b (g v) -> p (b g) v", v=64),
                                                            in1=maskg[0:64, :].unsqueeze(2).to_broadcast([64, 32, 64]), op=ALU.mult),
                     reads=[PU, maskg], writes=[RV])
                for j in range(4):
                    P.mm(PO, PO.t[:, j, :], LBK, LBK[:, i, :], RV, RV[:, j * 512:(j + 1) * 512], True, True)
                k.op(k.dve, lambda: nc.vector.tensor_tensor(out=Mn[:], in0=Mw[:], in1=PO.t[:, :, :].rearrange("p b c -> p (b c)"), op=ALU.add),
                     reads=[Mw, PO], writes=[Mn])
                cur[0] = 1 - cur[0]
                k.op(k.act, lambda: nc.scalar.copy(out=Mb[:], in_=Mn[:]), reads=[Mn], writes=[Mb])
                for j in range(4):
                    P.mm(PU, PU.t[0:64, j, :], LRb, LRb[:, i, :], Mb, Mb[:, j * 512:(j + 1) * 512], True, True)
                k.op(k.act, lambda: nc.scalar.copy(out=ytmp[:], in_=PU.t[0:64, :, :].rearrange("p b c -> p (b c)")), reads=[PU], writes=[ytmp])
                k.op(k.pool, lambda: nc.gpsimd.tensor_tensor(out=ytmp[:].rearrange("p (g v) -> p g v", v=64), in0=ytmp[:].rearrange("p (g v) -> p g v", v=64),
                                                             in1=maskg[0:64, :].unsqueeze(2).to_broadcast([64, 32, 64]), op=ALU.mult), reads=[ytmp, maskg], writes=[ytmp])
                k.op(k.dve, lambda: nc.vector.tensor_reduce(out=yacc[:, :, i], in_=ytmp[:].rearrange("p (g v) -> p v g", v=64), axis=AX.X, op=ALU.add),
                     reads=[ytmp], writes=[yacc])
            yv = yacc[:, :, 0:n]
            ytv = yv.rearrange("r v t -> r t v")
            k.op(k.dve, lambda: nc.vector.tensor_reduce(out=st[:, 0, 0:n], in_=ytv, axis=AX.X, op=ALU.add), reads=[yacc], writes=[st])
            k.op(k.dve, lambda: nc.vector.tensor_tensor(out=t1[:, :, 0:n], in0=yv, in1=yv, op=ALU.mult), reads=[yacc], writes=[t1])
            k.op(k.dve, lambda: nc.vector.tensor_reduce(out=st[:, 1, 0:n], in_=t1[:, :, 0:n].rearrange("r v t -> r t v"), axis=AX.X, op=ALU.add), reads=[t1], writes=[st])
            k.op(k.dve, lambda: nc.vector.tensor_scalar(out=st[:, 0:2, 0:n], in0=st[:, 0:2, 0:n], scalar1=1.0 / 64, scalar2=None, op0=ALU.mult), reads=[st], writes=[st])
            k.op(k.dve, lambda: nc.vector.tensor_tensor(out=st[:, 2, 0:n], in0=st[:, 0, 0:n], in1=st[:, 0, 0:n], op=ALU.mult), reads=[st], writes=[st])
            k.op(k.dve, lambda: nc.vector.tensor_tensor(out=st[:, 3, 0:n], in0=st[:, 1, 0:n], in1=st[:, 2, 0:n], op=ALU.subtract), reads=[st], writes=[st])
            k.op(k.dve, lambda: nc.vector.tensor_scalar(out=st[:, 3, 0:n], in0=st[:, 3, 0:n], scalar1=GN_EPS, scalar2=None, op0=ALU.add), reads=[st], writes=[st])
            k.op(k.act, lambda: nc.scalar.activation(out=st[:, 3, 0:n], in_=st[:, 3, 0:n], func=AF.Sqrt), reads=[st], writes=[st])
            k.op(k.dve, lambda: nc.vector.reciprocal(out=st[:, 3, 0:n], in_=st[:, 3, 0:n]), reads=[st], writes=[st])
            k.op(k.dve, lambda: nc.vector.tensor_tensor(out=t1[:, :, 0:n], in0=yv, in1=st[:, 0, 0:n].unsqueeze(1).to_broadcast([64, 64, n]), op=ALU.subtract), reads=[yacc, st], writes=[t1])
            k.op(k.dve, lambda: nc.vector.tensor_tensor(out=t1[:, :, 0:n], in0=t1[:, :, 0:n], in1=st[:, 3, 0:n].unsqueeze(1).to_broadcast([64, 64, n]), op=ALU.mult), reads=[t1, st], writes=[t1])
            k.op(k.dve, lambda: nc.vector.tensor_tensor(out=t1[:, :, 0:n], in0=t1[:, :, 0:n], in1=gng[:].unsqueeze(2).to_broadcast([64, 64, n]), op=ALU.mult), reads=[t1, gng], writes=[t1])
            k.op(k.dve, lambda: nc.vector.tensor_tensor(out=t1[:, :, 0:n], in0=t1[:, :, 0:n], in1=gnb[:].unsqueeze(2).to_broadcast([64, 64, n]), op=ALU.add), reads=[t1, gnb], writes=[t1])
            k.op(k.pool, lambda: nc.gpsimd.tensor_tensor(out=t2[:, :, 0:n], in0=RH[:, :, 0:n], in1=KH[:, :, 0:n], op=ALU.mult), reads=[RH, KH], writes=[t2])
            k.op(k.pool, lambda: nc.gpsimd.tensor_tensor(out=t2[:, :, 0:n], in0=t2[:, :, 0:n], in1=rk[:].unsqueeze(2).to_broadcast([64, 64, n]), op=ALU.mult), reads=[t2, rk], writes=[t2])
            k.op(k.dve, lambda: nc.vector.tensor_reduce(out=st[:, 4, 0:n], in_=t2[:, :, 0:n].rearrange("r q t -> r t q"), axis=AX.X, op=ALU.add), reads=[t2], writes=[st])
            k.op(k.dve, lambda: nc.vector.tensor_tensor(out=t2[:, :, 0:n], in0=VH[0:64, :, 0:n], in1=st[:, 4, 0:n].unsqueeze(1).to_broadcast([64, 64, n]), op=ALU.mult), reads=[VH, st], writes=[t2])
            k.op(k.dve, lambda: nc.vector.tensor_tensor(out=t1[:, :, 0:n], in0=t1[:, :, 0:n], in1=t2[:, :, 0:n], op=ALU.add), reads=[t1, t2], writes=[t1])
            P.st(YC, hm(YC, c, n), t1, t1[:, :, 0:n])

        k.op(k.dve, lambda: nc.vector.memset(MM[0][:], 0.0), writes=[MM[0]])
        for c in range(0, SEQ, TCH):
            chunk(c, TCH)
        store_state(P.outs["wkv_p"], P.outs["wkv_p"].t)
        for b in range(4):
            load_state(b)
            chunk(SEQ + 4 * b, 4)
            store_state(P.outs["wkv_s"], P.outs["wkv_s"].t[b])


def phase_rwkv_out(P):
    nc, k = P.nc, P.k
    YC, GC, MIX = P.scr["YC"], P.scr["GC"], P.scr["MIX"]
    with PhaseStack() as es:
        actT = P.sb(es, "actT", [128, KC, T_ALL], BF16)
        with PhaseStack() as es2:
            yr = P.sbring(es2, "yr", [128, T_ALL], F32, 2)
            gr = P.sbring(es2, "gr", [128, T_ALL], F32, 2)
            for g in range(KC):
                yt, gt = yr.next(), gr.next()
                P.ld(yt, yt[:], YC, YC.t[g * 128:(g + 1) * 128, :])
                P.ld(gt, gt[:], GC, GC.t[g * 128:(g + 1) * 128, :])
                k.op(k.dve, lambda: nc.vector.tensor_tensor(out=actT[:, g, :], in0=yt[:], in1=gt[:], op=ALU.mult), reads=[yt, gt], writes=[actT])
        stg = P.sbring(es, "stgB", [128, 256], F32, 4)
        P.wsrc = P.ins["rw_w_o"]

        def emit(c0, cn, t0, tn, pb):
            s_ = stg.next()
            P.evac(s_, s_[0:tn, 0:cn], pb, pb[0:tn, 0:cn])
            P.st(MIX, MIX.t[t0:t0 + tn, c0:c0 + cn], s_, s_[0:tn, 0:cn])
        proj_pass(P, actT, P.ins["rw_w_o"].t, D, "B", emit)


def phase_rwkv(P):
    phase_tables_bf16(P, 1)
    phase_rwkv_prep(P)
    phase_rwkv_scan(P)
    phase_rwkv_out(P)
    X1 = P.scr["X1"]
    phase_res_ln(P, lambda t0, tn: (X1, X1.t[t0:t0 + tn, :]), P.scr["MIX"], 2, P.scr["XM"], P.scr["XMT"], None)
    phase_peer_q(P, 1, P.scr["XMT"], P.scr["QPT"])
    yp, ys = P.outs["y_p"], P.outs["y_s"]
    dst = lambda t0, tn: (yp, yp.t[t0:t0 + tn, :]) if t0 < SEQ else (ys, ys.t[:, :])
    phase_peer_main(P, 1, P.scr["XM"], P.scr["XMB"], P.scr["QPT"], 3, dst, None)

def relpos_bucket_np(dist):
    n = np.maximum(np.asarray(dist, np.int64), 0)
    nf = np.maximum(n, 1).astype(np.float32)
    large = 16 + (np.log(nf / np.float32(16)) / np.float32(np.log(2048 / 16)) * np.float32(16)).astype(np.int32)
    return np.where(n < 16, n, np.minimum(large, 31))


def host_consts():
    c = {"ident": np.eye(128, dtype=np.float32)}
    sel = np.zeros((32, 3, 384), np.float32)
    msk = np.zeros((32, 384), np.float32)
    for di, d in enumerate(DILS):
        for j in range(127, 256):
            s_ = 255 - j
            sel[relpos_bucket_np(d * s_), di, j] = 1.0
    msk[:, 127:256] = 1.0
    c["sel"] = sel
    c["selmask"] = msk
    c["iota16"] = np.tile(np.arange(16, dtype=np.float32)[None, :], (128, 1))
    ohw = np.zeros((128, 255), np.float32); ohw[:, 127] = 1.0
    c["ohw"] = ohw
    ohs = np.zeros((128, 31), np.float32); ohs[:, 15] = 1.0
    c["ohs"] = ohs
    bones = np.zeros((128, 128), np.float32); bones[:64, :64] = 1.0; bones[64:, 64:] = 1.0
    c["bones"] = bones
    p_ = np.arange(128)
    c["maskp"] = (p_[:, None] // 64 == np.arange(2)[None, :]).astype(np.float32)
    c["maskh"] = (((p_ % 64) % 2)[:, None] == np.arange(2)[None, :]).astype(np.float32)
    c["maskg"] = (((p_ % 64) // 2)[:, None] == np.arange(32)[None, :]).astype(np.float32)
    return c


def build(debug=False, phases=("l0", "sample", "l1")):
    P = Prog(debug=debug)
    nc = P.nc
    P.inp("xp", [SEQ, D]); P.inp("xs", [NS, D])
    P.inp("ca_k", [4, 128, 512]); P.inp("ca_v", [4, 128, 512])
    P.inp("cb_k", [4, 2048, 1536]); P.inp("cb_v", [4, 2048, 1536])
    P.inp("st_shift", [4, D]); P.inp("st_wkv", [4, 64, 64, 64])
    P.inp("w_in_att", [D, 8192]); P.inp("w_out_att", [D, D])
    P.inp("att_sinks", [1, 20]); P.inp("rel_bias", [32, 32])
    P.inp("peer_w_q", [2, D, 2048]); P.inp("peer_sub_keys", [2, 8, 2, 128, 128])
    P.inp("peer_u", [2, 16384, D]); P.inp("peer_v", [2, 16384, D])
    P.inp("ln_g", [4, D]); P.inp("ln_b", [4, D])
    P.inp("rw_mu", [6, D]); P.inp("rw_w_rkv", [3, D, D]); P.inp("rw_w0", [1, D]); P.inp("rw_w1", [D, 128]); P.inp("rw_w2", [128, D])
    P.inp("rw_a0", [1, D]); P.inp("rw_a1", [D, 128]); P.inp("rw_a2", [128, D]); P.inp("rw_g1", [D, 480]); P.inp("rw_g2", [480, D])
    P.inp("rw_k_k", [1, D]); P.inp("rw_k_a", [1, D]); P.inp("rw_r_k", [64, 64]); P.inp("rw_gn_g", [1, D]); P.inp("rw_gn_b", [1, D])
    P.inp("rw_w_o", [D, D])
    P.inp("ident", [128, 128]); P.inp("sel", [32, 3, 384]); P.inp("selmask", [32, 384])
    P.inp("iota16", [128, 16]); P.inp("ohw", [128, 255]); P.inp("ohs", [128, 31])
    P.inp("bones", [128, 128]); P.inp("maskp", [128, 2]); P.inp("maskh", [128, 2]); P.inp("maskg", [128, 32])
    P.out("y_p", [SEQ, D]); P.out("y_s", [NS, D])
    P.out("a_k_p", [128, 512]); P.out("a_v_p", [128, 512])
    P.out("b_k_p", [SEQ, 1536]); P.out("b_v_p", [SEQ, 1536])
    P.out("shift_p", [1, D]); P.out("wkv_p", [64, 64, 64])
    P.out("a_k_s", [4, 128, 512]); P.out("a_v_s", [4, 128, 512])
    P.out("b_k_s", [4, 2048, 1536]); P.out("b_v_s", [4, 2048, 1536])
    P.out("shift_s", [4, D]); P.out("wkv_s", [4, 64, 64, 64])
    P.scratch("QK", [48, 128, SEQ], BF16)
    P.scratch("VA", [SEQ, 512], F32)
    P.scratch("PS", [NS, 8192], F32)
    P.scratch("GGD", [3, 32, 384], F32)
    P.scratch("OT", [32, 128, T_ALL], BF16)
    P.scratch("MIX", [T_ALL, D], F32)
    P.scratch("XM", [T_ALL, D], F32)
    P.scratch("XMB", [T_ALL, D], BF16)
    P.scratch("XMT", [KC, 128, T_ALL], BF16)
    P.scratch("QPT", [16, 128, T_ALL], BF16)
    P.scratch("X1", [T_ALL, D], F32, dbg=True)
    P.scratch("X1T", [KC, 128, T_ALL], BF16)
    for l in range(2):
        P.scratch("UB%d" % l, [16384, D], BF16)
        P.scratch("VB%d" % l, [16384, D], BF16)
    for n_ in ("RC", "WC", "AC", "BC", "KFC", "VC", "GC", "A1C", "YC"):
        P.scratch(n_, [D, T_ALL], F32, dbg=True)
    k = P.k
    with ExitStack() as es:
        P.psum_all = es.enter_context(nc.psum_tensor("psall", [128, 8, 512], F32))
        P.psum = [Buf("ps%d" % i, P.psum_all[:, i, :]) for i in range(8)]
        phase_tables_bf16(P, 0)
        phase_l0a(P)
        phase_sample_caches(P)
        phase_bias_setup(P)
        phase_l0b(P)
        phase_sample_attn(P)
        phase_outproj(P, P.scr["OT"], P.ins["w_out_att"], P.scr["MIX"])
        xp, xs = P.ins["xp"], P.ins["xs"]
        xres = lambda t0, tn: (xp, xp.t[t0:t0 + tn, :]) if t0 < SEQ else (xs, xs.t[:, :])
        phase_res_ln(P, xres, P.scr["MIX"], 0, P.scr["XM"], P.scr["XMT"], None)
        phase_peer_q(P, 0, P.scr["XMT"], P.scr["QPT"])
        phase_peer_main(P, 0, P.scr["XM"], P.scr["XMB"], P.scr["QPT"], 1, P.scr["X1"], P.scr["X1T"])
        X1 = P.scr["X1"]
        o = P.outs["shift_p"]
        k.dma(k.sp, lambda: nc.sync.dma_start(out=o.t, in_=X1.t[SEQ - 1:SEQ, :]), reads=[X1], writes=[o], acc=True)
        o2 = P.outs["shift_s"]
        k.dma(k.sp, lambda: nc.sync.dma_start(out=o2.t, in_=X1.t[SEQ + 3:SEQ + NS:4, :]), reads=[X1], writes=[o2], acc=True)
        if "l1" in phases:
            phase_rwkv(P)
        P.k.finish()
    return P


_CACHE = {}


def kernel(**inputs):
    x_prompt = np.ascontiguousarray(inputs["x_prompt"], dtype=np.float32)
    n_cores = 8
    if "P" not in _CACHE:
        _CACHE["P"] = build()
    P = _CACHE["P"]
    consts = host_consts()
    shared = {}
    for n in ["w_in_att", "w_out_att", "rel_bias", "peer_w_q", "peer_sub_keys", "peer_u", "peer_v",
              "rw_mu", "rw_w_rkv", "rw_w1", "rw_w2", "rw_a1", "rw_a2", "rw_g1", "rw_g2", "rw_r_k", "rw_w_o"]:
        shared[n] = np.ascontiguousarray(inputs[n], dtype=np.float32)
    for n in ["rw_w0", "rw_a0", "rw_k_k", "rw_k_a", "rw_gn_g", "rw_gn_b"]:
        shared[n] = np.ascontiguousarray(inputs[n], dtype=np.float32).reshape(1, D)
    shared["att_sinks"] = np.ascontiguousarray(inputs["att_sinks"], dtype=np.float32).reshape(1, 20)
    shared["ln_g"] = np.ascontiguousarray(inputs["ln_g"], dtype=np.float32).reshape(4, D)
    shared["ln_b"] = np.ascontiguousarray(inputs["ln_b"], dtype=np.float32).reshape(4, D)
    shared.update(consts)
    xps = [np.ascontiguousarray(x_prompt[b]) for b in range(4)]
    in_maps = []
    for c in range(n_cores):
        sl = slice(4 * c, 4 * c + 4)
        m = dict(shared)
        m["xp"] = xps[c // 2]
        m["xs"] = np.ascontiguousarray(inputs["x_sample"][sl], dtype=np.float32).reshape(NS, D)
        m["ca_k"] = np.ascontiguousarray(inputs["cache_a_k"][sl], dtype=np.float32).reshape(4, 128, 512)
        m["ca_v"] = np.ascontiguousarray(inputs["cache_a_v"][sl], dtype=np.float32).reshape(4, 128, 512)
        m["cb_k"] = np.ascontiguousarray(inputs["cache_b_k"][sl], dtype=np.float32).reshape(4, 2048, 1536)
        m["cb_v"] = np.ascontiguousarray(inputs["cache_b_v"][sl], dtype=np.float32).reshape(4, 2048, 1536)
        m["st_shift"] = np.ascontiguousarray(inputs["state_shift"][sl], dtype=np.float32)
        m["st_wkv"] = np.ascontiguousarray(inputs["state_wkv"][sl], dtype=np.float32)
        m = {kk: v for kk, v in m.items() if kk in P.ins}
        in_maps.append(m)
    res = run_bass_kernel_spmd(P.nc, in_maps, core_ids=list(range(n_cores)))
    R = res.results
    ev = [R[2 * b] for b in range(4)]
    y_p = np.stack([r["y_p"] for r in ev])
    y_s = np.concatenate([r["y_s"].reshape(4, 4, D) for r in R])
    a_k_p = np.stack([r["a_k_p"].reshape(128, 4, 128) for r in ev])
    a_v_p = np.stack([r["a_v_p"].reshape(128, 4, 128) for r in ev])
    b_k_p = np.stack([r["b_k_p"].reshape(SEQ, 12, 128) for r in ev])
    b_v_p = np.stack([r["b_v_p"].reshape(SEQ, 12, 128) for r in ev])
    shift_p = np.concatenate([r["shift_p"] for r in ev])
    wkv_p = np.stack([r["wkv_p"] for r in ev])
    a_k_s = np.concatenate([r["a_k_s"].reshape(4, 128, 4, 128) for r in R])
    a_v_s = np.concatenate([r["a_v_s"].reshape(4, 128, 4, 128) for r in R])
    b_k_s = np.concatenate([r["b_k_s"].reshape(4, 2048, 12, 128) for r in R])
    b_v_s = np.concatenate([r["b_v_s"].reshape(4, 2048, 12, 128) for r in R])
    shift_s = np.concatenate([r["shift_s"] for r in R])
    wkv_s = np.concatenate([r["wkv_s"] for r in R])
    return (y_p, y_s, a_k_p, a_v_p, b_k_p, b_v_p, shift_p, wkv_p, a_k_s, a_v_s, b_k_s, b_v_s, shift_s, wkv_s)
```

```python
import numpy as np
from contextlib import ExitStack
import concourse.bass as bass
import concourse.mybir as mybir
from concourse.bass_utils import run_bass_kernel_spmd

F32 = mybir.dt.float32
BF16 = mybir.dt.bfloat16
I32 = mybir.dt.int32
U32 = mybir.dt.uint32
ALU = mybir.AluOpType
AF = mybir.ActivationFunctionType
AX = mybir.AxisListType

D = 4096
KC = 32
SEQ = 2048
NS = 16
HD = 128
SCALE = HD ** -0.5
ALPHA = 4.0 ** 0.25
LN_EPS = 1e-5


class Buf:
    __slots__ = ("name", "w", "r", "t")

    def __init__(self, name, t=None):
        self.name = name
        self.w = {}
        self.r = {}
        self.t = t

    def __getitem__(self, key):
        return self.t[key]


class Eng:
    def __init__(self, k, name, e, is_pe=False):
        self.k = k
        self.name = name
        self.e = e
        self.is_pe = is_pe
        self.sem = None
        self.cnt = 0
        self.waited = {}
        self.dma_sems = []
        self.dma_vals = []
        self.dma_i = 0

    def new_sem(self):
        self.sem = self.k.alloc_sem(self.name)
        self.cnt = 0

    def wait_tok(self, tok):
        if tok is None:
            return
        sem, val = tok
        if self.is_pe and sem is self.sem:
            return
        key = sem.name
        if self.waited.get(key, 0) >= val:
            return
        self.e.wait_ge(sem, val)
        self.waited[key] = val
        self.k.n_wait += 1


class KB:
    def __init__(self):
        self.nc = bass.Bass("TRN2", target_bir_lowering=False)
        nc = self.nc
        self._sem_i = 0
        self.n_wait = 0
        self.n_ins = 0
        self.pe = Eng(self, "pe", nc.tensor, is_pe=True)
        self.dve = Eng(self, "dve", nc.vector)
        self.act = Eng(self, "act", nc.scalar)
        self.pool = Eng(self, "pool", nc.gpsimd)
        self.sp = Eng(self, "sp", nc.sync)
        self.engs = [self.pe, self.dve, self.act, self.pool, self.sp]
        for e in self.engs:
            e.new_sem()
        for e in (self.sp, self.pool):
            for i in range(16):
                e.dma_sems.append(self.alloc_sem(e.name + "_dma%d" % i))
                e.dma_vals.append(0)

    def alloc_sem(self, name):
        self._sem_i += 1
        cm = self.nc.semaphore("%s_s%d" % (name, self._sem_i))
        return cm.__enter__()

    def _deps(self, eng, reads, writes, acc=False):
        toks = []
        for b in reads:
            toks.extend(b.w.values())
        if not acc:
            for b in writes:
                toks.extend(b.w.values())
                toks.extend(b.r.values())
        for t in toks:
            eng.wait_tok(t)

    def _commit(self, tok, reads, writes, acc=False):
        for b in reads:
            old = b.r.get(tok[0].name)
            if old is None or old[1] < tok[1]:
                b.r[tok[0].name] = tok
        for b in writes:
            if acc:
                old = b.w.get(tok[0].name)
                if old is None or old[1] < tok[1]:
                    b.w[tok[0].name] = tok
            else:
                b.w = {tok[0].name: tok}
                b.r = {}

    def op(self, eng, fn, reads=(), writes=()):
        self._deps(eng, reads, writes)
        if eng.cnt >= 30000:
            eng.new_sem()
        ins = fn()
        eng.cnt += 1
        ins.then_inc(eng.sem, 1)
        tok = (eng.sem, eng.cnt)
        self._commit(tok, reads, writes)
        self.n_ins += 1
        return tok

    def dma(self, eng, fn, reads=(), writes=(), acc=False):
        self._deps(eng, reads, writes, acc)
        i = eng.dma_i % len(eng.dma_sems)
        eng.dma_i += 1
        sem = eng.dma_sems[i]
        prev = eng.dma_vals[i]
        if prev > 0:
            eng.wait_tok((sem, prev))
        ins = fn()
        ins.then_inc(sem, 16)
        eng.dma_vals[i] = prev + 16
        tok = (sem, prev + 16)
        self._commit(tok, reads, writes, acc)
        self.n_ins += 1
        return tok

    def barrier(self):
        toks = []
        for eng in (self.sp, self.pool):
            for sem, v in zip(eng.dma_sems, eng.dma_vals):
                if v > 0:
                    toks.append((sem, v))
        for eng in self.engs:
            if eng.cnt > 0:
                toks.append((eng.sem, eng.cnt))
        for e in self.engs:
            for t in toks:
                if t[0] is e.sem:
                    continue
                e.wait_tok(t)

    def finish(self):
        e = self.sp
        for eng in (self.sp, self.pool):
            for sem, v in zip(eng.dma_sems, eng.dma_vals):
                if v > 0:
                    e.wait_tok((sem, v))
        for eng in self.engs:
            if eng is not e and eng.cnt > 0:
                e.wait_tok((eng.sem, eng.cnt))


class Ring:
    def __init__(self, bufs):
        self.bufs = bufs
        self.i = 0

    def next(self):
        b = self.bufs[self.i % len(self.bufs)]
        self.i += 1
        return b


class PhaseStack(ExitStack):
    kb = None

    def __exit__(self, *a):
        if a[0] is None and PhaseStack.kb is not None:
            PhaseStack.kb.barrier()
        return super().__exit__(*a)


class Prog:
    def __init__(self, debug=False):
        self.k = KB()
        PhaseStack.kb = self.k
        self.nc = self.k.nc
        self.debug = debug
        self.ins = {}
        self.outs = {}
        self.scr = {}
        self._evac_i = 0
        self.bg = []

    def inp(self, name, shape, dt=F32):
        t = self.nc.dram_tensor(name, list(shape), dt, kind="ExternalInput")
        self.ins[name] = Buf(name, t.ap())
        return self.ins[name]

    def out(self, name, shape, dt=F32):
        t = self.nc.dram_tensor(name, list(shape), dt, kind="ExternalOutput")
        self.outs[name] = Buf(name, t.ap())
        return self.outs[name]

    def scratch(self, name, shape, dt=F32, dbg=False):
        kind = "ExternalOutput" if (dbg and self.debug) else "Internal"
        t = self.nc.dram_tensor(name, list(shape), dt, kind=kind)
        self.scr[name] = Buf(name, t.ap())
        return self.scr[name]

    def sb(self, es, name, shape, dt=F32):
        self._sb_i = getattr(self, "_sb_i", 0) + 1
        nm = "sb%d_%s" % (self._sb_i, name)
        return Buf(nm, es.enter_context(self.nc.sbuf_tensor(nm, list(shape), dt)))

    def sbring(self, es, name, shape, dt, n):
        return Ring([self.sb(es, "%s%d" % (name, i), shape, dt) for i in range(n)])

    def mm(self, out_b, out_ap, lhsT_b, lhsT_ap, rhs_b, rhs_ap, start, stop):
        nc = self.nc
        return self.k.op(self.k.pe, lambda: nc.tensor.matmul(out_ap, lhsT=lhsT_ap, rhs=rhs_ap, start=start, stop=stop),
                         reads=[lhsT_b, rhs_b], writes=[out_b])

    def tr(self, out_b, out_ap, in_b, in_ap, ident_b, ident_ap):
        nc = self.nc
        return self.k.op(self.k.pe, lambda: nc.tensor.transpose(out=out_ap, in_=in_ap, identity=ident_ap),
                         reads=[in_b, ident_b], writes=[out_b])

    def ld(self, out_b, out_ap, in_b, in_ap, eng=None):
        eng = eng or self.k.sp
        return self.k.dma(eng, lambda: eng.e.dma_start(out=out_ap, in_=in_ap), reads=[in_b], writes=[out_b])

    def st(self, out_b, out_ap, in_b, in_ap, eng=None):
        eng = eng or self.k.sp
        return self.k.dma(eng, lambda: eng.e.dma_start(out=out_ap, in_=in_ap), reads=[in_b], writes=[out_b], acc=True)

    def evac(self, out_b, out_ap, in_b, in_ap, scale=None, extra_reads=()):
        nc = self.nc
        self._evac_i += 1
        if self._evac_i % 2 == 0:
            if scale is None:
                f = lambda: nc.scalar.copy(out=out_ap, in_=in_ap)
            else:
                f = lambda: nc.scalar.mul(out=out_ap, in_=in_ap, mul=scale)
            return self.k.op(self.k.act, f, reads=[in_b, *extra_reads], writes=[out_b])
        else:
            if scale is None:
                f = lambda: nc.vector.tensor_copy(out=out_ap, in_=in_ap)
            else:
                f = lambda: nc.vector.tensor_scalar_mul(out=out_ap, in0=in_ap, scalar1=scale)
            return self.k.op(self.k.dve, f, reads=[in_b, *extra_reads], writes=[out_b])


def phase_l0a(P):
    nc, k = P.nc, P.k
    xp, xs, w_in, ident = P.ins["xp"], P.ins["xs"], P.ins["w_in_att"], P.ins["ident"]
    QK, VA, PS = P.scr["QK"], P.scr["VA"], P.scr["PS"]
    wv = w_in.t.rearrange("(kc p) n -> p kc n", p=128)
    with PhaseStack() as es:
        NT = 1024 + NS
        xT = P.sb(es, "xT", [128, KC, NT], BF16)
        xring = P.sbring(es, "xtile", [128, D], F32, 2)
        wring = P.sbring(es, "wb", [128, KC, 256], BF16, 3)
        stA = P.sbring(es, "stA", [128, 512], BF16, 3)
        stB = P.sbring(es, "stB", [128, 256], F32, 3)
        idt = P.sb(es, "idt", [128, 128], F32)
        P.ld(idt, idt[:], ident, ident.t)
        pbanks = Ring(P.psum)

        def load_xT(src_b, src_ap, ntok, col0):
            xt = xring.next()
            P.ld(xt, xt[0:ntok, :], src_b, src_ap)
            for g in range(8):
                pb = pbanks.next()
                for j in range(4):
                    kc = g * 4 + j
                    P.tr(pb, pb[:, j * 128:j * 128 + ntok], xt, xt[0:ntok, kc * 128:(kc + 1) * 128], idt, idt[0:ntok, 0:ntok])
                P.evac(xT, xT[:, g * 4:(g + 1) * 4, col0:col0 + ntok], pb,
                       pb[:, 0:512].rearrange("p (j t) -> p j t", j=4)[:, :, 0:ntok])

        for ps_i in range(2):
            tok0 = ps_i * 1024
            for tt in range(8):
                load_xT(xp, xp.t[tok0 + tt * 128: tok0 + (tt + 1) * 128, :], 128, tt * 128)
            has_s = (ps_i == 1)
            if has_s:
                load_xT(xs, xs.t[:, :], NS, 1024)
            for j in range(32):
                wb = wring.next()
                P.ld(wb, wb[:], w_in, wv[:, :, j * 256:(j + 1) * 256], eng=k.pool)
                drain_bg(P)
                if j < 10:
                    kind, chunk0 = "q", 2 * j
                elif j < 12:
                    kind, chunk0 = "kA", 20 + 2 * (j - 10)
                elif j < 14:
                    kind, chunk0 = "vA", None
                elif j < 20:
                    kind, chunk0 = "q", 24 + 2 * (j - 14)
                elif j < 26:
                    kind, chunk0 = "kB", 36 + 2 * (j - 20)
                else:
                    kind, chunk0 = "vB", None
                if kind in ("q", "kA", "kB"):
                    for h in range(2):
                        for tc in range(2):
                            pb = pbanks.next()
                            for kc in range(KC):
                                P.mm(pb, pb[:, 0:512], wb, wb[:, kc, h * 128:(h + 1) * 128],
                                     xT, xT[:, kc, tc * 512:(tc + 1) * 512], kc == 0, kc == KC - 1)
                            s = stA.next()
                            P.evac(s, s[:], pb, pb[:, 0:512], scale=(SCALE if kind == "q" else None))
                            P.st(QK, QK.t[chunk0 + h, :, tok0 + tc * 512: tok0 + (tc + 1) * 512], s, s[:])
                if kind in ("kA", "vA", "kB", "vB"):
                    for tt in range(8):
                        gt = tok0 + tt * 128
                        if kind == "kA" and gt != SEQ - 128:
                            continue
                        pb = pbanks.next()
                        for kc in range(KC):
                            P.mm(pb, pb[:, 0:256], xT, xT[:, kc, tt * 128:(tt + 1) * 128], wb, wb[:, kc, :], kc == 0, kc == KC - 1)
                        s = stB.next()
                        P.evac(s, s[:], pb, pb[:, 0:256])
                        if kind == "kA":
                            o = P.outs["a_k_p"]
                            P.st(o, o.t[:, (j - 10) * 256:(j - 9) * 256], s, s[:])
                        elif kind == "vA":
                            P.st(VA, VA.t[gt:gt + 128, (j - 12) * 256:(j - 11) * 256], s, s[:])
                            if gt == SEQ - 128:
                                o = P.outs["a_v_p"]
                                P.st(o, o.t[:, (j - 12) * 256:(j - 11) * 256], s, s[:])
                        elif kind == "kB":
                            o = P.outs["b_k_p"]
                            P.st(o, o.t[gt:gt + 128, (j - 20) * 256:(j - 19) * 256], s, s[:])
                        else:
                            o = P.outs["b_v_p"]
                            P.st(o, o.t[gt:gt + 128, (j - 26) * 256:(j - 25) * 256], s, s[:])
                if has_s:
                    pb = pbanks.next()
                    for kc in range(KC):
                        P.mm(pb, pb[0:NS, 0:256], xT, xT[:, kc, 1024:1024 + NS], wb, wb[:, kc, :], kc == 0, kc == KC - 1)
                    s = stB.next()
                    P.evac(s, s[0:NS, :], pb, pb[0:NS, 0:256])
                    P.st(PS, PS.t[:, j * 256:(j + 1) * 256], s, s[0:NS, :])


DILS = (1, 4, 16)


def phase_bias_setup(P):
    nc, k = P.nc, P.k
    GGD = P.scr["GGD"]
    with PhaseStack() as es:
        tab = P.sb(es, "tab", [32, 32], F32)
        P.ld(tab, tab[:], P.ins["rel_bias"], P.ins["rel_bias"].t)
        sel = P.sb(es, "sel", [32, 3, 384], F32)
        P.ld(sel, sel[:], P.ins["sel"], P.ins["sel"].t)
        msk = P.sb(es, "msk", [32, 384], F32)
        P.ld(msk, msk[:], P.ins["selmask"], P.ins["selmask"].t)
        gg = P.sb(es, "gg", [32, 3, 384], F32)
        for di in range(3):
            pb = P.psum[di]
            P.mm(pb, pb[0:32, 0:384], tab, tab[:, :], sel, sel[:, di, :], True, True)
            k.op(k.act, lambda: nc.scalar.activation(out=gg[:, di, :], in_=pb[0:32, 0:384], func=AF.Exp), reads=[pb], writes=[gg])
            k.op(k.dve, lambda: nc.vector.tensor_tensor(out=gg[:, di, :], in0=gg[:, di, :], in1=msk[:, :], op=ALU.mult), reads=[gg, msk], writes=[gg])
        P.st(GGD, GGD.t.rearrange("d h j -> h d j"), gg, gg[:])


def phase_l0b(P):
    nc, k = P.nc, P.k
    QK, VA, OT, GGD = P.scr["QK"], P.scr["VA"], P.scr["OT"], P.scr["GGD"]
    bvp = P.outs["b_v_p"]
    with PhaseStack() as es:
        qT = P.sbring(es, "qT", [128, SEQ], BF16, 2)
        kT = P.sbring(es, "kT", [128, SEQ], BF16, 2)
        Vt = P.sbring(es, "Vt", [128, 16, 128], BF16, 2)
        Et = P.sbring(es, "Et", [128, 2, 128], F32, 2)
        p1r = P.sbring(es, "p1", [128, 128], F32, 3)
        ptr = P.sbring(es, "pt", [128, 128], BF16, 3)
        rzr = P.sbring(es, "rz", [128, 128], F32, 2)
        ost = P.sbring(es, "ost", [128, SEQ], BF16, 2)
        oacc = P.sb(es, "oacc", [128, SEQ], F32)
        zacc = P.sb(es, "zacc", [128, SEQ], F32)
        ones = P.sb(es, "ones", [128, 128], BF16)
        esk = P.sb(es, "esk", [128, 20], F32)
        k.op(k.dve, lambda: nc.vector.memset(ones[:], 1.0), writes=[ones])
        sinks = P.ins["att_sinks"]
        P.ld(esk, esk[:], sinks, sinks.t.partition_broadcast(128))
        k.op(k.act, lambda: nc.scalar.activation(out=esk[:], in_=esk[:], func=AF.Exp), reads=[esk], writes=[esk])
        pbanks = Ring(P.psum)

        def hankel(eb, slot, di, h, base):
            src = bass.AP(tensor=GGD.t.tensor, offset=(di * 32 + h) * 384 + base, ap=[[1, 128], [1, 128]])
            P.ld(eb, eb[:, slot, :], GGD, src)

        def attend(qb, q_ap, pairs, eb, ob, zb):
            np_ = len(pairs)
            for i, (kb_, k_ap, vb_, v_ap, slot) in enumerate(pairs):
                sb_ = pbanks.next()
                P.mm(sb_, sb_[:, 0:128], kb_, k_ap, qb, q_ap, True, True)
                p1 = p1r.next()
                k.op(k.act, lambda: nc.scalar.activation(out=p1[:], in_=sb_[:, 0:128], func=AF.Exp), reads=[sb_], writes=[p1])
                pt = ptr.next()
                k.op(k.dve, lambda: nc.vector.tensor_tensor(out=pt[:], in0=p1[:], in1=eb[:, slot, ::-1], op=ALU.mult),
                     reads=[p1, eb], writes=[pt])
                P.mm(ob, ob[:, 0:128], vb_, v_ap, pt, pt[:], i == 0, i == np_ - 1)
                P.mm(zb, zb[:, 0:128], ones, ones[:], pt, pt[:], i == 0, i == np_ - 1)

        cur_g = -1
        for h in range(20):
            g = h // 5
            q = qT.next()
            P.ld(q, q[:], QK, QK.t[h])
            if g != cur_g:
                kk = kT.next()
                P.ld(kk, kk[:], QK, QK.t[20 + g])
                vv = Vt.next()
                P.ld(vv, vv[:], VA, VA.t[:, g * 128:(g + 1) * 128].rearrange("(n p) d -> p n d", p=128), eng=k.pool)
                cur_g = g
            eb = Et.next()
            hankel(eb, 0, 0, h, 128)
            hankel(eb, 1, 0, h, 0)
            o_s = ost.next()
            for n in range(16):
                ob, zb = pbanks.next(), pbanks.next()
                pairs = []
                if n > 0:
                    pairs.append((kk, kk[:, (n - 1) * 128:n * 128], vv, vv[:, n - 1, :], 1))
                pairs.append((kk, kk[:, n * 128:(n + 1) * 128], vv, vv[:, n, :], 0))
                attend(q, q[:, n * 128:(n + 1) * 128], pairs, eb, ob, zb)
                rz = rzr.next()
                k.op(k.dve, lambda: nc.vector.tensor_scalar(out=rz[:], in0=zb[:, 0:128], scalar1=esk[:, h:h + 1], scalar2=None, op0=ALU.add),
                     reads=[zb, esk], writes=[rz])
                k.op(k.dve, lambda: nc.vector.reciprocal(out=rz[:], in_=rz[:]), reads=[rz], writes=[rz])
                k.op(k.dve, lambda: nc.vector.tensor_tensor(out=o_s[:, n * 128:(n + 1) * 128], in0=ob[:, 0:128], in1=rz[:], op=ALU.mult),
                     reads=[ob, rz], writes=[o_s])
            P.st(OT, OT.t[h, :, 0:SEQ], o_s, o_s[:])

        for hb in range(12):
            h = 20 + hb
            q = qT.next()
            P.ld(q, q[:], QK, QK.t[24 + hb])
            kk = kT.next()
            P.ld(kk, kk[:], QK, QK.t[36 + hb])
            for di, d in enumerate(DILS):
                nb = SEQ // (128 * d)
                vv = Vt.next()
                P.ld(vv, vv[:].rearrange("p (r n) c -> p r n c", r=d),
                     bvp, bvp.t[:, hb * 128:(hb + 1) * 128].rearrange("(n i r) c -> i r n c", i=128, r=d), eng=k.pool)
                eb = Et.next()
                hankel(eb, 0, di, h, 128)
                hankel(eb, 1, di, h, 0)
                qv = q[:].rearrange("p (n i r) -> p r n i", i=128, r=d)
                kv = kk[:].rearrange("p (n i r) -> p r n i", i=128, r=d)
                ov = oacc[:].rearrange("p (n i r) -> p r n i", i=128, r=d)
                zv = zacc[:].rearrange("p (n i r) -> p r n i", i=128, r=d)
                for r in range(d):
                    for n in range(nb):
                        ob, zb = pbanks.next(), pbanks.next()
                        pairs = []
                        if n > 0:
                            pairs.append((kk, kv[:, r, n - 1, :], vv, vv[:, r * nb + n - 1, :], 1))
                        pairs.append((kk, kv[:, r, n, :], vv, vv[:, r * nb + n, :], 0))
                        attend(q, qv[:, r, n, :], pairs, eb, ob, zb)
                        if di == 0:
                            k.op(k.act, lambda: nc.scalar.copy(out=ov[:, r, n, :], in_=ob[:, 0:128]), reads=[ob], writes=[oacc])
                            k.op(k.dve, lambda: nc.vector.tensor_copy(out=zv[:, r, n, :], in_=zb[:, 0:128]), reads=[zb], writes=[zacc])
                        else:
                            k.op(k.dve, lambda: nc.vector.tensor_tensor(out=ov[:, r, n, :], in0=ov[:, r, n, :], in1=ob[:, 0:128], op=ALU.add),
                                 reads=[ob, oacc], writes=[oacc])
                            k.op(k.dve, lambda: nc.vector.tensor_tensor(out=zv[:, r, n, :], in0=zv[:, r, n, :], in1=zb[:, 0:128], op=ALU.add),
                                 reads=[zb, zacc], writes=[zacc])
            k.op(k.dve, lambda: nc.vector.reciprocal(out=zacc[:], in_=zacc[:]), reads=[zacc], writes=[zacc])
            o_s = ost.next()
            k.op(k.dve, lambda: nc.vector.tensor_tensor(out=o_s[:], in0=oacc[:], in1=zacc[:], op=ALU.mult), reads=[oacc, zacc], writes=[o_s])
            P.st(OT, OT.t[h, :, 0:SEQ], o_s, o_s[:])


T_ALL = SEQ + NS
TOK_TILES = [(i * 128, 128) for i in range(16)] + [(SEQ, NS)]
TOK_CHUNKS = [(i * 512, 512) for i in range(4)] + [(SEQ, NS)]


def load_actT(P, es, src, name="actT"):
    a = P.sb(es, name, [128, KC, T_ALL], BF16)
    for c0 in range(0, KC, 8):
        P.ld(a, a[:, c0:c0 + 8, :], src, src.t[c0:c0 + 8].rearrange("c p t -> p c t"))
    return a


def proj_pass(P, actT, W_ap, ncols, form, emit, kc_n=KC, blk=256, chunks=None, tiles=None):
    k = P.k
    wv = W_ap.rearrange("(kc p) n -> p kc n", p=128)
    with PhaseStack() as es:
        wring = P.sbring(es, "wblk", [128, kc_n, blk], BF16, 3)
        pbanks = Ring(P.psum)
        for c0 in range(0, ncols, blk):
            cn = min(blk, ncols - c0)
            wb = wring.next()
            P.ld(wb, wb[:, :, 0:cn], P.wsrc, wv[:, :, c0:c0 + cn], eng=k.pool)
            drain_bg(P)
            if form == "A":
                for h in range((cn + 127) // 128):
                    m_ = min(128, cn - h * 128)
                    for (t0, tn) in (chunks or TOK_CHUNKS):
                        pb = pbanks.next()
                        for kc in range(kc_n):
                            P.mm(pb, pb[0:m_, 0:tn], wb, wb[:, kc, h * 128:h * 128 + m_], actT, actT[:, kc, t0:t0 + tn], kc == 0, kc == kc_n - 1)
                        emit((c0 // 128) + h, t0, tn, pb)
            else:
                for (t0, tn) in (tiles or TOK_TILES):
                    pb = pbanks.next()
                    for kc in range(kc_n):
                        P.mm(pb, pb[0:tn, 0:cn], actT, actT[:, kc, t0:t0 + tn], wb, wb[:, kc, 0:cn], kc == 0, kc == kc_n - 1)
                    emit(c0, cn, t0, tn, pb)


def ln_rows(P, es_bufs, h, nt, gbc, bbc):
    nc, k = P.nc, P.k
    junk, st = es_bufs
    k.op(k.act, lambda: nc.scalar.activation(out=junk[0:nt, :], in_=h[0:nt, :], func=AF.Copy, accum_out=st[0:nt, 0:1]), reads=[h], writes=[junk, st])
    k.op(k.act, lambda: nc.scalar.activation(out=junk[0:nt, :], in_=h[0:nt, :], func=AF.Square, accum_out=st[0:nt, 1:2]), reads=[h], writes=[junk, st])
    k.op(k.dve, lambda: nc.vector.tensor_scalar(out=st[0:nt, 0:2], in0=st[0:nt, 0:2], scalar1=1.0 / D, scalar2=None, op0=ALU.mult), reads=[st], writes=[st])
    k.op(k.dve, lambda: nc.vector.tensor_tensor(out=st[0:nt, 2:3], in0=st[0:nt, 0:1], in1=st[0:nt, 0:1], op=ALU.mult), reads=[st], writes=[st])
    k.op(k.dve, lambda: nc.vector.tensor_tensor(out=st[0:nt, 3:4], in0=st[0:nt, 1:2], in1=st[0:nt, 2:3], op=ALU.subtract), reads=[st], writes=[st])
    k.op(k.dve, lambda: nc.vector.tensor_scalar(out=st[0:nt, 3:4], in0=st[0:nt, 3:4], scalar1=LN_EPS, scalar2=None, op0=ALU.add), reads=[st], writes=[st])
    k.op(k.act, lambda: nc.scalar.activation(out=st[0:nt, 3:4], in_=st[0:nt, 3:4], func=AF.Sqrt), reads=[st], writes=[st])
    k.op(k.dve, lambda: nc.vector.reciprocal(out=st[0:nt, 3:4], in_=st[0:nt, 3:4]), reads=[st], writes=[st])
    k.op(k.dve, lambda: nc.vector.tensor_scalar(out=h[0:nt, :], in0=h[0:nt, :], scalar1=st[0:nt, 0:1], scalar2=st[0:nt, 3:4], op0=ALU.subtract, op1=ALU.mult), reads=[h, st], writes=[h])
    k.op(k.dve, lambda: nc.vector.tensor_tensor(out=h[0:nt, :], in0=h[0:nt, :], in1=gbc[0:nt, :], op=ALU.mult), reads=[h, gbc], writes=[h])
    k.op(k.pool, lambda: nc.gpsimd.tensor_tensor(out=h[0:nt, :], in0=h[0:nt, :], in1=bbc[0:nt, :], op=ALU.add), reads=[h, bbc], writes=[h])


def to_featT(P, pbanks, h, nt, idt, dstT_b, dst_scr, t0, stg_ring):
    stg = stg_ring.next()
    for g in range(8):
        pb = pbanks.next()
        for j in range(4):
            kc = g * 4 + j
            P.tr(pb, pb[:, j * 128:j * 128 + nt], h, h[0:nt, kc * 128:(kc + 1) * 128], idt, idt[0:nt, 0:nt])
        P.evac(stg, stg[:, g * 4:(g + 1) * 4, 0:nt], pb, pb[:, 0:512].rearrange("p (j t) -> p j t", j=4)[:, :, 0:nt])
    P.st(dst_scr, dst_scr.t[:, :, t0:t0 + nt].rearrange("c p t -> p c t"), stg, stg[:, :, 0:nt])


def phase_outproj(P, srcT, W_b, dst):
    with PhaseStack() as es:
        actT = load_actT(P, es, srcT)
        stg = P.sbring(es, "stgB", [128, 256], F32, 4)
        P.wsrc = W_b

        def emit(c0, cn, t0, tn, pb):
            s_ = stg.next()
            P.evac(s_, s_[0:tn, 0:cn], pb, pb[0:tn, 0:cn])
            P.st(dst, dst.t[t0:t0 + tn, c0:c0 + cn], s_, s_[0:tn, 0:cn])
        proj_pass(P, actT, W_b.t, D, "B", emit)


def phase_res_ln(P, xres_fn, mix, ln_row, dst_tok, dst_T, dst_bf=None, alpha=ALPHA):
    nc, k = P.nc, P.k
    with PhaseStack() as es:
        gbc = P.sb(es, "gbc", [128, D], F32); bbc = P.sb(es, "bbc", [128, D], F32)
        lg, lb = P.ins["ln_g"], P.ins["ln_b"]
        P.ld(gbc, gbc[:], lg, lg.t[ln_row:ln_row + 1, :].partition_broadcast(128))
        P.ld(bbc, bbc[:], lb, lb.t[ln_row:ln_row + 1, :].partition_broadcast(128))
        idt = P.sb(es, "idt", [128, 128], F32)
        P.ld(idt, idt[:], P.ins["ident"], P.ins["ident"].t)
        xr = P.sbring(es, "xr", [128, D], F32, 2)
        mr = P.sbring(es, "mr", [128, D], F32, 2)
        junk = P.sb(es, "junk", [128, D], F32); st = P.sb(es, "lnst", [128, 4], F32)
        stgT = P.sbring(es, "stgT", [128, KC, 128], BF16, 2)
        mbf = P.sbring(es, "mbf", [128, D], BF16, 2)
        pbanks = Ring(P.psum)
        for (t0, tn) in TOK_TILES:
            x = xr.next(); m = mr.next()
            xb_, xap = xres_fn(t0, tn)
            P.ld(x, x[0:tn, :], xb_, xap)
            P.ld(m, m[0:tn, :], mix, mix.t[t0:t0 + tn, :])
            k.op(k.dve, lambda: nc.vector.scalar_tensor_tensor(out=m[0:tn, :], in0=x[0:tn, :], scalar=alpha, in1=m[0:tn, :], op0=ALU.mult, op1=ALU.add),
                 reads=[x, m], writes=[m])
            ln_rows(P, (junk, st), m, tn, gbc, bbc)
            P.st(dst_tok, dst_tok.t[t0:t0 + tn, :], m, m[0:tn, :])
            if dst_bf is not None:
                mb = mbf.next()
                k.op(k.act, lambda: nc.scalar.copy(out=mb[0:tn, :], in_=m[0:tn, :]), reads=[m], writes=[mb])
                P.st(dst_bf, dst_bf.t[t0:t0 + tn, :], mb, mb[0:tn, :])
            if dst_T is not None:
                to_featT(P, pbanks, m, tn, idt, None, dst_T, t0, stgT)


def phase_tables_bf16(P, layer):
    k = P.k
    for nm, dst in (("peer_u", P.scr["UB%d" % layer]), ("peer_v", P.scr["VB%d" % layer])):
        src = P.ins[nm]
        for r0 in range(0, 16384, 512):
            def job(src=src, dst=dst, r0=r0):
                k.dma(k.pool, lambda: P.nc.gpsimd.dma_start(out=dst.t[r0:r0 + 512, :], in_=src.t[layer, r0:r0 + 512, :]),
                      reads=[src], writes=[dst], acc=True)
            P.bg.append(job)


def drain_bg(P, n=1):
    for _ in range(n):
        if P.bg:
            P.bg.pop(0)()


def phase_peer_q(P, layer, XT, QPT):
    with PhaseStack() as es:
        actT = load_actT(P, es, XT)
        stg = P.sbring(es, "stgA", [128, 512], BF16, 4)
        P.wsrc = P.ins["peer_w_q"]

        def emit(ci, t0, tn, pb):
            s_ = stg.next()
            P.evac(s_, s_[:, 0:tn], pb, pb[:, 0:tn])
            P.st(QPT, QPT.t[ci, :, t0:t0 + tn], s_, s_[:, 0:tn])
        proj_pass(P, actT, P.ins["peer_w_q"].t[layer], 2048, "A", emit)


def phase_peer_main(P, layer, XM, XMB, QPT, ln_row, dst_tok, dst_T):
    nc, k = P.nc, P.k
    UB, VB = P.scr["UB%d" % layer], P.scr["VB%d" % layer]
    drain_bg(P, len(P.bg))
    with PhaseStack() as es:
        gbc = P.sb(es, "gbc", [128, D], F32); bbc = P.sb(es, "bbc", [128, D], F32)
        lg, lb = P.ins["ln_g"], P.ins["ln_b"]
        P.ld(gbc, gbc[:], lg, lg.t[ln_row:ln_row + 1, :].partition_broadcast(128))
        P.ld(bbc, bbc[:], lb, lb.t[ln_row:ln_row + 1, :].partition_broadcast(128))
        idt = P.sb(es, "idt", [128, 128], F32)
        P.ld(idt, idt[:], P.ins["ident"], P.ins["ident"].t)
        io16 = P.sb(es, "io16", [128, 16], F32)
        P.ld(io16, io16[:], P.ins["iota16"], P.ins["iota16"].t)
        ohw = P.sb(es, "ohw", [128, 255], F32)
        P.ld(ohw, ohw[:], P.ins["ohw"], P.ins["ohw"].t)
        skT = P.sb(es, "skT", [128, 16, 128], BF16)
        skn = P.sb(es, "skn", [128, 16, 128], F32)
        sk = P.ins["peer_sub_keys"]
        P.ld(skn, skn[:], sk, sk.t[layer].rearrange("h c n d -> n (h c) d"))
        pbanks = Ring(P.psum)
        for g in range(4):
            pb = pbanks.next()
            for j in range(4):
                P.tr(pb, pb[:, j * 128:(j + 1) * 128], skn, skn[:, g * 4 + j, :], idt, idt[:])
            P.evac(skT, skT[:, g * 4:(g + 1) * 4, :], pb, pb[:, 0:512].rearrange("p (j t) -> p j t", j=4))
        qTr = P.sbring(es, "qpT", [128, 16, 128], BF16, 2)
        sc = P.sb(es, "sc", [128, 16, 128], F32)
        wk = P.sb(es, "wk", [128, 256], F32)
        sv = P.sb(es, "sv", [128, 16, 16], F32)
        si = P.sb(es, "si", [128, 16, 16], U32)
        sif = P.sb(es, "sif", [128, 16, 16], F32)
        cand = P.sb(es, "cand", [128, 8, 256], F32)
        big = P.sb(es, "big", [128, 8, 256], F32)
        cv = P.sb(es, "cv", [128, 8, 16], F32)
        ci = P.sb(es, "ci", [128, 8, 16], U32)
        cii = P.sb(es, "cii", [128, 8, 16], U32)
        cif = P.sb(es, "cif", [128, 2, 8, 16], F32)
        i12 = P.sb(es, "i12", [128, 2, 8, 16], F32)
        idxf = P.sb(es, "idxf", [128, 128], F32)
        idxT = P.sb(es, "idxT", [128, 128], I32)
        gate = P.sb(es, "gate", [128, 8, 16], F32)
        gsum = P.sb(es, "gsum", [128, 8], F32)
        hidT = P.sb(es, "hidT", [128, 128], F32)
        hid = P.sb(es, "hid", [128, 128], F32)
        coefT = P.sb(es, "coefT", [128, 128], F32)
        Ur = P.sbring(es, "Ug", [128, D], BF16, 3)
        Vr = P.sbring(es, "Vg", [128, D], BF16, 3)
        xmb = P.sb(es, "xmb", [128, D], BF16)
        onesb = P.sb(es, "onesb", [128, 128], BF16)
        k.op(k.dve, lambda: nc.vector.memset(onesb[:], 1.0), writes=[onesb])
        selr = P.sbring(es, "selr", [128, 128], BF16, 3)
        hid2 = P.sb(es, "hid2", [128, 2, 128], F32)
        Zr = P.sbring(es, "Zt", [128, 128], BF16, 3)
        junkb = P.sb(es, "junkb", [128, D], BF16)
        xm = P.sb(es, "xm", [128, D], F32)
        junk = junkb; st = P.sb(es, "lnst", [128, 4], F32)
        stgT = P.sbring(es, "stgT", [128, KC, 128], BF16, 1)

        def top16(src_b, src_ap, n, val_b, val_ap, idx_b, idx_ap):
            k.op(k.dve, lambda: nc.vector.max(out=val_ap[:, 0:8], in_=src_ap), reads=[src_b], writes=[val_b])
            k.op(k.dve, lambda: nc.vector.max_index(out=idx_ap[:, 0:8], in_max=val_ap[:, 0:8], in_values=src_ap), reads=[src_b, val_b], writes=[idx_b])
            k.op(k.dve, lambda: nc.vector.match_replace(out=wk[0:src_ap.shape[0], 0:n], in_to_replace=val_ap[:, 0:8], in_values=src_ap, imm_value=-1e30),
                 reads=[src_b, val_b], writes=[wk])
            wv_ = wk[0:src_ap.shape[0], 0:n]
            k.op(k.dve, lambda: nc.vector.max(out=val_ap[:, 8:16], in_=wv_), reads=[wk], writes=[val_b])
            k.op(k.dve, lambda: nc.vector.max_index(out=idx_ap[:, 8:16], in_max=val_ap[:, 8:16], in_values=wv_), reads=[wk, val_b], writes=[idx_b])

        for (t0, tn) in TOK_TILES:
            qT = qTr.next()
            P.ld(qT, qT[:, :, 0:tn], QPT, QPT.t[:, :, t0:t0 + tn].rearrange("c p t -> p c t"))
            P.ld(xm, xm[0:tn, :], XM, XM.t[t0:t0 + tn, :])
            for g in range(4):
                pb = pbanks.next()
                for j in range(4):
                    ch = g * 4 + j
                    P.mm(pb, pb[0:tn, j * 128:(j + 1) * 128], qT, qT[:, ch, 0:tn], skT, skT[:, ch, :], True, True)
                P.evac(sc, sc[0:tn, g * 4:(g + 1) * 4, :], pb, pb[0:tn, 0:512].rearrange("p (j n) -> p j n", j=4))
            for ch in range(16):
                top16(sc, sc[0:tn, ch, :], 128, sv, sv[0:tn, ch, :], si, si[0:tn, ch, :])
            k.op(k.dve, lambda: nc.vector.tensor_copy(out=sif[0:tn], in_=si[0:tn]), reads=[si], writes=[sif])
            svv = sv[0:tn].rearrange("p (h c) k -> p h c k", c=2)
            k.op(k.dve, lambda: nc.vector.tensor_tensor(out=cand[0:tn].rearrange("p h (i j) -> p h i j", i=16),
                                                          in0=svv[:, :, 0, :].unsqueeze(3).to_broadcast([tn, 8, 16, 16]),
                                                          in1=svv[:, :, 1, :].unsqueeze(2).to_broadcast([tn, 8, 16, 16]), op=ALU.add),
                 reads=[sv], writes=[cand])
            for h in range(8):
                top16(cand, cand[0:tn, h, :], 256, cv, cv[0:tn, h, :], ci, ci[0:tn, h, :])
            k.op(k.dve, lambda: nc.vector.tensor_single_scalar(out=cii[0:tn], in_=ci[0:tn], scalar=4, op=ALU.logical_shift_right), reads=[ci], writes=[cii])
            k.op(k.dve, lambda: nc.vector.tensor_copy(out=cif[0:tn, 0], in_=cii[0:tn]), reads=[cii], writes=[cif])
            k.op(k.dve, lambda: nc.vector.tensor_single_scalar(out=cii[0:tn], in_=ci[0:tn], scalar=15, op=ALU.bitwise_and), reads=[ci], writes=[cii])
            k.op(k.dve, lambda: nc.vector.tensor_copy(out=cif[0:tn, 1], in_=cii[0:tn]), reads=[cii], writes=[cif])
            sifv = sif[0:tn].rearrange("p (h c) k -> p h c k", c=2)
            for c in range(2):
                bv = big[0:tn].rearrange("p h (k i) -> p h k i", k=16)
                k.op(k.dve, lambda: nc.vector.tensor_tensor(out=bv, in0=cif[0:tn, c].unsqueeze(3).to_broadcast([tn, 8, 16, 16]),
                                                              in1=io16[0:tn].unsqueeze(1).unsqueeze(1).to_broadcast([tn, 8, 16, 16]), op=ALU.is_equal),
                     reads=[cif, io16], writes=[big])
                k.op(k.dve, lambda: nc.vector.tensor_tensor(out=bv, in0=bv, in1=sifv[:, :, c, :].unsqueeze(2).to_broadcast([tn, 8, 16, 16]), op=ALU.mult),
                     reads=[big, sif], writes=[big])
                k.op(k.dve, lambda: nc.vector.tensor_reduce(out=i12[0:tn, c], in_=bv, axis=AX.X, op=ALU.add), reads=[big], writes=[i12])
            k.op(k.dve, lambda: nc.vector.scalar_tensor_tensor(out=idxf[0:tn].rearrange("p (h k) -> p h k", h=8), in0=i12[0:tn, 0], scalar=128.0, in1=i12[0:tn, 1],
                                                                 op0=ALU.mult, op1=ALU.add), reads=[i12], writes=[idxf])
            k.op(k.dve, lambda: nc.vector.tensor_tensor(out=gate[0:tn], in0=cv[0:tn], in1=cv[0:tn, :, 0:1].to_broadcast([tn, 8, 16]), op=ALU.subtract), reads=[cv], writes=[gate])
            k.op(k.act, lambda: nc.scalar.activation(out=gate[0:tn], in_=gate[0:tn], func=AF.Exp), reads=[gate], writes=[gate])
            k.op(k.dve, lambda: nc.vector.tensor_reduce(out=gsum[0:tn], in_=gate[0:tn], axis=AX.X, op=ALU.add), reads=[gate], writes=[gsum])
            k.op(k.dve, lambda: nc.vector.reciprocal(out=gsum[0:tn], in_=gsum[0:tn]), reads=[gsum], writes=[gsum])
            k.op(k.dve, lambda: nc.vector.tensor_tensor(out=gate[0:tn], in0=gate[0:tn], in1=gsum[0:tn].unsqueeze(2).to_broadcast([tn, 8, 16]), op=ALU.mult), reads=[gate, gsum], writes=[gate])
            pb = pbanks.next()
            P.tr(pb, pb[:, 0:tn], idxf, idxf[0:tn, :], idt, idt[0:tn, 0:tn])
            k.op(k.dve, lambda: nc.vector.tensor_copy(out=idxT[:, 0:tn], in_=pb[:, 0:tn]), reads=[pb], writes=[idxT])
            k.op(k.dve, lambda: nc.vector.memset(hid2[:], 0.0), writes=[hid2])
            k.op(k.act, lambda: nc.scalar.copy(out=xmb[0:tn, :], in_=xm[0:tn, :]), reads=[xm], writes=[xmb])
            for t in range(tn):
                ug = Ur.next()
                k.dma(k.pool, lambda: nc.gpsimd.indirect_dma_start(out=ug[:, :], out_offset=None, in_=UB.t,
                                                                     in_offset=bass.IndirectOffsetOnAxis(ap=idxT[:, t:t + 1], axis=0)),
                      reads=[UB, idxT], writes=[ug])
                sel = selr.next()
                k.op(k.act, lambda: nc.scalar.activation(out=sel[0:tn, :], in_=onesb[0:tn, :], func=AF.Copy, scale=idt[0:tn, t:t + 1]),
                     reads=[onesb, idt], writes=[sel])
                for hh in range(2):
                    for j in range(4):
                        pbj = P.psum[hh * 4 + j]
                        P.mm(pbj, pbj[:, 0:512], sel, sel[0:tn, :], xmb, xmb[0:tn, (hh * 4 + j) * 512:(hh * 4 + j + 1) * 512], True, True)
                    pgrp = [P.psum[hh * 4 + j] for j in range(4)]
                    k.op(k.dve, lambda: nc.vector.scalar_tensor_tensor(out=junkb[:, hh * 2048:(hh + 1) * 2048], in0=ug[:, hh * 2048:(hh + 1) * 2048], scalar=1.0,
                                                                         in1=P.psum_all[:, hh * 4:(hh + 1) * 4, :].rearrange("p b c -> p (b c)"),
                                                                         op0=ALU.mult, op1=ALU.mult, accum_out=hid2[:, hh, t:t + 1]),
                         reads=[ug] + pgrp, writes=[junkb, hid2])
            k.op(k.dve, lambda: nc.vector.tensor_tensor(out=hidT[:], in0=hid2[:, 0, :], in1=hid2[:, 1, :], op=ALU.add), reads=[hid2], writes=[hidT])
            pb = pbanks.next()
            P.tr(pb, pb[0:tn, 0:128], hidT, hidT[:, 0:tn], idt, idt[:])
            k.op(k.act, lambda: nc.scalar.activation(out=hid[0:tn], in_=pb[0:tn, 0:128], func=AF.Gelu), reads=[pb], writes=[hid])
            k.op(k.dve, lambda: nc.vector.tensor_tensor(out=hid[0:tn], in0=hid[0:tn], in1=gate[0:tn].rearrange("p h k -> p (h k)"), op=ALU.mult), reads=[hid, gate], writes=[hid])
            pb = pbanks.next()
            P.tr(pb, pb[:, 0:tn], hid, hid[0:tn, :], idt, idt[0:tn, 0:tn])
            k.op(k.dve, lambda: nc.vector.tensor_copy(out=coefT[:, 0:tn], in_=pb[:, 0:tn]), reads=[pb], writes=[coefT])
            for t in range(tn):
                vg = Vr.next(); zt = Zr.next()
                k.dma(k.pool, lambda: nc.gpsimd.indirect_dma_start(out=vg[:, :], out_offset=None, in_=VB.t,
                                                                     in_offset=bass.IndirectOffsetOnAxis(ap=idxT[:, t:t + 1], axis=0)),
                      reads=[VB, idxT], writes=[vg])
                k.op(k.dve, lambda: nc.vector.tensor_scalar(out=zt[:], in0=ohw[:, 127 - t:255 - t], scalar1=coefT[:, t:t + 1], scalar2=None, op0=ALU.mult),
                     reads=[ohw, coefT], writes=[zt])
                for j in range(8):
                    pbj = P.psum[j]
                    P.mm(pbj, pbj[:, 0:512], zt, zt[:], vg, vg[:, j * 512:(j + 1) * 512], t == 0, t == tn - 1)
            for j in range(8):
                pbj = P.psum[j]
                k.op(k.dve, lambda: nc.vector.scalar_tensor_tensor(out=xm[0:tn, j * 512:(j + 1) * 512], in0=xm[0:tn, j * 512:(j + 1) * 512], scalar=ALPHA,
                                                                     in1=pbj[0:tn, 0:512], op0=ALU.mult, op1=ALU.add), reads=[xm, pbj], writes=[xm])
            ln_rows(P, (junk, st), xm, tn, gbc, bbc)
            if callable(dst_tok):
                db, dap = dst_tok(t0, tn)
                P.st(db, dap, xm, xm[0:tn, :])
            else:
                P.st(dst_tok, dst_tok.t[t0:t0 + tn, :], xm, xm[0:tn, :])
            if dst_T is not None:
                to_featT(P, pbanks, xm, tn, idt, None, dst_T, t0, stgT)


def phase_sample_caches(P):
    k, nc = P.k, P.nc
    PS = P.scr["PS"]
    specs = [("ca_k", "a_k_s", 128, 2560, 512), ("ca_v", "a_v_s", 128, 3072, 512),
             ("cb_k", "b_k_s", 2048, 5120, 1536), ("cb_v", "b_v_s", 2048, 6656, 1536)]
    for cin, cout, L, c0, w in specs:
        ci, co = P.ins[cin], P.outs[cout]
        for b in range(4):
            k.dma(k.sp, lambda: nc.sync.dma_start(out=co.t[b, 0:L - 4, :], in_=ci.t[b, 4:L, :]), reads=[ci], writes=[co], acc=True)
            k.dma(k.sp, lambda: nc.sync.dma_start(out=co.t[b, L - 4:L, :], in_=PS.t[4 * b:4 * b + 4, c0:c0 + w]), reads=[PS], writes=[co], acc=True)


def phase_sample_attn(P):
    nc, k = P.nc, P.k
    PS, GGD, OT = P.scr["PS"], P.scr["GGD"], P.scr["OT"]
    with PhaseStack() as es:
        idt = P.sb(es, "idt", [128, 128], F32)
        P.ld(idt, idt[:], P.ins["ident"], P.ins["ident"].t)
        ohs = P.sb(es, "ohs", [128, 31], F32)
        P.ld(ohs, ohs[:], P.ins["ohs"], P.ins["ohs"].t)
        eb = P.sb(es, "ebS", [128, 3, 32], F32)
        for di in range(3):
            src = bass.AP(tensor=GGD.t.tensor, offset=di * 32 * 384 + 127, ap=[[1, 128], [384, 32]])
            k.dma(k.sp, lambda: nc.sync.dma_start(out=eb[:, di, :], in_=src, allow_slow_non_contiguous=True), reads=[GGD], writes=[eb])
        es0 = P.sb(es, "es0", [16, 32], F32)
        src = bass.AP(tensor=GGD.t.tensor, offset=255, ap=[[0, 16], [384, 32]])
        k.dma(k.sp, lambda: nc.sync.dma_start(out=es0[:], in_=src, allow_slow_non_contiguous=True), reads=[GGD], writes=[es0])
        esk = P.sb(es, "eskS", [16, 20], F32)
        sinks = P.ins["att_sinks"]
        P.ld(esk, esk[:], sinks, sinks.t.partition_broadcast(16))
        k.op(k.act, lambda: nc.scalar.activation(out=esk[:], in_=esk[:], func=AF.Exp), reads=[esk], writes=[esk])
        pst = P.sb(es, "pst", [16, 8192], F32)
        P.ld(pst, pst[:], PS, PS.t)
        qbc = P.sbring(es, "qbcS", [128, 2560], F32, 2)
        Kt = P.sbring(es, "KtS", [128, 1536], F32, 2)
        Vt = P.sbring(es, "VtS", [128, 1536], F32, 2)
        prod = P.sb(es, "prodS", [128, 2560], F32)
        Sx = P.sbring(es, "SxS", [128, 20], F32, 2)
        osb = P.sb(es, "osb", [16, D], F32)
        zsb = P.sb(es, "zsb", [16, 32], F32)
        pself = P.sb(es, "pself", [16, 32], F32)
        ptmp = P.sb(es, "ptmp", [16, 2560], F32)
        stgT = P.sbring(es, "stgTS", [128, KC, 128], BF16, 1)

        def variant(mix, di, d, b, sidx, first, last):
            bs = 4 * b + sidx
            H, rep, nkv = (20, 5, 4) if mix == "A" else (12, 1, 12)
            w = nkv * 128
            qc0, kc0, vc0 = (0, 2560, 3072) if mix == "A" else (3584, 5120, 6656)
            ck, cv = (P.ins["ca_k"], P.ins["ca_v"]) if mix == "A" else (P.ins["cb_k"], P.ins["cb_v"])
            L = 128 if mix == "A" else 2048
            q = qbc.next()
            P.ld(q, q[:, 0:H * 128], PS, PS.t[bs:bs + 1, qc0:qc0 + H * 128].partition_broadcast(128))
            kt, vt = Kt.next(), Vt.next()
            base = L + sidx - 128 * d
            ncache = 128 - sidx if d == 1 else 128
            for (tile, cin, c0) in ((kt, ck, kc0), (vt, cv, vc0)):
                src = cin.t[b, base:base + d * (ncache - 1) + 1:d, :] if d > 1 else cin.t[b, base:base + ncache, :]
                P.ld(tile, tile[0:ncache, 0:w], cin, src)
                if ncache < 128:
                    P.ld(tile, tile[ncache:128, 0:w], PS, PS.t[4 * b:4 * b + sidx, c0:c0 + w])
            pv = prod[:, 0:H * 128].rearrange("p (g r c) -> p g r c", g=nkv, r=rep)
            kv = kt[:, 0:w].rearrange("p (g c) -> p g c", g=nkv).unsqueeze(2).to_broadcast([128, nkv, rep, 128])
            qv = q[:, 0:H * 128].rearrange("p (g r c) -> p g r c", g=nkv, r=rep)
            k.op(k.dve, lambda: nc.vector.tensor_tensor(out=pv, in0=kv, in1=qv, op=ALU.mult), reads=[kt, q], writes=[prod])
            sx = Sx.next()
            k.op(k.dve, lambda: nc.vector.tensor_reduce(out=sx[:, 0:H], in_=prod[:, 0:H * 128].rearrange("p (h c) -> p h c", h=H), axis=AX.X, op=ALU.add),
                 reads=[prod], writes=[sx])
            k.op(k.act, lambda: nc.scalar.activation(out=sx[:, 0:H], in_=sx[:, 0:H], func=AF.Exp, scale=SCALE), reads=[sx], writes=[sx])
            h0 = 0 if mix == "A" else 20
            k.op(k.dve, lambda: nc.vector.tensor_tensor(out=sx[:, 0:H], in0=sx[:, 0:H], in1=eb[:, di, h0:h0 + H], op=ALU.mult), reads=[sx, eb], writes=[sx])
            vv = vt[:, 0:w].rearrange("p (g c) -> p g c", g=nkv).unsqueeze(2).to_broadcast([128, nkv, rep, 128])
            sv_ = sx[:, 0:H].rearrange("p (g r) -> p g r", g=nkv).unsqueeze(3).to_broadcast([128, nkv, rep, 128])
            k.op(k.dve, lambda: nc.vector.tensor_tensor(out=pv, in0=vv, in1=sv_, op=ALU.mult), reads=[vt, sx], writes=[prod])
            lh = ohs[:, 15 - bs:31 - bs]
            nb = (H * 128) // 512
            for j in range(nb):
                pb = P.psum[j]
                P.mm(pb, pb[0:16, 0:512], ohs, lh, prod, prod[:, j * 512:(j + 1) * 512], first, last)
            pb = P.psum[7]
            P.mm(pb, pb[0:16, 0:H], ohs, lh, sx, sx[:, 0:H], first, last)

        def self_and_norm(mix, mult):
            H, rep, nkv = (20, 5, 4) if mix == "A" else (12, 1, 12)
            qc0, kc0, vc0 = (0, 2560, 3072) if mix == "A" else (3584, 5120, 6656)
            h0 = 0 if mix == "A" else 20
            oc0 = 0 if mix == "A" else 2560
            pt = ptmp[:, 0:H * 128].rearrange("p (g r c) -> p g r c", g=nkv, r=rep)
            qv = pst[:, qc0:qc0 + H * 128].rearrange("p (g r c) -> p g r c", g=nkv, r=rep)
            kv = pst[:, kc0:kc0 + nkv * 128].rearrange("p (g c) -> p g c", g=nkv).unsqueeze(2).to_broadcast([16, nkv, rep, 128])
            vv = pst[:, vc0:vc0 + nkv * 128].rearrange("p (g c) -> p g c", g=nkv).unsqueeze(2).to_broadcast([16, nkv, rep, 128])
            k.op(k.dve, lambda: nc.vector.tensor_tensor(out=pt, in0=kv, in1=qv, op=ALU.mult), reads=[pst], writes=[ptmp])
            ps_ = pself[:, h0:h0 + H]
            k.op(k.dve, lambda: nc.vector.tensor_reduce(out=ps_, in_=ptmp[:, 0:H * 128].rearrange("p (h c) -> p h c", h=H), axis=AX.X, op=ALU.add), reads=[ptmp], writes=[pself])
            k.op(k.act, lambda: nc.scalar.activation(out=ps_, in_=ps_, func=AF.Exp, scale=SCALE), reads=[pself], writes=[pself])
            k.op(k.dve, lambda: nc.vector.scalar_tensor_tensor(out=ps_, in0=ps_, scalar=float(mult), in1=es0[:, h0:h0 + H], op0=ALU.mult, op1=ALU.mult), reads=[pself, es0], writes=[pself])
            zt = zsb[:, h0:h0 + H]
            pb7 = P.psum[7]
            k.op(k.dve, lambda: nc.vector.tensor_tensor(out=zt, in0=pb7[0:16, 0:H], in1=ps_, op=ALU.add), reads=[pb7, pself], writes=[zsb])
            if mix == "A":
                k.op(k.dve, lambda: nc.vector.tensor_tensor(out=zt, in0=zt, in1=esk[:, 0:20], op=ALU.add), reads=[zsb, esk], writes=[zsb])
            k.op(k.dve, lambda: nc.vector.reciprocal(out=zt, in_=zt), reads=[zsb], writes=[zsb])
            k.op(k.dve, lambda: nc.vector.tensor_tensor(out=pt, in0=vv, in1=ps_.rearrange("p (g r) -> p g r", g=nkv).unsqueeze(3).to_broadcast([16, nkv, rep, 128]), op=ALU.mult),
                 reads=[pst, pself], writes=[ptmp])
            for j in range((H * 128) // 512):
                pb = P.psum[j]
                k.op(k.dve, lambda: nc.vector.tensor_tensor(out=ptmp[:, j * 512:(j + 1) * 512], in0=ptmp[:, j * 512:(j + 1) * 512], in1=pb[0:16, 0:512], op=ALU.add),
                     reads=[ptmp, pb], writes=[ptmp])
            k.op(k.dve, lambda: nc.vector.tensor_tensor(out=osb[:, oc0:oc0 + H * 128].rearrange("p (h c) -> p h c", h=H),
                                                          in0=ptmp[:, 0:H * 128].rearrange("p (h c) -> p h c", h=H),
                                                          in1=zt.unsqueeze(2).to_broadcast([16, H, 128]), op=ALU.mult), reads=[ptmp, zsb], writes=[osb])

        for b in range(4):
            for sidx in range(4):
                variant("A", 0, 1, b, sidx, b == 0 and sidx == 0, b == 3 and sidx == 3)
        self_and_norm("A", 1)
        n = 0
        for b in range(4):
            for sidx in range(4):
                for di, d in enumerate(DILS):
                    variant("B", di, d, b, sidx, n == 0, n == 47)
                    n += 1
        self_and_norm("B", 3)
        to_featT(P, Ring(P.psum), osb, NS, idt, None, OT, SEQ, stgT)


EXPM05 = float(np.exp(-0.5))


def phase_rwkv_prep(P):
    nc, k = P.nc, P.k
    X1T = P.scr["X1T"]
    RC, WC, AC, BC, KFC, VC, GC, A1C = (P.scr[n] for n in ("RC", "WC", "AC", "BC", "KFC", "VC", "GC", "A1C"))
    with PhaseStack() as es:
        idt = P.sb(es, "idt", [128, 128], F32)
        P.ld(idt, idt[:], P.ins["ident"], P.ins["ident"].t)
        bones = P.sb(es, "bones", [128, 128], F32)
        P.ld(bones, bones[:], P.ins["bones"], P.ins["bones"].t)
        PF = P.sb(es, "PF", [128, KC, 10], F32)
        OMM = P.sb(es, "OMM", [128, KC, 6], F32)
        shT = P.sb(es, "shT", [128, KC, 4], BF16)
        es_setup = PhaseStack()
        es_setup.__enter__()
        prm = P.sb(es_setup, "prm", [10, D], F32)
        P.ld(prm, prm[0:6, :], P.ins["rw_mu"], P.ins["rw_mu"].t)
        for i, nm in enumerate(("rw_w0", "rw_a0", "rw_k_k", "rw_k_a")):
            P.ld(prm, prm[6 + i:7 + i, :], P.ins[nm], P.ins[nm].t)
        pb = P.psum[0]
        for kc in range(KC):
            P.tr(pb, pb[:, kc * 10:kc * 10 + 10], prm, prm[0:10, kc * 128:(kc + 1) * 128], idt, idt[0:10, 0:10])
        k.op(k.dve, lambda: nc.vector.tensor_copy(out=PF[:], in_=pb[:, 0:320].rearrange("p (c i) -> p c i", i=10)), reads=[pb], writes=[PF])
        k.op(k.dve, lambda: nc.vector.tensor_scalar(out=OMM[:], in0=PF[:, :, 0:6], scalar1=-1.0, scalar2=1.0, op0=ALU.mult, op1=ALU.add), reads=[PF], writes=[OMM])
        shs = P.sb(es_setup, "shs", [4, D], F32)
        P.ld(shs, shs[:], P.ins["st_shift"], P.ins["st_shift"].t)
        pb = P.psum[1]
        for kc in range(KC):
            P.tr(pb, pb[:, kc * 4:kc * 4 + 4], shs, shs[0:4, kc * 128:(kc + 1) * 128], idt, idt[0:4, 0:4])
        k.op(k.dve, lambda: nc.vector.tensor_copy(out=shT[:], in_=pb[:, 0:128].rearrange("p (c i) -> p c i", i=4)), reads=[pb], writes=[shT])
        es_setup.__exit__(None, None, None)

        NH = 1024
        xh = P.sb(es, "xh", [128, KC, 1 + NH + NS], BF16)
        pvs = P.sb(es, "pvs", [128, KC, NS], BF16)
        xi = P.sb(es, "xi", [128, KC, NH + NS], BF16)
        hw = P.sb(es, "hw", [128, NH + NS], BF16)
        ha = P.sb(es, "ha", [128, NH + NS], BF16)
        hg = P.sb(es, "hg", [128, 4, NH + NS], BF16)
        stg = P.sbring(es, "stgR", [128, 512], F32, 4)
        ew = P.sbring(es, "ewR", [128, 512], F32, 6)

        for hf in range(2):
            c0 = hf * NH
            nsamp = NS if hf == 1 else 0
            ncol = NH + nsamp
            chunks = [(0, 512), (512, 512)] + ([(NH, NS)] if hf == 1 else [])
            gcol = lambda t0: (c0 + t0) if t0 < NH else SEQ
            if hf == 0:
                k.op(k.dve, lambda: nc.vector.memset(xh[:, :, 0:1], 0.0), writes=[xh])
                for q0 in range(0, KC, 8):
                    P.ld(xh, xh[:, q0:q0 + 8, 1:1 + NH], X1T, X1T.t[q0:q0 + 8, :, 0:NH].rearrange("c p t -> p c t"))
            else:
                for q0 in range(0, KC, 8):
                    P.ld(xh, xh[:, q0:q0 + 8, 0:1 + NH], X1T, X1T.t[q0:q0 + 8, :, NH - 1:2 * NH].rearrange("c p t -> p c t"))
                    P.ld(xh, xh[:, q0:q0 + 8, 1 + NH:1 + NH + NS], X1T, X1T.t[q0:q0 + 8, :, SEQ:SEQ + NS].rearrange("c p t -> p c t"))
                xs4 = xh[:, :, 1 + NH:1 + NH + NS].rearrange("p c (b s) -> p c b s", s=4)
                pv4 = pvs[:].rearrange("p c (b s) -> p c b s", s=4)
                k.op(k.dve, lambda: nc.vector.tensor_copy(out=pv4[:, :, :, 1:4], in_=xs4[:, :, :, 0:3]), reads=[xh], writes=[pvs])
                k.op(k.dve, lambda: nc.vector.tensor_copy(out=pv4[:, :, :, 0], in_=shT[:]), reads=[shT], writes=[pvs])

            def mix(i):
                for kc in range(KC):
                    eng = k.dve
                    e = nc.vector
                    k.op(eng, lambda: e.tensor_scalar(out=xi[:, kc, 0:NH], in0=xh[:, kc, 1:1 + NH], scalar1=OMM[:, kc, i:i + 1], scalar2=None, op0=ALU.mult),
                         reads=[xh, OMM], writes=[xi])
                    k.op(eng, lambda: e.scalar_tensor_tensor(out=xi[:, kc, 0:NH], in0=xh[:, kc, 0:NH], scalar=PF[:, kc, i:i + 1], in1=xi[:, kc, 0:NH], op0=ALU.mult, op1=ALU.add),
                         reads=[xh, PF, xi], writes=[xi])
                    if nsamp:
                        k.op(eng, lambda: e.tensor_scalar(out=xi[:, kc, NH:NH + NS], in0=xh[:, kc, 1 + NH:1 + NH + NS], scalar1=OMM[:, kc, i:i + 1], scalar2=None, op0=ALU.mult),
                             reads=[xh, OMM], writes=[xi])
                        k.op(eng, lambda: e.scalar_tensor_tensor(out=xi[:, kc, NH:NH + NS], in0=pvs[:, kc, :], scalar=PF[:, kc, i:i + 1], in1=xi[:, kc, NH:NH + NS], op0=ALU.mult, op1=ALU.add),
                             reads=[pvs, PF, xi], writes=[xi])

            mix(1)
            P.wsrc = P.ins["rw_w1"]
            proj_pass(P, xi, P.ins["rw_w1"].t, 128, "A", lambda ci, t0, tn, pb: k.op(
                k.act, lambda: nc.scalar.activation(out=hw[:, t0:t0 + tn], in_=pb[:, 0:tn], func=AF.Tanh), reads=[pb], writes=[hw]), chunks=chunks, blk=128)
            mix(4)
            P.wsrc = P.ins["rw_a1"]
            proj_pass(P, xi, P.ins["rw_a1"].t, 128, "A", lambda ci, t0, tn, pb: k.op(
                k.act, lambda: nc.scalar.copy(out=ha[:, t0:t0 + tn], in_=pb[:, 0:tn]), reads=[pb], writes=[ha]), chunks=chunks, blk=128)
            mix(5)
            P.wsrc = P.ins["rw_g1"]
            for ci in range(4):
                cw = 128 if ci < 3 else 96
                proj_pass(P, xi, P.ins["rw_g1"].t[:, ci * 128:ci * 128 + cw], cw, "A", lambda _c, t0, tn, pb, ci=ci, cw=cw: k.op(
                    k.act, lambda: nc.scalar.activation(out=hg[0:cw, ci, t0:t0 + tn], in_=pb[0:cw, 0:tn], func=AF.Sigmoid), reads=[pb], writes=[hg]), chunks=chunks, blk=cw)

            def second(Wb, hsrc, hview, kparts, emit2):
                P.wsrc = Wb
                with PhaseStack() as es2:
                    wr = P.sbring(es2, "w2blk", [128, 4, 256], BF16, 2)
                    pbk = Ring(P.psum)
                    for cb in range(0, D, 256):
                        wb = wr.next()
                        for ki, (r0, rn) in enumerate(kparts):
                            P.ld(wb, wb[0:rn, ki, :], Wb, Wb.t[r0:r0 + rn, cb:cb + 256], eng=k.pool)
                        for h in range(2):
                            gch = cb // 128 + h
                            for (t0, tn) in chunks:
                                pb = pbk.next()
                                for ki, (r0, rn) in enumerate(kparts):
                                    P.mm(pb, pb[:, 0:tn], wb, wb[0:rn, ki, h * 128:(h + 1) * 128], hsrc, hview(ki, rn, t0, tn), ki == 0, ki == len(kparts) - 1)
                                emit2(gch, t0, tn, pb)

            def emit_a(gch, t0, tn, pb):
                s_ = stg.next()
                k.op(k.act, lambda: nc.scalar.activation(out=s_[:, 0:tn], in_=pb[:, 0:tn], func=AF.Sigmoid, bias=PF[:, gch, 7:8]), reads=[pb, PF], writes=[s_])
                P.st(A1C, A1C.t[gch * 128:(gch + 1) * 128, gcol(t0):gcol(t0) + tn], s_, s_[:, 0:tn])

            def emit_w(gch, t0, tn, pb):
                s_ = stg.next()
                k.op(k.act, lambda: nc.scalar.activation(out=s_[:, 0:tn], in_=pb[:, 0:tn], func=AF.Sigmoid, bias=PF[:, gch, 6:7]), reads=[pb, PF], writes=[s_])
                k.op(k.act, lambda: nc.scalar.activation(out=s_[:, 0:tn], in_=s_[:, 0:tn], func=AF.Exp, scale=-EXPM05), reads=[s_], writes=[s_])
                P.st(WC, WC.t[gch * 128:(gch + 1) * 128, gcol(t0):gcol(t0) + tn], s_, s_[:, 0:tn])

            def emit_g(gch, t0, tn, pb):
                s_ = stg.next()
                P.evac(s_, s_[:, 0:tn], pb, pb[:, 0:tn])
                P.st(GC, GC.t[gch * 128:(gch + 1) * 128, gcol(t0):gcol(t0) + tn], s_, s_[:, 0:tn])

            second(P.ins["rw_a2"], ha, lambda ki, rn, t0, tn: ha[0:rn, t0:t0 + tn], [(0, 128)], emit_a)
            second(P.ins["rw_w2"], hw, lambda ki, rn, t0, tn: hw[0:rn, t0:t0 + tn], [(0, 128)], emit_w)
            second(P.ins["rw_g2"], hg, lambda ki, rn, t0, tn: hg[0:rn, ki, t0:t0 + tn], [(0, 128), (128, 128), (256, 128), (384, 96)], emit_g)

            def emit_plain(dst):
                def f(ci, t0, tn, pb):
                    s_ = stg.next()
                    P.evac(s_, s_[:, 0:tn], pb, pb[:, 0:tn])
                    P.st(dst, dst.t[ci * 128:(ci + 1) * 128, gcol(t0):gcol(t0) + tn], s_, s_[:, 0:tn])
                return f
            P.wsrc = P.ins["rw_w_rkv"]
            mix(0)
            proj_pass(P, xi, P.ins["rw_w_rkv"].t[0], D, "A", emit_plain(RC), chunks=chunks, blk=128)
            mix(3)
            proj_pass(P, xi, P.ins["rw_w_rkv"].t[2], D, "A", emit_plain(VC), chunks=chunks, blk=128)
            mix(2)

            def emit_k(gch, t0, tn, pb):
                kraw, a_t, kkn, sq, t1 = ew.next(), ew.next(), ew.next(), ew.next(), ew.next()
                gc_ = gcol(t0)
                P.evac(kraw, kraw[:, 0:tn], pb, pb[:, 0:tn])
                P.ld(a_t, a_t[:, 0:tn], A1C, A1C.t[gch * 128:(gch + 1) * 128, gc_:gc_ + tn])
                k.op(k.dve, lambda: nc.vector.tensor_scalar(out=kkn[:, 0:tn], in0=kraw[:, 0:tn], scalar1=PF[:, gch, 8:9], scalar2=None, op0=ALU.mult), reads=[kraw, PF], writes=[kkn])
                k.op(k.pool, lambda: nc.gpsimd.tensor_tensor(out=sq[:, 0:tn], in0=kkn[:, 0:tn], in1=kkn[:, 0:tn], op=ALU.mult), reads=[kkn], writes=[sq])
                pb2 = P.psum[7]
                P.mm(pb2, pb2[:, 0:tn], bones, bones[:], sq, sq[:, 0:tn], True, True)
                k.op(k.act, lambda: nc.scalar.activation(out=sq[:, 0:tn], in_=pb2[:, 0:tn], func=AF.Sqrt), reads=[pb2], writes=[sq])
                k.op(k.dve, lambda: nc.vector.tensor_scalar_max(out=sq[:, 0:tn], in0=sq[:, 0:tn], scalar1=1e-12), reads=[sq], writes=[sq])
                k.op(k.dve, lambda: nc.vector.reciprocal(out=sq[:, 0:tn], in_=sq[:, 0:tn]), reads=[sq], writes=[sq])
                k.op(k.dve, lambda: nc.vector.tensor_tensor(out=kkn[:, 0:tn], in0=kkn[:, 0:tn], in1=sq[:, 0:tn], op=ALU.mult), reads=[kkn, sq], writes=[kkn])
                k.op(k.pool, lambda: nc.gpsimd.tensor_scalar(out=sq[:, 0:tn], in0=kkn[:, 0:tn], scalar1=-1.0, scalar2=None, op0=ALU.mult), reads=[kkn], writes=[sq])
                P.st(AC, AC.t[gch * 128:(gch + 1) * 128, gc_:gc_ + tn], sq, sq[:, 0:tn])
                k.op(k.dve, lambda: nc.vector.tensor_tensor(out=kkn[:, 0:tn], in0=kkn[:, 0:tn], in1=a_t[:, 0:tn], op=ALU.mult), reads=[kkn, a_t], writes=[kkn])
                P.st(BC, BC.t[gch * 128:(gch + 1) * 128, gc_:gc_ + tn], kkn, kkn[:, 0:tn])
                k.op(k.dve, lambda: nc.vector.tensor_scalar(out=t1[:, 0:tn], in0=a_t[:, 0:tn], scalar1=-1.0, scalar2=PF[:, gch, 9:10], op0=ALU.add, op1=ALU.mult), reads=[a_t, PF], writes=[t1])
                k.op(k.dve, lambda: nc.vector.scalar_tensor_tensor(out=t1[:, 0:tn], in0=t1[:, 0:tn], scalar=1.0, in1=kraw[:, 0:tn], op0=ALU.add, op1=ALU.mult), reads=[t1, kraw], writes=[t1])
                P.st(KFC, KFC.t[gch * 128:(gch + 1) * 128, gc_:gc_ + tn], t1, t1[:, 0:tn])
            proj_pass(P, xi, P.ins["rw_w_rkv"].t[1], D, "A", emit_k, chunks=chunks, blk=128)


GN_EPS = 64e-5
TCH = 32


def phase_rwkv_scan(P):
    nc, k = P.nc, P.k
    RC, WC, AC, BC, KFC, VC, YC = (P.scr[n] for n in ("RC", "WC", "AC", "BC", "KFC", "VC", "YC"))
    with PhaseStack() as es:
        idt = P.sb(es, "idt", [128, 128], F32)
        P.ld(idt, idt[:], P.ins["ident"], P.ins["ident"].t)
        maskp = P.sb(es, "maskp", [128, 2], F32); P.ld(maskp, maskp[:], P.ins["maskp"], P.ins["maskp"].t)
        maskh = P.sb(es, "maskh", [128, 2], F32); P.ld(maskh, maskh[:], P.ins["maskh"], P.ins["maskh"].t)
        maskg = P.sb(es, "maskg", [128, 32], F32); P.ld(maskg, maskg[:], P.ins["maskg"], P.ins["maskg"].t)
        rk = P.sb(es, "rk", [64, 64], F32); P.ld(rk, rk[:], P.ins["rw_r_k"], P.ins["rw_r_k"].t)
        gng = P.sb(es, "gng", [64, 64], F32); P.ld(gng, gng[:], P.ins["rw_gn_g"], P.ins["rw_gn_g"].t.rearrange("o (r v) -> (o r) v", v=64))
        gnb = P.sb(es, "gnb", [64, 64], F32); P.ld(gnb, gnb[:], P.ins["rw_gn_b"], P.ins["rw_gn_b"].t.rearrange("o (r v) -> (o r) v", v=64))
        MM = [P.sb(es, "MA", [128, 2048], F32), P.sb(es, "MB", [128, 2048], F32)]
        Mw = P.sb(es, "Mw", [128, 2048], F32)
        RVr = P.sbring(es, "RV", [128, 2048], F32, 2)
        ytmp = P.sb(es, "ytmp", [64, 2048], F32)
        aF = P.sb(es, "aF", [128, 32, TCH], F32); wF = P.sb(es, "wF", [128, 32, TCH], F32); rF = P.sb(es, "rF", [128, 32, TCH], F32)
        LU = P.sb(es, "LU", [128, TCH, 64], F32); LRb = P.sb(es, "LRb", [128, TCH, 64], BF16)
        Mb = P.sb(es, "Mb", [128, 2048], BF16)
        PU = Buf("psU", P.psum_all[:, 0:4, :]); PO = Buf("psO", P.psum_all[:, 4:8, :])
        LBK = P.sb(es, "LBK", [128, TCH, 128], F32)
        BK = P.sb(es, "BK", [128, 64, TCH], F32); VH = P.sb(es, "VH", [128, 64, TCH], F32)
        RH = P.sb(es, "RH", [64, 64, TCH], F32); KH = P.sb(es, "KH", [64, 64, TCH], F32)
        yacc = P.sb(es, "yacc", [64, 64, TCH], F32)
        t1 = P.sb(es, "gt1", [64, 64, TCH], F32); t2 = P.sb(es, "gt2", [64, 64, TCH], F32)
        st = P.sb(es, "gst", [64, 6, TCH], F32)
        Sio = P.sb(es, "Sio", [64, 32, 128], F32)
        cur = [0]

        def hm(src, c, n):
            return src.t[:, c:c + n].rearrange("(r q) t -> r q t", q=64)

        def load_state(b):
            sw = P.ins["st_wkv"]
            P.ld(Sio, Sio[:].rearrange("v g (h q) -> v g h q", h=2), sw, sw.t[b].rearrange("(g h) v q -> v g h q", h=2))
            M = MM[cur[0]]
            for g4 in range(8):
                pb = P.psum[g4]
                for j in range(4):
                    g = g4 * 4 + j
                    P.tr(pb, pb[:, j * 64:(j + 1) * 64], Sio, Sio[:, g, :], idt, idt[0:64, 0:64])
                P.evac(M, M[:, g4 * 256:(g4 + 1) * 256], pb, pb[:, 0:256])
            k.barrier()

        def store_state(dst_b, dst_ap):
            k.barrier()
            M = MM[cur[0]]
            for g4 in range(8):
                pb = P.psum[g4]
                for j in range(4):
                    g = g4 * 4 + j
                    P.tr(pb, pb[0:64, j * 128:(j + 1) * 128], M, M[:, g * 64:(g + 1) * 64], idt, idt[:])
                P.evac(Sio, Sio[:, g4 * 4:(g4 + 1) * 4, :], pb, pb[0:64, 0:512].rearrange("v (j q) -> v j q", j=4))
            k.dma(k.sp, lambda: nc.sync.dma_start(out=dst_ap.rearrange("(g h) v q -> v g h q", h=2), in_=Sio[:].rearrange("v g (h q) -> v g h q", h=2)),
                  reads=[Sio], writes=[dst_b], acc=True)
            k.barrier()

        def chunk(c, n):
            for (tile, src) in ((aF, AC), (wF, WC), (rF, RC)):
                P.ld(tile, tile[:, :, 0:n], src, src.t[:, c:c + n].rearrange("(g p) t -> p g t", p=128))
            P.ld(BK, BK[0:64, :, 0:n], BC, hm(BC, c, n))
            P.ld(BK, BK[64:128, :, 0:n], KFC, hm(KFC, c, n))
            P.ld(VH, VH[0:64, :, 0:n], VC, hm(VC, c, n))
            P.ld(VH, VH[64:128, :, 0:n], VC, hm(VC, c, n))
            P.ld(RH, RH[:, :, 0:n], RC, hm(RC, c, n))
            P.ld(KH, KH[:, :, 0:n], KFC, hm(KFC, c, n))
            for (L_, F_) in ((LU, aF), (LRb, rF)):
                k.op(k.pool, lambda: nc.gpsimd.tensor_tensor(out=L_[:, 0:n, :].rearrange("p t (g h) -> p t g h", h=2),
                                                             in0=F_[:, :, 0:n].rearrange("p g t -> p t g").unsqueeze(3).to_broadcast([128, n, 32, 2]),
                                                             in1=maskp[:].unsqueeze(1).unsqueeze(1).to_broadcast([128, n, 32, 2]), op=ALU.mult),
                     reads=[F_, maskp], writes=[L_])
            k.op(k.pool, lambda: nc.gpsimd.tensor_tensor(out=LBK[:, 0:n, :].rearrange("p t (h q) -> p t h q", h=2),
                                                         in0=BK[:, :, 0:n].rearrange("p q t -> p t q").unsqueeze(2).to_broadcast([128, n, 2, 64]),
                                                         in1=maskh[:].unsqueeze(1).unsqueeze(3).to_broadcast([128, n, 2, 64]), op=ALU.mult),
                 reads=[BK, maskh], writes=[LBK])
            def ypath_a(i_, Msrc):
                k.op(k.act, lambda: nc.scalar.copy(out=Mb[:], in_=Msrc[:]), reads=[Msrc], writes=[Mb])
                for j in range(4):
                    P.mm(PU, PU.t[0:64, j, :], LRb, LRb[:, i_, :], Mb, Mb[:, j * 512:(j + 1) * 512], True, True)
                k.op(k.act, lambda: nc.scalar.copy(out=ytmp[:], in_=PU.t[0:64, :, :].rearrange("p b c -> p (b c)")), reads=[PU], writes=[ytmp])
                k.op(k.pool, lambda: nc.gpsimd.tensor_tensor(out=ytmp[:].rearrange("p (g v) -> p g v", v=64), in0=ytmp[:].rearrange("p (g v) -> p g v", v=64),
                                                             in1=maskg[0:64, :].unsqueeze(2).to_broadcast([64, 32, 64]), op=ALU.mult), reads=[ytmp, maskg], writes=[ytmp])

            def ypath_b(i_):
                k.op(k.dve, lambda: nc.vector.tensor_reduce(out=yacc[:, :, i_], in_=ytmp[:].rearrange("p (g v) -> p v g", v=64), axis=AX.X, op=ALU.add),
                     reads=[ytmp], writes=[yacc])

            pend_b = None
            for i in range(n):
                M = MM[cur[0]]; Mn = MM[1 - cur[0]]
                RV = RVr.next()
                k.op(k.pool, lambda: nc.gpsimd.tensor_tensor(out=RV[64:128, :].rearrange("p (g v) -> p g v", v=64),
                                                             in0=VH[64:128, :, i].unsqueeze(1).to_broadcast([64, 32, 64]),
                                                             in1=maskg[64:128, :].unsqueeze(2).to_broadcast([64, 32, 64]), op=ALU.mult),
                     reads=[VH, maskg], writes=[RV])
                for j in range(4):
                    P.mm(PU, PU.t[0:64, j, :], LU, LU[:, i, :], M, M[:, j * 512:(j + 1) * 512], True, True)
                k.op(k.dve, lambda: nc.vector.tensor_tensor(out=Mw[:].rearrange("p (g v) -> p g v", v=64), in0=M[:].rearrange("p (g v) -> p g v", v=64),
                                                            in1=wF[:, :, i].unsqueeze(2).to_broadcast([128, 32, 64]), op=ALU.mult),
                     reads=[M, wF], writes=[Mw])
                k.op(k.dve, lambda: nc.vector.tensor_tensor(out=RV[0:64, :].rearrange("p (g v) -> p g v", v=64),
                                                            in0=PU.t[0:64, :, :].rearrange("p b (g v) -> p (b g) v", v=64),
                                                            in1=maskg[0:64, :].unsqueeze(2).to_broadcast([64, 32, 64]), op=ALU.mult),
                     reads=[PU, maskg], writes=[RV])
                if pend_b is not None:
                    ypath_b(pend_b)
                    pend_b = None
                for j in range(4):
                    P.mm(PO, PO.t[:, j, :], LBK, LBK[:, i, :], RV, RV[:, j * 512:(j + 1) * 512], True, True)
                k.op(k.dve, lambda: nc.vector.tensor_tensor(out=Mn[:], in0=Mw[:], in1=PO.t[:, :, :].rearrange("p b c -> p (b c)"), op=ALU.add),
                     reads=[Mw, PO], writes=[Mn])
                cur[0] = 1 - cur[0]
                if i > 0:
                    ypath_a(i - 1, M)
                    pend_b = i - 1
            if pend_b is not None:
                ypath_b(pend_b)
            ypath_a(n - 1, MM[cur[0]])
            ypath_b(n - 1)
            yv = yacc[:, :, 0:n]
            ytv = yv.rearrange("r v t -> r t v")
            k.op(k.dve, lambda: nc.vector.tensor_reduce(out=st[:, 0, 0:n], in_=ytv, axis=AX.X, op=ALU.add), reads=[yacc], writes=[st])
            k.op(k.dve, lambda: nc.vector.tensor_tensor(out=t1[:, :, 0:n], in0=yv, in1=yv, op=ALU.mult), reads=[yacc], writes=[t1])
            k.op(k.dve, lambda: nc.vector.tensor_reduce(out=st[:, 1, 0:n], in_=t1[:, :, 0:n].rearrange("r v t -> r t v"), axis=AX.X, op=ALU.add), reads=[t1], writes=[st])
            k.op(k.dve, lambda: nc.vector.tensor_scalar(out=st[:, 0:2, 0:n], in0=st[:, 0:2, 0:n], scalar1=1.0 / 64, scalar2=None, op0=ALU.mult), reads=[st], writes=[st])
            k.op(k.dve, lambda: nc.vector.tensor_tensor(out=st[:, 2, 0:n], in0=st[:, 0, 0:n], in1=st[:, 0, 0:n], op=ALU.mult), reads=[st], writes=[st])
            k.op(k.dve, lambda: nc.vector.tensor_tensor(out=st[:, 3, 0:n], in0=st[:, 1, 0:n], in1=st[:, 2, 0:n], op=ALU.subtract), reads=[st], writes=[st])
            k.op(k.dve, lambda: nc.vector.tensor_scalar(out=st[:, 3, 0:n], in0=st[:, 3, 0:n], scalar1=GN_EPS, scalar2=None, op0=ALU.add), reads=[st], writes=[st])
            k.op(k.act, lambda: nc.scalar.activation(out=st[:, 3, 0:n], in_=st[:, 3, 0:n], func=AF.Sqrt), reads=[st], writes=[st])
            k.op(k.dve, lambda: nc.vector.reciprocal(out=st[:, 3, 0:n], in_=st[:, 3, 0:n]), reads=[st], writes=[st])
            k.op(k.dve, lambda: nc.vector.tensor_tensor(out=t1[:, :, 0:n], in0=yv, in1=st[:, 0, 0:n].unsqueeze(1).to_broadcast([64, 64, n]), op=ALU.subtract), reads=[yacc, st], writes=[t1])
            k.op(k.dve, lambda: nc.vector.tensor_tensor(out=t1[:, :, 0:n], in0=t1[:, :, 0:n], in1=st[:, 3, 0:n].unsqueeze(1).to_broadcast([64, 64, n]), op=ALU.mult), reads=[t1, st], writes=[t1])
            k.op(k.dve, lambda: nc.vector.tensor_tensor(out=t1[:, :, 0:n], in0=t1[:, :, 0:n], in1=gng[:].unsqueeze(2).to_broadcast([64, 64, n]), op=ALU.mult), reads=[t1, gng], writes=[t1])
            k.op(k.dve, lambda: nc.vector.tensor_tensor(out=t1[:, :, 0:n], in0=t1[:, :, 0:n], in1=gnb[:].unsqueeze(2).to_broadcast([64, 64, n]), op=ALU.add), reads=[t1, gnb], writes=[t1])
            k.op(k.pool, lambda: nc.gpsimd.tensor_tensor(out=t2[:, :, 0:n], in0=RH[:, :, 0:n], in1=KH[:, :, 0:n], op=ALU.mult), reads=[RH, KH], writes=[t2])
            k.op(k.pool, lambda: nc.gpsimd.tensor_tensor(out=t2[:, :, 0:n], in0=t2[:, :, 0:n], in1=rk[:].unsqueeze(2).to_broadcast([64, 64, n]), op=ALU.mult), reads=[t2, rk], writes=[t2])
            k.op(k.dve, lambda: nc.vector.tensor_reduce(out=st[:, 4, 0:n], in_=t2[:, :, 0:n].rearrange("r q t -> r t q"), axis=AX.X, op=ALU.add), reads=[t2], writes=[st])
            k.op(k.dve, lambda: nc.vector.tensor_tensor(out=t2[:, :, 0:n], in0=VH[0:64, :, 0:n], in1=st[:, 4, 0:n].unsqueeze(1).to_broadcast([64, 64, n]), op=ALU.mult), reads=[VH, st], writes=[t2])
            k.op(k.dve, lambda: nc.vector.tensor_tensor(out=t1[:, :, 0:n], in0=t1[:, :, 0:n], in1=t2[:, :, 0:n], op=ALU.add), reads=[t1, t2], writes=[t1])
            P.st(YC, hm(YC, c, n), t1, t1[:, :, 0:n])

        k.op(k.dve, lambda: nc.vector.memset(MM[0][:], 0.0), writes=[MM[0]])
        for c in range(0, SEQ, TCH):
            chunk(c, TCH)
        store_state(P.outs["wkv_p"], P.outs["wkv_p"].t)
        for b in range(4):
            load_state(b)
            chunk(SEQ + 4 * b, 4)
            store_state(P.outs["wkv_s"], P.outs["wkv_s"].t[b])


def phase_rwkv_out(P):
    nc, k = P.nc, P.k
    YC, GC, MIX = P.scr["YC"], P.scr["GC"], P.scr["MIX"]
    with PhaseStack() as es:
        actT = P.sb(es, "actT", [128, KC, T_ALL], BF16)
        with PhaseStack() as es2:
            yr = P.sbring(es2, "yr", [128, T_ALL], F32, 2)
            gr = P.sbring(es2, "gr", [128, T_ALL], F32, 2)
            for g in range(KC):
                yt, gt = yr.next(), gr.next()
                P.ld(yt, yt[:], YC, YC.t[g * 128:(g + 1) * 128, :])
                P.ld(gt, gt[:], GC, GC.t[g * 128:(g + 1) * 128, :])
                k.op(k.dve, lambda: nc.vector.tensor_tensor(out=actT[:, g, :], in0=yt[:], in1=gt[:], op=ALU.mult), reads=[yt, gt], writes=[actT])
        stg = P.sbring(es, "stgB", [128, 256], F32, 4)
        P.wsrc = P.ins["rw_w_o"]

        def emit(c0, cn, t0, tn, pb):
            s_ = stg.next()
            P.evac(s_, s_[0:tn, 0:cn], pb, pb[0:tn, 0:cn])
            P.st(MIX, MIX.t[t0:t0 + tn, c0:c0 + cn], s_, s_[0:tn, 0:cn])
        proj_pass(P, actT, P.ins["rw_w_o"].t, D, "B", emit)


def phase_rwkv(P):
    phase_tables_bf16(P, 1)
    phase_rwkv_prep(P)
    phase_rwkv_scan(P)
    phase_rwkv_out(P)
    X1 = P.scr["X1"]
    phase_res_ln(P, lambda t0, tn: (X1, X1.t[t0:t0 + tn, :]), P.scr["MIX"], 2, P.scr["XM"], P.scr["XMT"], None)
    phase_peer_q(P, 1, P.scr["XMT"], P.scr["QPT"])
    yp, ys = P.outs["y_p"], P.outs["y_s"]
    dst = lambda t0, tn: (yp, yp.t[t0:t0 + tn, :]) if t0 < SEQ else (ys, ys.t[:, :])
    phase_peer_main(P, 1, P.scr["XM"], P.scr["XMB"], P.scr["QPT"], 3, dst, None)

def relpos_bucket_np(dist):
    n = np.maximum(np.asarray(dist, np.int64), 0)
    nf = np.maximum(n, 1).astype(np.float32)
    large = 16 + (np.log(nf / np.float32(16)) / np.float32(np.log(2048 / 16)) * np.float32(16)).astype(np.int32)
    return np.where(n < 16, n, np.minimum(large, 31))


def host_consts():
    c = {"ident": np.eye(128, dtype=np.float32)}
    sel = np.zeros((32, 3, 384), np.float32)
    msk = np.zeros((32, 384), np.float32)
    for di, d in enumerate(DILS):
        for j in range(127, 256):
            s_ = 255 - j
            sel[relpos_bucket_np(d * s_), di, j] = 1.0
    msk[:, 127:256] = 1.0
    c["sel"] = sel
    c["selmask"] = msk
    c["iota16"] = np.tile(np.arange(16, dtype=np.float32)[None, :], (128, 1))
    ohw = np.zeros((128, 255), np.float32); ohw[:, 127] = 1.0
    c["ohw"] = ohw
    ohs = np.zeros((128, 31), np.float32); ohs[:, 15] = 1.0
    c["ohs"] = ohs
    bones = np.zeros((128, 128), np.float32); bones[:64, :64] = 1.0; bones[64:, 64:] = 1.0
    c["bones"] = bones
    p_ = np.arange(128)
    c["maskp"] = (p_[:, None] // 64 == np.arange(2)[None, :]).astype(np.float32)
    c["maskh"] = (((p_ % 64) % 2)[:, None] == np.arange(2)[None, :]).astype(np.float32)
    c["maskg"] = (((p_ % 64) // 2)[:, None] == np.arange(32)[None, :]).astype(np.float32)
    return c


def build(debug=False, phases=("l0", "sample", "l1")):
    P = Prog(debug=debug)
    nc = P.nc
    P.inp("xp", [SEQ, D]); P.inp("xs", [NS, D])
    P.inp("ca_k", [4, 128, 512]); P.inp("ca_v", [4, 128, 512])
    P.inp("cb_k", [4, 2048, 1536]); P.inp("cb_v", [4, 2048, 1536])
    P.inp("st_shift", [4, D]); P.inp("st_wkv", [4, 64, 64, 64])
    P.inp("w_in_att", [D, 8192]); P.inp("w_out_att", [D, D])
    P.inp("att_sinks", [1, 20]); P.inp("rel_bias", [32, 32])
    P.inp("peer_w_q", [2, D, 2048]); P.inp("peer_sub_keys", [2, 8, 2, 128, 128])
    P.inp("peer_u", [2, 16384, D]); P.inp("peer_v", [2, 16384, D])
    P.inp("ln_g", [4, D]); P.inp("ln_b", [4, D])
    P.inp("rw_mu", [6, D]); P.inp("rw_w_rkv", [3, D, D]); P.inp("rw_w0", [1, D]); P.inp("rw_w1", [D, 128]); P.inp("rw_w2", [128, D])
    P.inp("rw_a0", [1, D]); P.inp("rw_a1", [D, 128]); P.inp("rw_a2", [128, D]); P.inp("rw_g1", [D, 480]); P.inp("rw_g2", [480, D])
    P.inp("rw_k_k", [1, D]); P.inp("rw_k_a", [1, D]); P.inp("rw_r_k", [64, 64]); P.inp("rw_gn_g", [1, D]); P.inp("rw_gn_b", [1, D])
    P.inp("rw_w_o", [D, D])
    P.inp("ident", [128, 128]); P.inp("sel", [32, 3, 384]); P.inp("selmask", [32, 384])
    P.inp("iota16", [128, 16]); P.inp("ohw", [128, 255]); P.inp("ohs", [128, 31])
    P.inp("bones", [128, 128]); P.inp("maskp", [128, 2]); P.inp("maskh", [128, 2]); P.inp("maskg", [128, 32])
    P.out("y_p", [SEQ, D]); P.out("y_s", [NS, D])
    P.out("a_k_p", [128, 512]); P.out("a_v_p", [128, 512])
    P.out("b_k_p", [SEQ, 1536]); P.out("b_v_p", [SEQ, 1536])
    P.out("shift_p", [1, D]); P.out("wkv_p", [64, 64, 64])
    P.out("a_k_s", [4, 128, 512]); P.out("a_v_s", [4, 128, 512])
    P.out("b_k_s", [4, 2048, 1536]); P.out("b_v_s", [4, 2048, 1536])
    P.out("shift_s", [4, D]); P.out("wkv_s", [4, 64, 64, 64])
    P.scratch("QK", [48, 128, SEQ], BF16)
    P.scratch("VA", [SEQ, 512], F32)
    P.scratch("PS", [NS, 8192], F32)
    P.scratch("GGD", [3, 32, 384], F32)
    P.scratch("OT", [32, 128, T_ALL], BF16)
    P.scratch("MIX", [T_ALL, D], F32)
    P.scratch("XM", [T_ALL, D], F32)
    P.scratch("XMB", [T_ALL, D], BF16)
    P.scratch("XMT", [KC, 128, T_ALL], BF16)
    P.scratch("QPT", [16, 128, T_ALL], BF16)
    P.scratch("X1", [T_ALL, D], F32, dbg=True)
    P.scratch("X1T", [KC, 128, T_ALL], BF16)
    for l in range(2):
        P.scratch("UB%d" % l, [16384, D], BF16)
        P.scratch("VB%d" % l, [16384, D], BF16)
    for n_ in ("RC", "WC", "AC", "BC", "KFC", "VC", "GC", "A1C", "YC"):
        P.scratch(n_, [D, T_ALL], F32, dbg=True)
    k = P.k
    with ExitStack() as es:
        P.psum_all = es.enter_context(nc.psum_tensor("psall", [128, 8, 512], F32))
        P.psum = [Buf("ps%d" % i, P.psum_all[:, i, :]) for i in range(8)]
        phase_tables_bf16(P, 0)
        phase_l0a(P)
        phase_sample_caches(P)
        phase_bias_setup(P)
        phase_l0b(P)
        phase_sample_attn(P)
        phase_outproj(P, P.scr["OT"], P.ins["w_out_att"], P.scr["MIX"])
        xp, xs = P.ins["xp"], P.ins["xs"]
        xres = lambda t0, tn: (xp, xp.t[t0:t0 + tn, :]) if t0 < SEQ else (xs, xs.t[:, :])
        phase_res_ln(P, xres, P.scr["MIX"], 0, P.scr["XM"], P.scr["XMT"], None)
        phase_peer_q(P, 0, P.scr["XMT"], P.scr["QPT"])
        phase_peer_main(P, 0, P.scr["XM"], P.scr["XMB"], P.scr["QPT"], 1, P.scr["X1"], P.scr["X1T"])
        X1 = P.scr["X1"]
        o = P.outs["shift_p"]
        k.dma(k.sp, lambda: nc.sync.dma_start(out=o.t, in_=X1.t[SEQ - 1:SEQ, :]), reads=[X1], writes=[o], acc=True)
        o2 = P.outs["shift_s"]
        k.dma(k.sp, lambda: nc.sync.dma_start(out=o2.t, in_=X1.t[SEQ + 3:SEQ + NS:4, :]), reads=[X1], writes=[o2], acc=True)
        if "l1" in phases:
            phase_rwkv(P)
        P.k.finish()
    return P


_CACHE = {}


def kernel(**inputs):
    x_prompt = np.ascontiguousarray(inputs["x_prompt"], dtype=np.float32)
    n_cores = 8
    if "P" not in _CACHE:
        _CACHE["P"] = build()
    P = _CACHE["P"]
    consts = host_consts()
    shared = {}
    for n in ["w_in_att", "w_out_att", "rel_bias", "peer_w_q", "peer_sub_keys", "peer_u", "peer_v",
              "rw_mu", "rw_w_rkv", "rw_w1", "rw_w2", "rw_a1", "rw_a2", "rw_g1", "rw_g2", "rw_r_k", "rw_w_o"]:
        shared[n] = np.ascontiguousarray(inputs[n], dtype=np.float32)
    for n in ["rw_w0", "rw_a0", "rw_k_k", "rw_k_a", "rw_gn_g", "rw_gn_b"]:
        shared[n] = np.ascontiguousarray(inputs[n], dtype=np.float32).reshape(1, D)
    shared["att_sinks"] = np.ascontiguousarray(inputs["att_sinks"], dtype=np.float32).reshape(1, 20)
    shared["ln_g"] = np.ascontiguousarray(inputs["ln_g"], dtype=np.float32).reshape(4, D)
    shared["ln_b"] = np.ascontiguousarray(inputs["ln_b"], dtype=np.float32).reshape(4, D)
    shared.update(consts)
    xps = [np.ascontiguousarray(x_prompt[b]) for b in range(4)]
    in_maps = []
    for c in range(n_cores):
        sl = slice(4 * c, 4 * c + 4)
        m = dict(shared)
        m["xp"] = xps[c // 2]
        m["xs"] = np.ascontiguousarray(inputs["x_sample"][sl], dtype=np.float32).reshape(NS, D)
        m["ca_k"] = np.ascontiguousarray(inputs["cache_a_k"][sl], dtype=np.float32).reshape(4, 128, 512)
        m["ca_v"] = np.ascontiguousarray(inputs["cache_a_v"][sl], dtype=np.float32).reshape(4, 128, 512)
        m["cb_k"] = np.ascontiguousarray(inputs["cache_b_k"][sl], dtype=np.float32).reshape(4, 2048, 1536)
        m["cb_v"] = np.ascontiguousarray(inputs["cache_b_v"][sl], dtype=np.float32).reshape(4, 2048, 1536)
        m["st_shift"] = np.ascontiguousarray(inputs["state_shift"][sl], dtype=np.float32)
        m["st_wkv"] = np.ascontiguousarray(inputs["state_wkv"][sl], dtype=np.float32)
        m = {kk: v for kk, v in m.items() if kk in P.ins}
        in_maps.append(m)
    res = run_bass_kernel_spmd(P.nc, in_maps, core_ids=list(range(n_cores)))
    R = res.results
    ev = [R[2 * b] for b in range(4)]
    y_p = np.stack([r["y_p"] for r in ev])
    y_s = np.concatenate([r["y_s"].reshape(4, 4, D) for r in R])
    a_k_p = np.stack([r["a_k_p"].reshape(128, 4, 128) for r in ev])
    a_v_p = np.stack([r["a_v_p"].reshape(128, 4, 128) for r in ev])
    b_k_p = np.stack([r["b_k_p"].reshape(SEQ, 12, 128) for r in ev])
    b_v_p = np.stack([r["b_v_p"].reshape(SEQ, 12, 128) for r in ev])
    shift_p = np.concatenate([r["shift_p"] for r in ev])
    wkv_p = np.stack([r["wkv_p"] for r in ev])
    a_k_s = np.concatenate([r["a_k_s"].reshape(4, 128, 4, 128) for r in R])
    a_v_s = np.concatenate([r["a_v_s"].reshape(4, 128, 4, 128) for r in R])
    b_k_s = np.concatenate([r["b_k_s"].reshape(4, 2048, 12, 128) for r in R])
    b_v_s = np.concatenate([r["b_v_s"].reshape(4, 2048, 12, 128) for r in R])
    shift_s = np.concatenate([r["shift_s"] for r in R])
    wkv_s = np.concatenate([r["wkv_s"] for r in R])
    return (y_p, y_s, a_k_p, a_v_p, b_k_p, b_v_p, shift_p, wkv_p, a_k_s, a_v_s, b_k_s, b_v_s, shift_s, wkv_s)
```

```python
import numpy as np
from contextlib import ExitStack
import concourse.bass as bass
import concourse.mybir as mybir
from concourse.bass_utils import run_bass_kernel_spmd

F32 = mybir.dt.float32
BF16 = mybir.dt.bfloat16
I32 = mybir.dt.int32
U32 = mybir.dt.uint32
ALU = mybir.AluOpType
AF = mybir.ActivationFunctionType
AX = mybir.AxisListType

D = 4096
KC = 32
SEQ = 2048
NS = 16
HD = 128
SCALE = HD ** -0.5
ALPHA = 4.0 ** 0.25
LN_EPS = 1e-5


class Buf:
    __slots__ = ("name", "w", "r", "t")

    def __init__(self, name, t=None):
        self.name = name
        self.w = {}
        self.r = {}
        self.t = t

    def __getitem__(self, key):
        return self.t[key]


class Eng:
    def __init__(self, k, name, e, is_pe=False):
        self.k = k
        self.name = name
        self.e = e
        self.is_pe = is_pe
        self.sem = None
        self.cnt = 0
        self.waited = {}
        self.dma_sems = []
        self.dma_vals = []
        self.dma_i = 0

    def new_sem(self):
        self.sem = self.k.alloc_sem(self.name)
        self.cnt = 0

    def wait_tok(self, tok):
        if tok is None:
            return
        sem, val = tok
        if self.is_pe and sem is self.sem:
            return
        key = sem.name
        if self.waited.get(key, 0) >= val:
            return
        self.e.wait_ge(sem, val)
        self.waited[key] = val
        self.k.n_wait += 1


class KB:
    def __init__(self):
        self.nc = bass.Bass("TRN2", target_bir_lowering=False)
        nc = self.nc
        self._sem_i = 0
        self.n_wait = 0
        self.n_ins = 0
        self.pe = Eng(self, "pe", nc.tensor, is_pe=True)
        self.dve = Eng(self, "dve", nc.vector)
        self.act = Eng(self, "act", nc.scalar)
        self.pool = Eng(self, "pool", nc.gpsimd)
        self.sp = Eng(self, "sp", nc.sync)
        self.engs = [self.pe, self.dve, self.act, self.pool, self.sp]
        for e in self.engs:
            e.new_sem()
        for e in (self.sp, self.pool):
            for i in range(16):
                e.dma_sems.append(self.alloc_sem(e.name + "_dma%d" % i))
                e.dma_vals.append(0)

    def alloc_sem(self, name):
        self._sem_i += 1
        cm = self.nc.semaphore("%s_s%d" % (name, self._sem_i))
        return cm.__enter__()

    def _deps(self, eng, reads, writes, acc=False):
        toks = []
        for b in reads:
            toks.extend(b.w.values())
        if not acc:
            for b in writes:
                toks.extend(b.w.values())
                toks.extend(b.r.values())
        for t in toks:
            eng.wait_tok(t)

    def _commit(self, tok, reads, writes, acc=False):
        for b in reads:
            old = b.r.get(tok[0].name)
            if old is None or old[1] < tok[1]:
                b.r[tok[0].name] = tok
        for b in writes:
            if acc:
                old = b.w.get(tok[0].name)
                if old is None or old[1] < tok[1]:
                    b.w[tok[0].name] = tok
            else:
                b.w = {tok[0].name: tok}
                b.r = {}

    def op(self, eng, fn, reads=(), writes=()):
        self._deps(eng, reads, writes)
        if eng.cnt >= 30000:
            eng.new_sem()
        ins = fn()
        eng.cnt += 1
        ins.then_inc(eng.sem, 1)
        tok = (eng.sem, eng.cnt)
        self._commit(tok, reads, writes)
        self.n_ins += 1
        return tok

    def dma(self, eng, fn, reads=(), writes=(), acc=False):
        self._deps(eng, reads, writes, acc)
        i = eng.dma_i % len(eng.dma_sems)
        eng.dma_i += 1
        sem = eng.dma_sems[i]
        prev = eng.dma_vals[i]
        if prev > 0:
            eng.wait_tok((sem, prev))
        ins = fn()
        ins.then_inc(sem, 16)
        eng.dma_vals[i] = prev + 16
        tok = (sem, prev + 16)
        self._commit(tok, reads, writes, acc)
        self.n_ins += 1
        return tok

    def barrier(self):
        toks = []
        for eng in (self.sp, self.pool):
            for sem, v in zip(eng.dma_sems, eng.dma_vals):
                if v > 0:
                    toks.append((sem, v))
        for eng in self.engs:
            if eng.cnt > 0:
                toks.append((eng.sem, eng.cnt))
        for e in self.engs:
            for t in toks:
                if t[0] is e.sem:
                    continue
                e.wait_tok(t)

    def finish(self):
        e = self.sp
        for eng in (self.sp, self.pool):
            for sem, v in zip(eng.dma_sems, eng.dma_vals):
                if v > 0:
                    e.wait_tok((sem, v))
        for eng in self.engs:
            if eng is not e and eng.cnt > 0:
                e.wait_tok((eng.sem, eng.cnt))


class Ring:
    def __init__(self, bufs):
        self.bufs = bufs
        self.i = 0

    def next(self):
        b = self.bufs[self.i % len(self.bufs)]
        self.i += 1
        return b


class PhaseStack(ExitStack):
    kb = None

    def __exit__(self, *a):
        if a[0] is None and PhaseStack.kb is not None:
            PhaseStack.kb.barrier()
        return super().__exit__(*a)


class Prog:
    def __init__(self, debug=False):
        self.k = KB()
        PhaseStack.kb = self.k
        self.nc = self.k.nc
        self.debug = debug
        self.ins = {}
        self.outs = {}
        self.scr = {}
        self._evac_i = 0
        self.bg = []

    def inp(self, name, shape, dt=F32):
        t = self.nc.dram_tensor(name, list(shape), dt, kind="ExternalInput")
        self.ins[name] = Buf(name, t.ap())
        return self.ins[name]

    def out(self, name, shape, dt=F32):
        t = self.nc.dram_tensor(name, list(shape), dt, kind="ExternalOutput")
        self.outs[name] = Buf(name, t.ap())
        return self.outs[name]

    def scratch(self, name, shape, dt=F32, dbg=False):
        kind = "ExternalOutput" if (dbg and self.debug) else "Internal"
        t = self.nc.dram_tensor(name, list(shape), dt, kind=kind)
        self.scr[name] = Buf(name, t.ap())
        return self.scr[name]

    def sb(self, es, name, shape, dt=F32):
        self._sb_i = getattr(self, "_sb_i", 0) + 1
        nm = "sb%d_%s" % (self._sb_i, name)
        return Buf(nm, es.enter_context(self.nc.sbuf_tensor(nm, list(shape), dt)))

    def sbring(self, es, name, shape, dt, n):
        return Ring([self.sb(es, "%s%d" % (name, i), shape, dt) for i in range(n)])

    def mm(self, out_b, out_ap, lhsT_b, lhsT_ap, rhs_b, rhs_ap, start, stop):
        nc = self.nc
        return self.k.op(self.k.pe, lambda: nc.tensor.matmul(out_ap, lhsT=lhsT_ap, rhs=rhs_ap, start=start, stop=stop),
                         reads=[lhsT_b, rhs_b], writes=[out_b])

    def tr(self, out_b, out_ap, in_b, in_ap, ident_b, ident_ap):
        nc = self.nc
        return self.k.op(self.k.pe, lambda: nc.tensor.transpose(out=out_ap, in_=in_ap, identity=ident_ap),
                         reads=[in_b, ident_b], writes=[out_b])

    def ld(self, out_b, out_ap, in_b, in_ap, eng=None):
        eng = eng or self.k.sp
        return self.k.dma(eng, lambda: eng.e.dma_start(out=out_ap, in_=in_ap), reads=[in_b], writes=[out_b])

    def st(self, out_b, out_ap, in_b, in_ap, eng=None):
        eng = eng or self.k.sp
        return self.k.dma(eng, lambda: eng.e.dma_start(out=out_ap, in_=in_ap), reads=[in_b], writes=[out_b], acc=True)

    def evac(self, out_b, out_ap, in_b, in_ap, scale=None, extra_reads=()):
        nc = self.nc
        self._evac_i += 1
        if self._evac_i % 2 == 0:
            if scale is None:
                f = lambda: nc.scalar.copy(out=out_ap, in_=in_ap)
            else:
                f = lambda: nc.scalar.mul(out=out_ap, in_=in_ap, mul=scale)
            return self.k.op(self.k.act, f, reads=[in_b, *extra_reads], writes=[out_b])
        else:
            if scale is None:
                f = lambda: nc.vector.tensor_copy(out=out_ap, in_=in_ap)
            else:
                f = lambda: nc.vector.tensor_scalar_mul(out=out_ap, in0=in_ap, scalar1=scale)
            return self.k.op(self.k.dve, f, reads=[in_b, *extra_reads], writes=[out_b])


def phase_l0a(P):
    nc, k = P.nc, P.k
    xp, xs, w_in, ident = P.ins["xp"], P.ins["xs"], P.ins["w_in_att"], P.ins["ident"]
    QK, VA, PS = P.scr["QK"], P.scr["VA"], P.scr["PS"]
    wv = w_in.t.rearrange("(kc p) n -> p kc n", p=128)
    with PhaseStack() as es:
        NT = 1024 + NS
        xT = P.sb(es, "xT", [128, KC, NT], BF16)
        xring = P.sbring(es, "xtile", [128, D], F32, 2)
        wring = P.sbring(es, "wb", [128, KC, 256], BF16, 3)
        stA = P.sbring(es, "stA", [128, 512], BF16, 3)
        stB = P.sbring(es, "stB", [128, 256], F32, 3)
        idt = P.sb(es, "idt", [128, 128], F32)
        P.ld(idt, idt[:], ident, ident.t)
        pbanks = Ring(P.psum)

        def load_xT(src_b, src_ap, ntok, col0):
            xt = xring.next()
            P.ld(xt, xt[0:ntok, :], src_b, src_ap)
            for g in range(8):
                pb = pbanks.next()
                for j in range(4):
                    kc = g * 4 + j
                    P.tr(pb, pb[:, j * 128:j * 128 + ntok], xt, xt[0:ntok, kc * 128:(kc + 1) * 128], idt, idt[0:ntok, 0:ntok])
                P.evac(xT, xT[:, g * 4:(g + 1) * 4, col0:col0 + ntok], pb,
                       pb[:, 0:512].rearrange("p (j t) -> p j t", j=4)[:, :, 0:ntok])

        for ps_i in range(2):
            tok0 = ps_i * 1024
            for tt in range(8):
                load_xT(xp, xp.t[tok0 + tt * 128: tok0 + (tt + 1) * 128, :], 128, tt * 128)
            has_s = (ps_i == 1)
            if has_s:
                load_xT(xs, xs.t[:, :], NS, 1024)
            for j in range(32):
                wb = wring.next()
                P.ld(wb, wb[:], w_in, wv[:, :, j * 256:(j + 1) * 256], eng=k.pool)
                drain_bg(P)
                if j < 10:
                    kind, chunk0 = "q", 2 * j
                elif j < 12:
                    kind, chunk0 = "kA", 20 + 2 * (j - 10)
                elif j < 14:
                    kind, chunk0 = "vA", None
                elif j < 20:
                    kind, chunk0 = "q", 24 + 2 * (j - 14)
                elif j < 26:
                    kind, chunk0 = "kB", 36 + 2 * (j - 20)
                else:
                    kind, chunk0 = "vB", None
                if kind in ("q", "kA", "kB"):
                    for h in range(2):
                        for tc in range(2):
                            pb = pbanks.next()
                            for kc in range(KC):
                                P.mm(pb, pb[:, 0:512], wb, wb[:, kc, h * 128:(h + 1) * 128],
                                     xT, xT[:, kc, tc * 512:(tc + 1) * 512], kc == 0, kc == KC - 1)
                            s = stA.next()
                            P.evac(s, s[:], pb, pb[:, 0:512], scale=(SCALE if kind == "q" else None))
                            P.st(QK, QK.t[chunk0 + h, :, tok0 + tc * 512: tok0 + (tc + 1) * 512], s, s[:])
                if kind in ("kA", "vA", "kB", "vB"):
                    for tt in range(8):
                        gt = tok0 + tt * 128
                        if kind == "kA" and gt != SEQ - 128:
                            continue
                        pb = pbanks.next()
                        for kc in range(KC):
                            P.mm(pb, pb[:, 0:256], xT, xT[:, kc, tt * 128:(tt + 1) * 128], wb, wb[:, kc, :], kc == 0, kc == KC - 1)
                        s = stB.next()
                        P.evac(s, s[:], pb, pb[:, 0:256])
                        if kind == "kA":
                            o = P.outs["a_k_p"]
                            P.st(o, o.t[:, (j - 10) * 256:(j - 9) * 256], s, s[:])
                        elif kind == "vA":
                            P.st(VA, VA.t[gt:gt + 128, (j - 12) * 256:(j - 11) * 256], s, s[:])
                            if gt == SEQ - 128:
                                o = P.outs["a_v_p"]
                                P.st(o, o.t[:, (j - 12) * 256:(j - 11) * 256], s, s[:])
                        elif kind == "kB":
                            o = P.outs["b_k_p"]
                            P.st(o, o.t[gt:gt + 128, (j - 20) * 256:(j - 19) * 256], s, s[:])
                        else:
                            o = P.outs["b_v_p"]
                            P.st(o, o.t[gt:gt + 128, (j - 26) * 256:(j - 25) * 256], s, s[:])
                if has_s:
                    pb = pbanks.next()
                    for kc in range(KC):
                        P.mm(pb, pb[0:NS, 0:256], xT, xT[:, kc, 1024:1024 + NS], wb, wb[:, kc, :], kc == 0, kc == KC - 1)
                    s = stB.next()
                    P.evac(s, s[0:NS, :], pb, pb[0:NS, 0:256])
                    P.st(PS, PS.t[:, j * 256:(j + 1) * 256], s, s[0:NS, :])


DILS = (1, 4, 16)


def phase_bias_setup(P):
    nc, k = P.nc, P.k
    GGD = P.scr["GGD"]
    with PhaseStack() as es:
        tab = P.sb(es, "tab", [32, 32], F32)
        P.ld(tab, tab[:], P.ins["rel_bias"], P.ins["rel_bias"].t)
        sel = P.sb(es, "sel", [32, 3, 384], F32)
        P.ld(sel, sel[:], P.ins["sel"], P.ins["sel"].t)
        msk = P.sb(es, "msk", [32, 384], F32)
        P.ld(msk, msk[:], P.ins["selmask"], P.ins["selmask"].t)
        gg = P.sb(es, "gg", [32, 3, 384], F32)
        for di in range(3):
            pb = P.psum[di]
            P.mm(pb, pb[0:32, 0:384], tab, tab[:, :], sel, sel[:, di, :], True, True)
            k.op(k.act, lambda: nc.scalar.activation(out=gg[:, di, :], in_=pb[0:32, 0:384], func=AF.Exp), reads=[pb], writes=[gg])
            k.op(k.dve, lambda: nc.vector.tensor_tensor(out=gg[:, di, :], in0=gg[:, di, :], in1=msk[:, :], op=ALU.mult), reads=[gg, msk], writes=[gg])
        P.st(GGD, GGD.t.rearrange("d h j -> h d j"), gg, gg[:])


def phase_l0b(P):
    nc, k = P.nc, P.k
    QK, VA, OT, GGD = P.scr["QK"], P.scr["VA"], P.scr["OT"], P.scr["GGD"]
    bvp = P.outs["b_v_p"]
    with PhaseStack() as es:
        qT = P.sbring(es, "qT", [128, SEQ], BF16, 2)
        kT = P.sbring(es, "kT", [128, SEQ], BF16, 2)
        Vt = P.sbring(es, "Vt", [128, 16, 128], BF16, 2)
        Et = P.sbring(es, "Et", [128, 2, 128], F32, 2)
        p1r = P.sbring(es, "p1", [128, 128], F32, 3)
        ptr = P.sbring(es, "pt", [128, 128], BF16, 3)
        rzr = P.sbring(es, "rz", [128, 128], F32, 2)
        ost = P.sbring(es, "ost", [128, SEQ], BF16, 2)
        oacc = P.sb(es, "oacc", [128, SEQ], F32)
        zacc = P.sb(es, "zacc", [128, SEQ], F32)
        ones = P.sb(es, "ones", [128, 128], BF16)
        esk = P.sb(es, "esk", [128, 20], F32)
        k.op(k.dve, lambda: nc.vector.memset(ones[:], 1.0), writes=[ones])
        sinks = P.ins["att_sinks"]
        P.ld(esk, esk[:], sinks, sinks.t.partition_broadcast(128))
        k.op(k.act, lambda: nc.scalar.activation(out=esk[:], in_=esk[:], func=AF.Exp), reads=[esk], writes=[esk])
        pbanks = Ring(P.psum)

        def hankel(eb, slot, di, h, base):
            src = bass.AP(tensor=GGD.t.tensor, offset=(di * 32 + h) * 384 + base, ap=[[1, 128], [1, 128]])
            P.ld(eb, eb[:, slot, :], GGD, src)

        def attend(qb, q_ap, pairs, eb, ob, zb):
            np_ = len(pairs)
            for i, (kb_, k_ap, vb_, v_ap, slot) in enumerate(pairs):
                sb_ = pbanks.next()
                P.mm(sb_, sb_[:, 0:128], kb_, k_ap, qb, q_ap, True, True)
                p1 = p1r.next()
                k.op(k.act, lambda: nc.scalar.activation(out=p1[:], in_=sb_[:, 0:128], func=AF.Exp), reads=[sb_], writes=[p1])
                pt = ptr.next()
                k.op(k.dve, lambda: nc.vector.tensor_tensor(out=pt[:], in0=p1[:], in1=eb[:, slot, ::-1], op=ALU.mult),
                     reads=[p1, eb], writes=[pt])
                P.mm(ob, ob[:, 0:128], vb_, v_ap, pt, pt[:], i == 0, i == np_ - 1)
                P.mm(zb, zb[:, 0:128], ones, ones[:], pt, pt[:], i == 0, i == np_ - 1)

        cur_g = -1
        for h in range(20):
            g = h // 5
            q = qT.next()
            P.ld(q, q[:], QK, QK.t[h])
            if g != cur_g:
                kk = kT.next()
                P.ld(kk, kk[:], QK, QK.t[20 + g])
                vv = Vt.next()
                P.ld(vv, vv[:], VA, VA.t[:, g * 128:(g + 1) * 128].rearrange("(n p) d -> p n d", p=128), eng=k.pool)
                cur_g = g
            eb = Et.next()
            hankel(eb, 0, 0, h, 128)
            hankel(eb, 1, 0, h, 0)
            o_s = ost.next()
            for n in range(16):
                ob, zb = pbanks.next(), pbanks.next()
                pairs = []
                if n > 0:
                    pairs.append((kk, kk[:, (n - 1) * 128:n * 128], vv, vv[:, n - 1, :], 1))
                pairs.append((kk, kk[:, n * 128:(n + 1) * 128], vv, vv[:, n, :], 0))
                attend(q, q[:, n * 128:(n + 1) * 128], pairs, eb, ob, zb)
                rz = rzr.next()
                k.op(k.dve, lambda: nc.vector.tensor_scalar(out=rz[:], in0=zb[:, 0:128], scalar1=esk[:, h:h + 1], scalar2=None, op0=ALU.add),
                     reads=[zb, esk], writes=[rz])
                k.op(k.dve, lambda: nc.vector.reciprocal(out=rz[:], in_=rz[:]), reads=[rz], writes=[rz])
                k.op(k.dve, lambda: nc.vector.tensor_tensor(out=o_s[:, n * 128:(n + 1) * 128], in0=ob[:, 0:128], in1=rz[:], op=ALU.mult),
                     reads=[ob, rz], writes=[o_s])
            P.st(OT, OT.t[h, :, 0:SEQ], o_s, o_s[:])

        for hb in range(12):
            h = 20 + hb
            q = qT.next()
            P.ld(q, q[:], QK, QK.t[24 + hb])
            kk = kT.next()
            P.ld(kk, kk[:], QK, QK.t[36 + hb])
            for di, d in enumerate(DILS):
                nb = SEQ // (128 * d)
                vv = Vt.next()
                P.ld(vv, vv[:].rearrange("p (r n) c -> p r n c", r=d),
                     bvp, bvp.t[:, hb * 128:(hb + 1) * 128].rearrange("(n i r) c -> i r n c", i=128, r=d), eng=k.pool)
                eb = Et.next()
                hankel(eb, 0, di, h, 128)
                hankel(eb, 1, di, h, 0)
                qv = q[:].rearrange("p (n i r) -> p r n i", i=128, r=d)
                kv = kk[:].rearrange("p (n i r) -> p r n i", i=128, r=d)
                ov = oacc[:].rearrange("p (n i r) -> p r n i", i=128, r=d)
                zv = zacc[:].rearrange("p (n i r) -> p r n i", i=128, r=d)
                for r in range(d):
                    for n in range(nb):
                        ob, zb = pbanks.next(), pbanks.next()
                        pairs = []
                        if n > 0:
                            pairs.append((kk, kv[:, r, n - 1, :], vv, vv[:, r * nb + n - 1, :], 1))
                        pairs.append((kk, kv[:, r, n, :], vv, vv[:, r * nb + n, :], 0))
                        attend(q, qv[:, r, n, :], pairs, eb, ob, zb)
                        if di == 0:
                            k.op(k.act, lambda: nc.scalar.copy(out=ov[:, r, n, :], in_=ob[:, 0:128]), reads=[ob], writes=[oacc])
                            k.op(k.dve, lambda: nc.vector.tensor_copy(out=zv[:, r, n, :], in_=zb[:, 0:128]), reads=[zb], writes=[zacc])
                        else:
                            k.op(k.dve, lambda: nc.vector.tensor_tensor(out=ov[:, r, n, :], in0=ov[:, r, n, :], in1=ob[:, 0:128], op=ALU.add),
                                 reads=[ob, oacc], writes=[oacc])
                            k.op(k.dve, lambda: nc.vector.tensor_tensor(out=zv[:, r, n, :], in0=zv[:, r, n, :], in1=zb[:, 0:128], op=ALU.add),
                                 reads=[zb, zacc], writes=[zacc])
            k.op(k.dve, lambda: nc.vector.reciprocal(out=zacc[:], in_=zacc[:]), reads=[zacc], writes=[zacc])
            o_s = ost.next()
            k.op(k.dve, lambda: nc.vector.tensor_tensor(out=o_s[:], in0=oacc[:], in1=zacc[:], op=ALU.mult), reads=[oacc, zacc], writes=[o_s])
            P.st(OT, OT.t[h, :, 0:SEQ], o_s, o_s[:])


T_ALL = SEQ + NS
TOK_TILES = [(i * 128, 128) for i in range(16)] + [(SEQ, NS)]
TOK_CHUNKS = [(i * 512, 512) for i in range(4)] + [(SEQ, NS)]


def load_actT(P, es, src, name="actT"):
    a = P.sb(es, name, [128, KC, T_ALL], BF16)
    for c0 in range(0, KC, 8):
        P.ld(a, a[:, c0:c0 + 8, :], src, src.t[c0:c0 + 8].rearrange("c p t -> p c t"))
    return a


def proj_pass(P, actT, W_ap, ncols, form, emit, kc_n=KC, blk=256, chunks=None, tiles=None):
    k = P.k
    wv = W_ap.rearrange("(kc p) n -> p kc n", p=128)
    with PhaseStack() as es:
        wring = P.sbring(es, "wblk", [128, kc_n, blk], BF16, 3)
        pbanks = Ring(P.psum)
        for c0 in range(0, ncols, blk):
            cn = min(blk, ncols - c0)
            wb = wring.next()
            P.ld(wb, wb[:, :, 0:cn], P.wsrc, wv[:, :, c0:c0 + cn], eng=k.pool)
            drain_bg(P)
            if form == "A":
                for h in range((cn + 127) // 128):
                    m_ = min(128, cn - h * 128)
                    for (t0, tn) in (chunks or TOK_CHUNKS):
                        pb = pbanks.next()
                        for kc in range(kc_n):
                            P.mm(pb, pb[0:m_, 0:tn], wb, wb[:, kc, h * 128:h * 128 + m_], actT, actT[:, kc, t0:t0 + tn], kc == 0, kc == kc_n - 1)
                        emit((c0 // 128) + h, t0, tn, pb)
            else:
                for (t0, tn) in (tiles or TOK_TILES):
                    pb = pbanks.next()
                    for kc in range(kc_n):
                        P.mm(pb, pb[0:tn, 0:cn], actT, actT[:, kc, t0:t0 + tn], wb, wb[:, kc, 0:cn], kc == 0, kc == kc_n - 1)
                    emit(c0, cn, t0, tn, pb)


def ln_rows(P, es_bufs, h, nt, gbc, bbc):
    nc, k = P.nc, P.k
    junk, st = es_bufs
    k.op(k.act, lambda: nc.scalar.activation(out=junk[0:nt, :], in_=h[0:nt, :], func=AF.Copy, accum_out=st[0:nt, 0:1]), reads=[h], writes=[junk, st])
    k.op(k.act, lambda: nc.scalar.activation(out=junk[0:nt, :], in_=h[0:nt, :], func=AF.Square, accum_out=st[0:nt, 1:2]), reads=[h], writes=[junk, st])
    k.op(k.dve, lambda: nc.vector.tensor_scalar(out=st[0:nt, 0:2], in0=st[0:nt, 0:2], scalar1=1.0 / D, scalar2=None, op0=ALU.mult), reads=[st], writes=[st])
    k.op(k.dve, lambda: nc.vector.tensor_tensor(out=st[0:nt, 2:3], in0=st[0:nt, 0:1], in1=st[0:nt, 0:1], op=ALU.mult), reads=[st], writes=[st])
    k.op(k.dve, lambda: nc.vector.tensor_tensor(out=st[0:nt, 3:4], in0=st[0:nt, 1:2], in1=st[0:nt, 2:3], op=ALU.subtract), reads=[st], writes=[st])
    k.op(k.dve, lambda: nc.vector.tensor_scalar(out=st[0:nt, 3:4], in0=st[0:nt, 3:4], scalar1=LN_EPS, scalar2=None, op0=ALU.add), reads=[st], writes=[st])
    k.op(k.act, lambda: nc.scalar.activation(out=st[0:nt, 3:4], in_=st[0:nt, 3:4], func=AF.Sqrt), reads=[st], writes=[st])
    k.op(k.dve, lambda: nc.vector.reciprocal(out=st[0:nt, 3:4], in_=st[0:nt, 3:4]), reads=[st], writes=[st])
    k.op(k.dve, lambda: nc.vector.tensor_scalar(out=h[0:nt, :], in0=h[0:nt, :], scalar1=st[0:nt, 0:1], scalar2=st[0:nt, 3:4], op0=ALU.subtract, op1=ALU.mult), reads=[h, st], writes=[h])
    k.op(k.dve, lambda: nc.vector.tensor_tensor(out=h[0:nt, :], in0=h[0:nt, :], in1=gbc[0:nt, :], op=ALU.mult), reads=[h, gbc], writes=[h])
    k.op(k.pool, lambda: nc.gpsimd.tensor_tensor(out=h[0:nt, :], in0=h[0:nt, :], in1=bbc[0:nt, :], op=ALU.add), reads=[h, bbc], writes=[h])


def to_featT(P, pbanks, h, nt, idt, dstT_b, dst_scr, t0, stg_ring):
    stg = stg_ring.next()
    for g in range(8):
        pb = pbanks.next()
        for j in range(4):
            kc = g * 4 + j
            P.tr(pb, pb[:, j * 128:j * 128 + nt], h, h[0:nt, kc * 128:(kc + 1) * 128], idt, idt[0:nt, 0:nt])
        P.evac(stg, stg[:, g * 4:(g + 1) * 4, 0:nt], pb, pb[:, 0:512].rearrange("p (j t) -> p j t", j=4)[:, :, 0:nt])
    P.st(dst_scr, dst_scr.t[:, :, t0:t0 + nt].rearrange("c p t -> p c t"), stg, stg[:, :, 0:nt])


def phase_outproj(P, srcT, W_b, dst):
    with PhaseStack() as es:
        actT = load_actT(P, es, srcT)
        stg = P.sbring(es, "stgB", [128, 256], F32, 4)
        P.wsrc = W_b

        def emit(c0, cn, t0, tn, pb):
            s_ = stg.next()
            P.evac(s_, s_[0:tn, 0:cn], pb, pb[0:tn, 0:cn])
            P.st(dst, dst.t[t0:t0 + tn, c0:c0 + cn], s_, s_[0:tn, 0:cn])
        proj_pass(P, actT, W_b.t, D, "B", emit)


def phase_res_ln(P, xres_fn, mix, ln_row, dst_tok, dst_T, dst_bf=None, alpha=ALPHA):
    nc, k = P.nc, P.k
    with PhaseStack() as es:
        gbc = P.sb(es, "gbc", [128, D], F32); bbc = P.sb(es, "bbc", [128, D], F32)
        lg, lb = P.ins["ln_g"], P.ins["ln_b"]
        P.ld(gbc, gbc[:], lg, lg.t[ln_row:ln_row + 1, :].partition_broadcast(128))
        P.ld(bbc, bbc[:], lb, lb.t[ln_row:ln_row + 1, :].partition_broadcast(128))
        idt = P.sb(es, "idt", [128, 128], F32)
        P.ld(idt, idt[:], P.ins["ident"], P.ins["ident"].t)
        xr = P.sbring(es, "xr", [128, D], F32, 2)
        mr = P.sbring(es, "mr", [128, D], F32, 2)
        junk = P.sb(es, "junk", [128, D], F32); st = P.sb(es, "lnst", [128, 4], F32)
        stgT = P.sbring(es, "stgT", [128, KC, 128], BF16, 2)
        mbf = P.sbring(es, "mbf", [128, D], BF16, 2)
        pbanks = Ring(P.psum)
        for (t0, tn) in TOK_TILES:
            x = xr.next(); m = mr.next()
            xb_, xap = xres_fn(t0, tn)
            P.ld(x, x[0:tn, :], xb_, xap)
            P.ld(m, m[0:tn, :], mix, mix.t[t0:t0 + tn, :])
            k.op(k.dve, lambda: nc.vector.scalar_tensor_tensor(out=m[0:tn, :], in0=x[0:tn, :], scalar=alpha, in1=m[0:tn, :], op0=ALU.mult, op1=ALU.add),
                 reads=[x, m], writes=[m])
            ln_rows(P, (junk, st), m, tn, gbc, bbc)
            P.st(dst_tok, dst_tok.t[t0:t0 + tn, :], m, m[0:tn, :])
            if dst_bf is not None:
                mb = mbf.next()
                k.op(k.act, lambda: nc.scalar.copy(out=mb[0:tn, :], in_=m[0:tn, :]), reads=[m], writes=[mb])
                P.st(dst_bf, dst_bf.t[t0:t0 + tn, :], mb, mb[0:tn, :])
            if dst_T is not None:
                to_featT(P, pbanks, m, tn, idt, None, dst_T, t0, stgT)


def phase_tables_bf16(P, layer):
    k = P.k
    for nm, dst in (("peer_u", P.scr["UB%d" % layer]), ("peer_v", P.scr["VB%d" % layer])):
        src = P.ins[nm]
        for r0 in range(0, 16384, 512):
            def job(src=src, dst=dst, r0=r0):
                k.dma(k.pool, lambda: P.nc.gpsimd.dma_start(out=dst.t[r0:r0 + 512, :], in_=src.t[layer, r0:r0 + 512, :]),
                      reads=[src], writes=[dst], acc=True)
            P.bg.append(job)


def drain_bg(P, n=1):
    for _ in range(n):
        if P.bg:
            P.bg.pop(0)()


def phase_peer_q(P, layer, XT, QPT):
    with PhaseStack() as es:
        actT = load_actT(P, es, XT)
        stg = P.sbring(es, "stgA", [128, 512], BF16, 4)
        P.wsrc = P.ins["peer_w_q"]

        def emit(ci, t0, tn, pb):
            s_ = stg.next()
            P.evac(s_, s_[:, 0:tn], pb, pb[:, 0:tn])
            P.st(QPT, QPT.t[ci, :, t0:t0 + tn], s_, s_[:, 0:tn])
        proj_pass(P, actT, P.ins["peer_w_q"].t[layer], 2048, "A", emit)


def phase_peer_main(P, layer, XM, XMB, QPT, ln_row, dst_tok, dst_T):
    nc, k = P.nc, P.k
    UB, VB = P.scr["UB%d" % layer], P.scr["VB%d" % layer]
    drain_bg(P, len(P.bg))
    with PhaseStack() as es:
        gbc = P.sb(es, "gbc", [128, D], F32); bbc = P.sb(es, "bbc", [128, D], F32)
        lg, lb = P.ins["ln_g"], P.ins["ln_b"]
        P.ld(gbc, gbc[:], lg, lg.t[ln_row:ln_row + 1, :].partition_broadcast(128))
        P.ld(bbc, bbc[:], lb, lb.t[ln_row:ln_row + 1, :].partition_broadcast(128))
        idt = P.sb(es, "idt", [128, 128], F32)
        P.ld(idt, idt[:], P.ins["ident"], P.ins["ident"].t)
        io16 = P.sb(es, "io16", [128, 16], F32)
        P.ld(io16, io16[:], P.ins["iota16"], P.ins["iota16"].t)
        ohw = P.sb(es, "ohw", [128, 255], F32)
        P.ld(ohw, ohw[:], P.ins["ohw"], P.ins["ohw"].t)
        skT = P.sb(es, "skT", [128, 16, 128], BF16)
        skn = P.sb(es, "skn", [128, 16, 128], F32)
        sk = P.ins["peer_sub_keys"]
        P.ld(skn, skn[:], sk, sk.t[layer].rearrange("h c n d -> n (h c) d"))
        pbanks = Ring(P.psum)
        for g in range(4):
            pb = pbanks.next()
            for j in range(4):
                P.tr(pb, pb[:, j * 128:(j + 1) * 128], skn, skn[:, g * 4 + j, :], idt, idt[:])
            P.evac(skT, skT[:, g * 4:(g + 1) * 4, :], pb, pb[:, 0:512].rearrange("p (j t) -> p j t", j=4))
        qTr = P.sbring(es, "qpT", [128, 16, 128], BF16, 2)
        sc = P.sb(es, "sc", [128, 16, 128], F32)
        wk = P.sb(es, "wk", [128, 256], F32)
        sv = P.sb(es, "sv", [128, 16, 16], F32)
        si = P.sb(es, "si", [128, 16, 16], U32)
        sif = P.sb(es, "sif", [128, 16, 16], F32)
        cand = P.sb(es, "cand", [128, 8, 256], F32)
        big = P.sb(es, "big", [128, 8, 256], F32)
        cv = P.sb(es, "cv", [128, 8, 16], F32)
        ci = P.sb(es, "ci", [128, 8, 16], U32)
        cii = P.sb(es, "cii", [128, 8, 16], U32)
        cif = P.sb(es, "cif", [128, 2, 8, 16], F32)
        i12 = P.sb(es, "i12", [128, 2, 8, 16], F32)
        idxf = P.sb(es, "idxf", [128, 128], F32)
        idxT = P.sb(es, "idxT", [128, 128], I32)
        gate = P.sb(es, "gate", [128, 8, 16], F32)
        gsum = P.sb(es, "gsum", [128, 8], F32)
        hidT = P.sb(es, "hidT", [128, 128], F32)
        hid = P.sb(es, "hid", [128, 128], F32)
        coefT = P.sb(es, "coefT", [128, 128], F32)
        Ur = P.sbring(es, "Ug", [128, D], BF16, 3)
        Vr = P.sbring(es, "Vg", [128, D], BF16, 3)
        xmb = P.sb(es, "xmb", [128, D], BF16)
        onesb = P.sb(es, "onesb", [128, 128], BF16)
        k.op(k.dve, lambda: nc.vector.memset(onesb[:], 1.0), writes=[onesb])
        selr = P.sbring(es, "selr", [128, 128], BF16, 3)
        hid2 = P.sb(es, "hid2", [128, 2, 128], F32)
        Zr = P.sbring(es, "Zt", [128, 128], BF16, 3)
        junkb = P.sb(es, "junkb", [128, D], BF16)
        xm = P.sb(es, "xm", [128, D], F32)
        junk = junkb; st = P.sb(es, "lnst", [128, 4], F32)
        stgT = P.sbring(es, "stgT", [128, KC, 128], BF16, 1)

        def top16(src_b, src_ap, n, val_b, val_ap, idx_b, idx_ap):
            k.op(k.dve, lambda: nc.vector.max(out=val_ap[:, 0:8], in_=src_ap), reads=[src_b], writes=[val_b])
            k.op(k.dve, lambda: nc.vector.max_index(out=idx_ap[:, 0:8], in_max=val_ap[:, 0:8], in_values=src_ap), reads=[src_b, val_b], writes=[idx_b])
            k.op(k.dve, lambda: nc.vector.match_replace(out=wk[0:src_ap.shape[0], 0:n], in_to_replace=val_ap[:, 0:8], in_values=src_ap, imm_value=-1e30),
                 reads=[src_b, val_b], writes=[wk])
            wv_ = wk[0:src_ap.shape[0], 0:n]
            k.op(k.dve, lambda: nc.vector.max(out=val_ap[:, 8:16], in_=wv_), reads=[wk], writes=[val_b])
            k.op(k.dve, lambda: nc.vector.max_index(out=idx_ap[:, 8:16], in_max=val_ap[:, 8:16], in_values=wv_), reads=[wk, val_b], writes=[idx_b])

        for (t0, tn) in TOK_TILES:
            qT = qTr.next()
            P.ld(qT, qT[:, :, 0:tn], QPT, QPT.t[:, :, t0:t0 + tn].rearrange("c p t -> p c t"))
            P.ld(xm, xm[0:tn, :], XM, XM.t[t0:t0 + tn, :])
            for g in range(4):
                pb = pbanks.next()
                for j in range(4):
                    ch = g * 4 + j
                    P.mm(pb, pb[0:tn, j * 128:(j + 1) * 128], qT, qT[:, ch, 0:tn], skT, skT[:, ch, :], True, True)
                P.evac(sc, sc[0:tn, g * 4:(g + 1) * 4, :], pb, pb[0:tn, 0:512].rearrange("p (j n) -> p j n", j=4))
            for ch in range(16):
                top16(sc, sc[0:tn, ch, :], 128, sv, sv[0:tn, ch, :], si, si[0:tn, ch, :])
            k.op(k.dve, lambda: nc.vector.tensor_copy(out=sif[0:tn], in_=si[0:tn]), reads=[si], writes=[sif])
            svv = sv[0:tn].rearrange("p (h c) k -> p h c k", c=2)
            k.op(k.dve, lambda: nc.vector.tensor_tensor(out=cand[0:tn].rearrange("p h (i j) -> p h i j", i=16),
                                                          in0=svv[:, :, 0, :].unsqueeze(3).to_broadcast([tn, 8, 16, 16]),
                                                          in1=svv[:, :, 1, :].unsqueeze(2).to_broadcast([tn, 8, 16, 16]), op=ALU.add),
                 reads=[sv], writes=[cand])
            for h in range(8):
                top16(cand, cand[0:tn, h, :], 256, cv, cv[0:tn, h, :], ci, ci[0:tn, h, :])
            k.op(k.dve, lambda: nc.vector.tensor_single_scalar(out=cii[0:tn], in_=ci[0:tn], scalar=4, op=ALU.logical_shift_right), reads=[ci], writes=[cii])
            k.op(k.dve, lambda: nc.vector.tensor_copy(out=cif[0:tn, 0], in_=cii[0:tn]), reads=[cii], writes=[cif])
            k.op(k.dve, lambda: nc.vector.tensor_single_scalar(out=cii[0:tn], in_=ci[0:tn], scalar=15, op=ALU.bitwise_and), reads=[ci], writes=[cii])
            k.op(k.dve, lambda: nc.vector.tensor_copy(out=cif[0:tn, 1], in_=cii[0:tn]), reads=[cii], writes=[cif])
            sifv = sif[0:tn].rearrange("p (h c) k -> p h c k", c=2)
            for c in range(2):
                bv = big[0:tn].rearrange("p h (k i) -> p h k i", k=16)
                k.op(k.dve, lambda: nc.vector.tensor_tensor(out=bv, in0=cif[0:tn, c].unsqueeze(3).to_broadcast([tn, 8, 16, 16]),
                                                              in1=io16[0:tn].unsqueeze(1).unsqueeze(1).to_broadcast([tn, 8, 16, 16]), op=ALU.is_equal),
                     reads=[cif, io16], writes=[big])
                k.op(k.dve, lambda: nc.vector.tensor_tensor(out=bv, in0=bv, in1=sifv[:, :, c, :].unsqueeze(2).to_broadcast([tn, 8, 16, 16]), op=ALU.mult),
                     reads=[big, sif], writes=[big])
                k.op(k.dve, lambda: nc.vector.tensor_reduce(out=i12[0:tn, c], in_=bv, axis=AX.X, op=ALU.add), reads=[big], writes=[i12])
            k.op(k.dve, lambda: nc.vector.scalar_tensor_tensor(out=idxf[0:tn].rearrange("p (h k) -> p h k", h=8), in0=i12[0:tn, 0], scalar=128.0, in1=i12[0:tn, 1],
                                                                 op0=ALU.mult, op1=ALU.add), reads=[i12], writes=[idxf])
            k.op(k.dve, lambda: nc.vector.tensor_tensor(out=gate[0:tn], in0=cv[0:tn], in1=cv[0:tn, :, 0:1].to_broadcast([tn, 8, 16]), op=ALU.subtract), reads=[cv], writes=[gate])
            k.op(k.act, lambda: nc.scalar.activation(out=gate[0:tn], in_=gate[0:tn], func=AF.Exp), reads=[gate], writes=[gate])
            k.op(k.dve, lambda: nc.vector.tensor_reduce(out=gsum[0:tn], in_=gate[0:tn], axis=AX.X, op=ALU.add), reads=[gate], writes=[gsum])
            k.op(k.dve, lambda: nc.vector.reciprocal(out=gsum[0:tn], in_=gsum[0:tn]), reads=[gsum], writes=[gsum])
            k.op(k.dve, lambda: nc.vector.tensor_tensor(out=gate[0:tn], in0=gate[0:tn], in1=gsum[0:tn].unsqueeze(2).to_broadcast([tn, 8, 16]), op=ALU.mult), reads=[gate, gsum], writes=[gate])
            pb = pbanks.next()
            P.tr(pb, pb[:, 0:tn], idxf, idxf[0:tn, :], idt, idt[0:tn, 0:tn])
            k.op(k.dve, lambda: nc.vector.tensor_copy(out=idxT[:, 0:tn], in_=pb[:, 0:tn]), reads=[pb], writes=[idxT])
            k.op(k.dve, lambda: nc.vector.memset(hid2[:], 0.0), writes=[hid2])
            k.op(k.act, lambda: nc.scalar.copy(out=xmb[0:tn, :], in_=xm[0:tn, :]), reads=[xm], writes=[xmb])
            for t in range(tn):
                ug = Ur.next()
                k.dma(k.pool, lambda: nc.gpsimd.indirect_dma_start(out=ug[:, :], out_offset=None, in_=UB.t,
                                                                     in_offset=bass.IndirectOffsetOnAxis(ap=idxT[:, t:t + 1], axis=0)),
                      reads=[UB, idxT], writes=[ug])
                sel = selr.next()
                k.op(k.act, lambda: nc.scalar.activation(out=sel[0:tn, :], in_=onesb[0:tn, :], func=AF.Copy, scale=idt[0:tn, t:t + 1]),
                     reads=[onesb, idt], writes=[sel])
                for hh in range(2):
                    for j in range(4):
                        pbj = P.psum[hh * 4 + j]
                        P.mm(pbj, pbj[:, 0:512], sel, sel[0:tn, :], xmb, xmb[0:tn, (hh * 4 + j) * 512:(hh * 4 + j + 1) * 512], True, True)
                    pgrp = [P.psum[hh * 4 + j] for j in range(4)]
                    k.op(k.dve, lambda: nc.vector.scalar_tensor_tensor(out=junkb[:, hh * 2048:(hh + 1) * 2048], in0=ug[:, hh * 2048:(hh + 1) * 2048], scalar=1.0,
                                                                         in1=P.psum_all[:, hh * 4:(hh + 1) * 4, :].rearrange("p b c -> p (b c)"),
                                                                         op0=ALU.mult, op1=ALU.mult, accum_out=hid2[:, hh, t:t + 1]),
                         reads=[ug] + pgrp, writes=[junkb, hid2])
            k.op(k.dve, lambda: nc.vector.tensor_tensor(out=hidT[:], in0=hid2[:, 0, :], in1=hid2[:, 1, :], op=ALU.add), reads=[hid2], writes=[hidT])
            pb = pbanks.next()
            P.tr(pb, pb[0:tn, 0:128], hidT, hidT[:, 0:tn], idt, idt[:])
            k.op(k.act, lambda: nc.scalar.activation(out=hid[0:tn], in_=pb[0:tn, 0:128], func=AF.Gelu), reads=[pb], writes=[hid])
            k.op(k.dve, lambda: nc.vector.tensor_tensor(out=hid[0:tn], in0=hid[0:tn], in1=gate[0:tn].rearrange("p h k -> p (h k)"), op=ALU.mult), reads=[hid, gate], writes=[hid])
            pb = pbanks.next()
            P.tr(pb, pb[:, 0:tn], hid, hid[0:tn, :], idt, idt[0:tn, 0:tn])
            k.op(k.dve, lambda: nc.vector.tensor_copy(out=coefT[:, 0:tn], in_=pb[:, 0:tn]), reads=[pb], writes=[coefT])
            for t in range(tn):
                vg = Vr.next(); zt = Zr.next()
                k.dma(k.pool, lambda: nc.gpsimd.indirect_dma_start(out=vg[:, :], out_offset=None, in_=VB.t,
                                                                     in_offset=bass.IndirectOffsetOnAxis(ap=idxT[:, t:t + 1], axis=0)),
                      reads=[VB, idxT], writes=[vg])
                k.op(k.dve, lambda: nc.vector.tensor_scalar(out=zt[:], in0=ohw[:, 127 - t:255 - t], scalar1=coefT[:, t:t + 1], scalar2=None, op0=ALU.mult),
                     reads=[ohw, coefT], writes=[zt])
                for j in range(8):
                    pbj = P.psum[j]
                    P.mm(pbj, pbj[:, 0:512], zt, zt[:], vg, vg[:, j * 512:(j + 1) * 512], t == 0, t == tn - 1)
            for j in range(8):
                pbj = P.psum[j]
                k.op(k.dve, lambda: nc.vector.scalar_tensor_tensor(out=xm[0:tn, j * 512:(j + 1) * 512], in0=xm[0:tn, j * 512:(j + 1) * 512], scalar=ALPHA,
                                                                     in1=pbj[0:tn, 0:512], op0=ALU.mult, op1=ALU.add), reads=[xm, pbj], writes=[xm])
            ln_rows(P, (junk, st), xm, tn, gbc, bbc)
            if callable(dst_tok):
                db, dap = dst_tok(t0, tn)
                P.st(db, dap, xm, xm[0:tn, :])
            else:
                P.st(dst_tok, dst_tok.t[t0:t0 + tn, :], xm, xm[0:tn, :])
            if dst_T is not None:
                to_featT(P, pbanks, xm, tn, idt, None, dst_T, t0, stgT)


def phase_sample_caches(P):
    k, nc = P.k, P.nc
    PS = P.scr["PS"]
    specs = [("ca_k", "a_k_s", 128, 2560, 512), ("ca_v", "a_v_s", 128, 3072, 512),
             ("cb_k", "b_k_s", 2048, 5120, 1536), ("cb_v", "b_v_s", 2048, 6656, 1536)]
    for cin, cout, L, c0, w in specs:
        ci, co = P.ins[cin], P.outs[cout]
        for b in range(4):
            k.dma(k.sp, lambda: nc.sync.dma_start(out=co.t[b, 0:L - 4, :], in_=ci.t[b, 4:L, :]), reads=[ci], writes=[co], acc=True)
            k.dma(k.sp, lambda: nc.sync.dma_start(out=co.t[b, L - 4:L, :], in_=PS.t[4 * b:4 * b + 4, c0:c0 + w]), reads=[PS], writes=[co], acc=True)


def phase_sample_attn(P):
    nc, k = P.nc, P.k
    PS, GGD, OT = P.scr["PS"], P.scr["GGD"], P.scr["OT"]
    with PhaseStack() as es:
        idt = P.sb(es, "idt", [128, 128], F32)
        P.ld(idt, idt[:], P.ins["ident"], P.ins["ident"].t)
        ohs = P.sb(es, "ohs", [128, 31], F32)
        P.ld(ohs, ohs[:], P.ins["ohs"], P.ins["ohs"].t)
        eb = P.sb(es, "ebS", [128, 3, 32], F32)
        for di in range(3):
            src = bass.AP(tensor=GGD.t.tensor, offset=di * 32 * 384 + 127, ap=[[1, 128], [384, 32]])
            k.dma(k.sp, lambda: nc.sync.dma_start(out=eb[:, di, :], in_=src, allow_slow_non_contiguous=True), reads=[GGD], writes=[eb])
        es0 = P.sb(es, "es0", [16, 32], F32)
        src = bass.AP(tensor=GGD.t.tensor, offset=255, ap=[[0, 16], [384, 32]])
        k.dma(k.sp, lambda: nc.sync.dma_start(out=es0[:], in_=src, allow_slow_non_contiguous=True), reads=[GGD], writes=[es0])
        esk = P.sb(es, "eskS", [16, 20], F32)
        sinks = P.ins["att_sinks"]
        P.ld(esk, esk[:], sinks, sinks.t.partition_broadcast(16))
        k.op(k.act, lambda: nc.scalar.activation(out=esk[:], in_=esk[:], func=AF.Exp), reads=[esk], writes=[esk])
        pst = P.sb(es, "pst", [16, 8192], F32)
        P.ld(pst, pst[:], PS, PS.t)
        qbc = P.sbring(es, "qbcS", [128, 2560], F32, 2)
        Kt = P.sbring(es, "KtS", [128, 1536], F32, 2)
        Vt = P.sbring(es, "VtS", [128, 1536], F32, 2)
        prod = P.sb(es, "prodS", [128, 2560], F32)
        Sx = P.sbring(es, "SxS", [128, 20], F32, 2)
        osb = P.sb(es, "osb", [16, D], F32)
        zsb = P.sb(es, "zsb", [16, 32], F32)
        pself = P.sb(es, "pself", [16, 32], F32)
        ptmp = P.sb(es, "ptmp", [16, 2560], F32)
        stgT = P.sbring(es, "stgTS", [128, KC, 128], BF16, 1)

        def variant(mix, di, d, b, sidx, first, last):
            bs = 4 * b + sidx
            H, rep, nkv = (20, 5, 4) if mix == "A" else (12, 1, 12)
            w = nkv * 128
            qc0, kc0, vc0 = (0, 2560, 3072) if mix == "A" else (3584, 5120, 6656)
            ck, cv = (P.ins["ca_k"], P.ins["ca_v"]) if mix == "A" else (P.ins["cb_k"], P.ins["cb_v"])
            L = 128 if mix == "A" else 2048
            q = qbc.next()
            P.ld(q, q[:, 0:H * 128], PS, PS.t[bs:bs + 1, qc0:qc0 + H * 128].partition_broadcast(128))
            kt, vt = Kt.next(), Vt.next()
            base = L + sidx - 128 * d
            ncache = 128 - sidx if d == 1 else 128
            for (tile, cin, c0) in ((kt, ck, kc0), (vt, cv, vc0)):
                src = cin.t[b, base:base + d * (ncache - 1) + 1:d, :] if d > 1 else cin.t[b, base:base + ncache, :]
                P.ld(tile, tile[0:ncache, 0:w], cin, src)
                if ncache < 128:
                    P.ld(tile, tile[ncache:128, 0:w], PS, PS.t[4 * b:4 * b + sidx, c0:c0 + w])
            pv = prod[:, 0:H * 128].rearrange("p (g r c) -> p g r c", g=nkv, r=rep)
            kv = kt[:, 0:w].rearrange("p (g c) -> p g c", g=nkv).unsqueeze(2).to_broadcast([128, nkv, rep, 128])
            qv = q[:, 0:H * 128].rearrange("p (g r c) -> p g r c", g=nkv, r=rep)
            k.op(k.dve, lambda: nc.vector.tensor_tensor(out=pv, in0=kv, in1=qv, op=ALU.mult), reads=[kt, q], writes=[prod])
            sx = Sx.next()
            k.op(k.dve, lambda: nc.vector.tensor_reduce(out=sx[:, 0:H], in_=prod[:, 0:H * 128].rearrange("p (h c) -> p h c", h=H), axis=AX.X, op=ALU.add),
                 reads=[prod], writes=[sx])
            k.op(k.act, lambda: nc.scalar.activation(out=sx[:, 0:H], in_=sx[:, 0:H], func=AF.Exp, scale=SCALE), reads=[sx], writes=[sx])
            h0 = 0 if mix == "A" else 20
            k.op(k.dve, lambda: nc.vector.tensor_tensor(out=sx[:, 0:H], in0=sx[:, 0:H], in1=eb[:, di, h0:h0 + H], op=ALU.mult), reads=[sx, eb], writes=[sx])
            vv = vt[:, 0:w].rearrange("p (g c) -> p g c", g=nkv).unsqueeze(2).to_broadcast([128, nkv, rep, 128])
            sv_ = sx[:, 0:H].rearrange("p (g r) -> p g r", g=nkv).unsqueeze(3).to_broadcast([128, nkv, rep, 128])
            k.op(k.dve, lambda: nc.vector.tensor_tensor(out=pv, in0=vv, in1=sv_, op=ALU.mult), reads=[vt, sx], writes=[prod])
            lh = ohs[:, 15 - bs:31 - bs]
            nb = (H * 128) // 512
            for j in range(nb):
                pb = P.psum[j]
                P.mm(pb, pb[0:16, 0:512], ohs, lh, prod, prod[:, j * 512:(j + 1) * 512], first, last)
            pb = P.psum[7]
            P.mm(pb, pb[0:16, 0:H], ohs, lh, sx, sx[:, 0:H], first, last)

        def self_and_norm(mix, mult):
            H, rep, nkv = (20, 5, 4) if mix == "A" else (12, 1, 12)
            qc0, kc0, vc0 = (0, 2560, 3072) if mix == "A" else (3584, 5120, 6656)
            h0 = 0 if mix == "A" else 20
            oc0 = 0 if mix == "A" else 2560
            pt = ptmp[:, 0:H * 128].rearrange("p (g r c) -> p g r c", g=nkv, r=rep)
            qv = pst[:, qc0:qc0 + H * 128].rearrange("p (g r c) -> p g r c", g=nkv, r=rep)
            kv = pst[:, kc0:kc0 + nkv * 128].rearrange("p (g c) -> p g c", g=nkv).unsqueeze(2).to_broadcast([16, nkv, rep, 128])
            vv = pst[:, vc0:vc0 + nkv * 128].rearrange("p (g c) -> p g c", g=nkv).unsqueeze(2).to_broadcast([16, nkv, rep, 128])
            k.op(k.dve, lambda: nc.vector.tensor_tensor(out=pt, in0=kv, in1=qv, op=ALU.mult), reads=[pst], writes=[ptmp])
            ps_ = pself[:, h0:h0 + H]
            k.op(k.dve, lambda: nc.vector.tensor_reduce(out=ps_, in_=ptmp[:, 0:H * 128].rearrange("p (h c) -> p h c", h=H), axis=AX.X, op=ALU.add), reads=[ptmp], writes=[pself])
            k.op(k.act, lambda: nc.scalar.activation(out=ps_, in_=ps_, func=AF.Exp, scale=SCALE), reads=[pself], writes=[pself])
            k.op(k.dve, lambda: nc.vector.scalar_tensor_tensor(out=ps_, in0=ps_, scalar=float(mult), in1=es0[:, h0:h0 + H], op0=ALU.mult, op1=ALU.mult), reads=[pself, es0], writes=[pself])
            zt = zsb[:, h0:h0 + H]
            pb7 = P.psum[7]
            k.op(k.dve, lambda: nc.vector.tensor_tensor(out=zt, in0=pb7[0:16, 0:H], in1=ps_, op=ALU.add), reads=[pb7, pself], writes=[zsb])
            if mix == "A":
                k.op(k.dve, lambda: nc.vector.tensor_tensor(out=zt, in0=zt, in1=esk[:, 0:20], op=ALU.add), reads=[zsb, esk], writes=[zsb])
            k.op(k.dve, lambda: nc.vector.reciprocal(out=zt, in_=zt), reads=[zsb], writes=[zsb])
            k.op(k.dve, lambda: nc.vector.tensor_tensor(out=pt, in0=vv, in1=ps_.rearrange("p (g r) -> p g r", g=nkv).unsqueeze(3).to_broadcast([16, nkv, rep, 128]), op=ALU.mult),
                 reads=[pst, pself], writes=[ptmp])
            for j in range((H * 128) // 512):
                pb = P.psum[j]
                k.op(k.dve, lambda: nc.vector.tensor_tensor(out=ptmp[:, j * 512:(j + 1) * 512], in0=ptmp[:, j * 512:(j + 1) * 512], in1=pb[0:16, 0:512], op=ALU.add),
                     reads=[ptmp, pb], writes=[ptmp])
            k.op(k.dve, lambda: nc.vector.tensor_tensor(out=osb[:, oc0:oc0 + H * 128].rearrange("p (h c) -> p h c", h=H),
                                                          in0=ptmp[:, 0:H * 128].rearrange("p (h c) -> p h c", h=H),
                                                          in1=zt.unsqueeze(2).to_broadcast([16, H, 128]), op=ALU.mult), reads=[ptmp, zsb], writes=[osb])

        for b in range(4):
            for sidx in range(4):
                variant("A", 0, 1, b, sidx, b == 0 and sidx == 0, b == 3 and sidx == 3)
        self_and_norm("A", 1)
        n = 0
        for b in range(4):
            for sidx in range(4):
                for di, d in enumerate(DILS):
                    variant("B", di, d, b, sidx, n == 0, n == 47)
                    n += 1
        self_and_norm("B", 3)
        to_featT(P, Ring(P.psum), osb, NS, idt, None, OT, SEQ, stgT)


EXPM05 = float(np.exp(-0.5))


def phase_rwkv_prep(P):
    nc, k = P.nc, P.k
    X1T = P.scr["X1T"]
    RC, WC, AC, BC, KFC, VC, GC, A1C = (P.scr[n] for n in ("RC", "WC", "AC", "BC", "KFC", "VC", "GC", "A1C"))
    with PhaseStack() as es:
        idt = P.sb(es, "idt", [128, 128], F32)
        P.ld(idt, idt[:], P.ins["ident"], P.ins["ident"].t)
        bones = P.sb(es, "bones", [128, 128], F32)
        P.ld(bones, bones[:], P.ins["bones"], P.ins["bones"].t)
        PF = P.sb(es, "PF", [128, KC, 10], F32)
        OMM = P.sb(es, "OMM", [128, KC, 6], F32)
        shT = P.sb(es, "shT", [128, KC, 4], BF16)
        es_setup = PhaseStack()
        es_setup.__enter__()
        prm = P.sb(es_setup, "prm", [10, D], F32)
        P.ld(prm, prm[0:6, :], P.ins["rw_mu"], P.ins["rw_mu"].t)
        for i, nm in enumerate(("rw_w0", "rw_a0", "rw_k_k", "rw_k_a")):
            P.ld(prm, prm[6 + i:7 + i, :], P.ins[nm], P.ins[nm].t)
        pb = P.psum[0]
        for kc in range(KC):
            P.tr(pb, pb[:, kc * 10:kc * 10 + 10], prm, prm[0:10, kc * 128:(kc + 1) * 128], idt, idt[0:10, 0:10])
        k.op(k.dve, lambda: nc.vector.tensor_copy(out=PF[:], in_=pb[:, 0:320].rearrange("p (c i) -> p c i", i=10)), reads=[pb], writes=[PF])
        k.op(k.dve, lambda: nc.vector.tensor_scalar(out=OMM[:], in0=PF[:, :, 0:6], scalar1=-1.0, scalar2=1.0, op0=ALU.mult, op1=ALU.add), reads=[PF], writes=[OMM])
        shs = P.sb(es_setup, "shs", [4, D], F32)
        P.ld(shs, shs[:], P.ins["st_shift"], P.ins["st_shift"].t)
        pb = P.psum[1]
        for kc in range(KC):
            P.tr(pb, pb[:, kc * 4:kc * 4 + 4], shs, shs[0:4, kc * 128:(kc + 1) * 128], idt, idt[0:4, 0:4])
        k.op(k.dve, lambda: nc.vector.tensor_copy(out=shT[:], in_=pb[:, 0:128].rearrange("p (c i) -> p c i", i=4)), reads=[pb], writes=[shT])
        es_setup.__exit__(None, None, None)

        NH = 1024
        xh = P.sb(es, "xh", [128, KC, 1 + NH + NS], BF16)
        pvs = P.sb(es, "pvs", [128, KC, NS], BF16)
        xi = P.sb(es, "xi", [128, KC, NH + NS], BF16)
        hw = P.sb(es, "hw", [128, NH + NS], BF16)
        ha = P.sb(es, "ha", [128, NH + NS], BF16)
        hg = P.sb(es, "hg", [128, 4, NH + NS], BF16)
        stg = P.sbring(es, "stgR", [128, 512], F32, 4)
        ew = P.sbring(es, "ewR", [128, 512], F32, 6)

        for hf in range(2):
            c0 = hf * NH
            nsamp = NS if hf == 1 else 0
            ncol = NH + nsamp
            chunks = [(0, 512), (512, 512)] + ([(NH, NS)] if hf == 1 else [])
            gcol = lambda t0: (c0 + t0) if t0 < NH else SEQ
            if hf == 0:
                k.op(k.dve, lambda: nc.vector.memset(xh[:, :, 0:1], 0.0), writes=[xh])
                for q0 in range(0, KC, 8):
                    P.ld(xh, xh[:, q0:q0 + 8, 1:1 + NH], X1T, X1T.t[q0:q0 + 8, :, 0:NH].rearrange("c p t -> p c t"))
            else:
                for q0 in range(0, KC, 8):
                    P.ld(xh, xh[:, q0:q0 + 8, 0:1 + NH], X1T, X1T.t[q0:q0 + 8, :, NH - 1:2 * NH].rearrange("c p t -> p c t"))
                    P.ld(xh, xh[:, q0:q0 + 8, 1 + NH:1 + NH + NS], X1T, X1T.t[q0:q0 + 8, :, SEQ:SEQ + NS].rearrange("c p t -> p c t"))
                xs4 = xh[:, :, 1 + NH:1 + NH + NS].rearrange("p c (b s) -> p c b s", s=4)
                pv4 = pvs[:].rearrange("p c (b s) -> p c b s", s=4)
                k.op(k.dve, lambda: nc.vector.tensor_copy(out=pv4[:, :, :, 1:4], in_=xs4[:, :, :, 0:3]), reads=[xh], writes=[pvs])
                k.op(k.dve, lambda: nc.vector.tensor_copy(out=pv4[:, :, :, 0], in_=shT[:]), reads=[shT], writes=[pvs])

            def mix(i):
                for kc in range(KC):
                    eng = k.dve
                    e = nc.vector
                    k.op(eng, lambda: e.tensor_scalar(out=xi[:, kc, 0:NH], in0=xh[:, kc, 1:1 + NH], scalar1=OMM[:, kc, i:i + 1], scalar2=None, op0=ALU.mult),
                         reads=[xh, OMM], writes=[xi])
                    k.op(eng, lambda: e.scalar_tensor_tensor(out=xi[:, kc, 0:NH], in0=xh[:, kc, 0:NH], scalar=PF[:, kc, i:i + 1], in1=xi[:, kc, 0:NH], op0=ALU.mult, op1=ALU.add),
                         reads=[xh, PF, xi], writes=[xi])
                    if nsamp:
                        k.op(eng, lambda: e.tensor_scalar(out=xi[:, kc, NH:NH + NS], in0=xh[:, kc, 1 + NH:1 + NH + NS], scalar1=OMM[:, kc, i:i + 1], scalar2=None, op0=ALU.mult),
                             reads=[xh, OMM], writes=[xi])
                        k.op(eng, lambda: e.scalar_tensor_tensor(out=xi[:, kc, NH:NH + NS], in0=pvs[:, kc, :], scalar=PF[:, kc, i:i + 1], in1=xi[:, kc, NH:NH + NS], op0=ALU.mult, op1=ALU.add),
                             reads=[pvs, PF, xi], writes=[xi])

            mix(1)
            P.wsrc = P.ins["rw_w1"]
            proj_pass(P, xi, P.ins["rw_w1"].t, 128, "A", lambda ci, t0, tn, pb: k.op(
                k.act, lambda: nc.scalar.activation(out=hw[:, t0:t0 + tn], in_=pb[:, 0:tn], func=AF.Tanh), reads=[pb], writes=[hw]), chunks=chunks, blk=128)
            mix(4)
            P.wsrc = P.ins["rw_a1"]
            proj_pass(P, xi, P.ins["rw_a1"].t, 128, "A", lambda ci, t0, tn, pb: k.op(
                k.act, lambda: nc.scalar.copy(out=ha[:, t0:t0 + tn], in_=pb[:, 0:tn]), reads=[pb], writes=[ha]), chunks=chunks, blk=128)
            mix(5)
            P.wsrc = P.ins["rw_g1"]
            for ci in range(4):
                cw = 128 if ci < 3 else 96
                proj_pass(P, xi, P.ins["rw_g1"].t[:, ci * 128:ci * 128 + cw], cw, "A", lambda _c, t0, tn, pb, ci=ci, cw=cw: k.op(
                    k.act, lambda: nc.scalar.activation(out=hg[0:cw, ci, t0:t0 + tn], in_=pb[0:cw, 0:tn], func=AF.Sigmoid), reads=[pb], writes=[hg]), chunks=chunks, blk=cw)

            def second(Wb, hsrc, hview, kparts, emit2):
                P.wsrc = Wb
                with PhaseStack() as es2:
                    wr = P.sbring(es2, "w2blk", [128, 4, 256], BF16, 2)
                    pbk = Ring(P.psum)
                    for cb in range(0, D, 256):
                        wb = wr.next()
                        for ki, (r0, rn) in enumerate(kparts):
                            P.ld(wb, wb[0:rn, ki, :], Wb, Wb.t[r0:r0 + rn, cb:cb + 256], eng=k.pool)
                        for h in range(2):
                            gch = cb // 128 + h
                            for (t0, tn) in chunks:
                                pb = pbk.next()
                                for ki, (r0, rn) in enumerate(kparts):
                                    P.mm(pb, pb[:, 0:tn], wb, wb[0:rn, ki, h * 128:(h + 1) * 128], hsrc, hview(ki, rn, t0, tn), ki == 0, ki == len(kparts) - 1)
                                emit2(gch, t0, tn, pb)

            def emit_a(gch, t0, tn, pb):
                s_ = stg.next()
                k.op(k.act, lambda: nc.scalar.activation(out=s_[:, 0:tn], in_=pb[:, 0:tn], func=AF.Sigmoid, bias=PF[:, gch, 7:8]), reads=[pb, PF], writes=[s_])
                P.st(A1C, A1C.t[gch * 128:(gch + 1) * 128, gcol(t0):gcol(t0) + tn], s_, s_[:, 0:tn])

            def emit_w(gch, t0, tn, pb):
                s_ = stg.next()
                k.op(k.act, lambda: nc.scalar.activation(out=s_[:, 0:tn], in_=pb[:, 0:tn], func=AF.Sigmoid, bias=PF[:, gch, 6:7]), reads=[pb, PF], writes=[s_])
                k.op(k.act, lambda: nc.scalar.activation(out=s_[:, 0:tn], in_=s_[:, 0:tn], func=AF.Exp, scale=-EXPM05), reads=[s_], writes=[s_])
                P.st(WC, WC.t[gch * 128:(gch + 1) * 128, gcol(t0):gcol(t0) + tn], s_, s_[:, 0:tn])

            def emit_g(gch, t0, tn, pb):
                s_ = stg.next()
                P.evac(s_, s_[:, 0:tn], pb, pb[:, 0:tn])
                P.st(GC, GC.t[gch * 128:(gch + 1) * 128, gcol(t0):gcol(t0) + tn], s_, s_[:, 0:tn])

            second(P.ins["rw_a2"], ha, lambda ki, rn, t0, tn: ha[0:rn, t0:t0 + tn], [(0, 128)], emit_a)
            second(P.ins["rw_w2"], hw, lambda ki, rn, t0, tn: hw[0:rn, t0:t0 + tn], [(0, 128)], emit_w)
            second(P.ins["rw_g2"], hg, lambda ki, rn, t0, tn: hg[0:rn, ki, t0:t0 + tn], [(0, 128), (128, 128), (256, 128), (384, 96)], emit_g)

            def emit_plain(dst):
                def f(ci, t0, tn, pb):
                    s_ = stg.next()
                    P.evac(s_, s_[:, 0:tn], pb, pb[:, 0:tn])
                    P.st(dst, dst.t[ci * 128:(ci + 1) * 128, gcol(t0):gcol(t0) + tn], s_, s_[:, 0:tn])
                return f
            P.wsrc = P.ins["rw_w_rkv"]
            mix(0)
            proj_pass(P, xi, P.ins["rw_w_rkv"].t[0], D, "A", emit_plain(RC), chunks=chunks, blk=128)
            mix(3)
            proj_pass(P, xi, P.ins["rw_w_rkv"].t[2], D, "A", emit_plain(VC), chunks=chunks, blk=128)
            mix(2)

            def emit_k(gch, t0, tn, pb):
                kraw, a_t, kkn, sq, t1 = ew.next(), ew.next(), ew.next(), ew.next(), ew.next()
                gc_ = gcol(t0)
                P.evac(kraw, kraw[:, 0:tn], pb, pb[:, 0:tn])
                P.ld(a_t, a_t[:, 0:tn], A1C, A1C.t[gch * 128:(gch + 1) * 128, gc_:gc_ + tn])
                k.op(k.dve, lambda: nc.vector.tensor_scalar(out=kkn[:, 0:tn], in0=kraw[:, 0:tn], scalar1=PF[:, gch, 8:9], scalar2=None, op0=ALU.mult), reads=[kraw, PF], writes=[kkn])
                k.op(k.pool, lambda: nc.gpsimd.tensor_tensor(out=sq[:, 0:tn], in0=kkn[:, 0:tn], in1=kkn[:, 0:tn], op=ALU.mult), reads=[kkn], writes=[sq])
                pb2 = P.psum[7]
                P.mm(pb2, pb2[:, 0:tn], bones, bones[:], sq, sq[:, 0:tn], True, True)
                k.op(k.act, lambda: nc.scalar.activation(out=sq[:, 0:tn], in_=pb2[:, 0:tn], func=AF.Sqrt), reads=[pb2], writes=[sq])
                k.op(k.dve, lambda: nc.vector.tensor_scalar_max(out=sq[:, 0:tn], in0=sq[:, 0:tn], scalar1=1e-12), reads=[sq], writes=[sq])
                k.op(k.dve, lambda: nc.vector.reciprocal(out=sq[:, 0:tn], in_=sq[:, 0:tn]), reads=[sq], writes=[sq])
                k.op(k.dve, lambda: nc.vector.tensor_tensor(out=kkn[:, 0:tn], in0=kkn[:, 0:tn], in1=sq[:, 0:tn], op=ALU.mult), reads=[kkn, sq], writes=[kkn])
                k.op(k.pool, lambda: nc.gpsimd.tensor_scalar(out=sq[:, 0:tn], in0=kkn[:, 0:tn], scalar1=-1.0, scalar2=None, op0=ALU.mult), reads=[kkn], writes=[sq])
                P.st(AC, AC.t[gch * 128:(gch + 1) * 128, gc_:gc_ + tn], sq, sq[:, 0:tn])
                k.op(k.dve, lambda: nc.vector.tensor_tensor(out=kkn[:, 0:tn], in0=kkn[:, 0:tn], in1=a_t[:, 0:tn], op=ALU.mult), reads=[kkn, a_t], writes=[kkn])
                P.st(BC, BC.t[gch * 128:(gch + 1) * 128, gc_:gc_ + tn], kkn, kkn[:, 0:tn])
                k.op(k.dve, lambda: nc.vector.tensor_scalar(out=t1[:, 0:tn], in0=a_t[:, 0:tn], scalar1=-1.0, scalar2=PF[:, gch, 9:10], op0=ALU.add, op1=ALU.mult), reads=[a_t, PF], writes=[t1])
                k.op(k.dve, lambda: nc.vector.scalar_tensor_tensor(out=t1[:, 0:tn], in0=t1[:, 0:tn], scalar=1.0, in1=kraw[:, 0:tn], op0=ALU.add, op1=ALU.mult), reads=[t1, kraw], writes=[t1])
                P.st(KFC, KFC.t[gch * 128:(gch + 1) * 128, gc_:gc_ + tn], t1, t1[:, 0:tn])
            proj_pass(P, xi, P.ins["rw_w_rkv"].t[1], D, "A", emit_k, chunks=chunks, blk=128)


GN_EPS = 64e-5
TCH = 32


def phase_rwkv_scan(P):
    nc, k = P.nc, P.k
    RC, WC, AC, BC, KFC, VC, YC = (P.scr[n] for n in ("RC", "WC", "AC", "BC", "KFC", "VC", "YC"))
    with PhaseStack() as es:
        idt = P.sb(es, "idt", [128, 128], F32)
        P.ld(idt, idt[:], P.ins["ident"], P.ins["ident"].t)
        maskp = P.sb(es, "maskp", [128, 2], F32); P.ld(maskp, maskp[:], P.ins["maskp"], P.ins["maskp"].t)
        maskh = P.sb(es, "maskh", [128, 2], F32); P.ld(maskh, maskh[:], P.ins["maskh"], P.ins["maskh"].t)
        maskg = P.sb(es, "maskg", [128, 32], F32); P.ld(maskg, maskg[:], P.ins["maskg"], P.ins["maskg"].t)
        rk = P.sb(es, "rk", [64, 64], F32); P.ld(rk, rk[:], P.ins["rw_r_k"], P.ins["rw_r_k"].t)
        gng = P.sb(es, "gng", [64, 64], F32); P.ld(gng, gng[:], P.ins["rw_gn_g"], P.ins["rw_gn_g"].t.rearrange("o (r v) -> (o r) v", v=64))
        gnb = P.sb(es, "gnb", [64, 64], F32); P.ld(gnb, gnb[:], P.ins["rw_gn_b"], P.ins["rw_gn_b"].t.rearrange("o (r v) -> (o r) v", v=64))
        MM = [P.sb(es, "MA", [128, 2048], F32), P.sb(es, "MB", [128, 2048], F32)]
        Mw = P.sb(es, "Mw", [128, 2048], F32)
        RVr = P.sbring(es, "RV", [128, 2048], BF16, 2)
        ytmp = P.sb(es, "ytmp", [64, 2048], BF16)
        maskgb = P.sb(es, "maskgb", [128, 32], BF16)
        k.op(k.dve, lambda: nc.vector.tensor_copy(out=maskgb[:], in_=maskg[:]), reads=[maskg], writes=[maskgb])
        aF = P.sb(es, "aF", [128, 32, TCH], F32); wF = P.sb(es, "wF", [128, 32, TCH], F32); rF = P.sb(es, "rF", [128, 32, TCH], F32)
        LU = P.sb(es, "LU", [128, TCH, 64], F32); LRb = P.sb(es, "LRb", [128, TCH, 64], BF16)
        Mb = P.sb(es, "Mb", [128, 2048], BF16)
        PU = Buf("psU", P.psum_all[:, 0:4, :]); PO = Buf("psO", P.psum_all[:, 4:8, :])
        LBK = P.sb(es, "LBK", [128, TCH, 128], BF16)
        BK = P.sb(es, "BK", [128, 64, TCH], F32); VH = P.sb(es, "VH", [128, 64, TCH], F32)
        RH = P.sb(es, "RH", [64, 64, TCH], F32); KH = P.sb(es, "KH", [64, 64, TCH], F32)
        yacc = P.sb(es, "yacc", [64, 64, TCH], F32)
        t1 = P.sb(es, "gt1", [64, 64, TCH], F32); t2 = P.sb(es, "gt2", [64, 64, TCH], F32)
        st = P.sb(es, "gst", [64, 6, TCH], F32)
        Sio = P.sb(es, "Sio", [64, 32, 128], F32)
        cur = [0]

        def hm(src, c, n):
            return src.t[:, c:c + n].rearrange("(r q) t -> r q t", q=64)

        def load_state(b):
            sw = P.ins["st_wkv"]
            P.ld(Sio, Sio[:].rearrange("v g (h q) -> v g h q", h=2), sw, sw.t[b].rearrange("(g h) v q -> v g h q", h=2))
            M = MM[cur[0]]
            for g4 in range(8):
                pb = P.psum[g4]
                for j in range(4):
                    g = g4 * 4 + j
                    P.tr(pb, pb[:, j * 64:(j + 1) * 64], Sio, Sio[:, g, :], idt, idt[0:64, 0:64])
                P.evac(M, M[:, g4 * 256:(g4 + 1) * 256], pb, pb[:, 0:256])
            k.barrier()

        def store_state(dst_b, dst_ap):
            k.barrier()
            M = MM[cur[0]]
            for g4 in range(8):
                pb = P.psum[g4]
                for j in range(4):
                    g = g4 * 4 + j
                    P.tr(pb, pb[0:64, j * 128:(j + 1) * 128], M, M[:, g * 64:(g + 1) * 64], idt, idt[:])
                P.evac(Sio, Sio[:, g4 * 4:(g4 + 1) * 4, :], pb, pb[0:64, 0:512].rearrange("v (j q) -> v j q", j=4))
            k.dma(k.sp, lambda: nc.sync.dma_start(out=dst_ap.rearrange("(g h) v q -> v g h q", h=2), in_=Sio[:].rearrange("v g (h q) -> v g h q", h=2)),
                  reads=[Sio], writes=[dst_b], acc=True)
            k.barrier()

        def chunk(c, n):
            for (tile, src) in ((aF, AC), (wF, WC), (rF, RC)):
                P.ld(tile, tile[:, :, 0:n], src, src.t[:, c:c + n].rearrange("(g p) t -> p g t", p=128))
            P.ld(BK, BK[0:64, :, 0:n], BC, hm(BC, c, n))
            P.ld(BK, BK[64:128, :, 0:n], KFC, hm(KFC, c, n))
            P.ld(VH, VH[0:64, :, 0:n], VC, hm(VC, c, n))
            P.ld(VH, VH[64:128, :, 0:n], VC, hm(VC, c, n))
            P.ld(RH, RH[:, :, 0:n], RC, hm(RC, c, n))
            P.ld(KH, KH[:, :, 0:n], KFC, hm(KFC, c, n))
            for (L_, F_) in ((LU, aF), (LRb, rF)):
                k.op(k.pool, lambda: nc.gpsimd.tensor_tensor(out=L_[:, 0:n, :].rearrange("p t (g h) -> p t g h", h=2),
                                                             in0=F_[:, :, 0:n].rearrange("p g t -> p t g").unsqueeze(3).to_broadcast([128, n, 32, 2]),
                                                             in1=maskp[:].unsqueeze(1).unsqueeze(1).to_broadcast([128, n, 32, 2]), op=ALU.mult),
                     reads=[F_, maskp], writes=[L_])
            k.op(k.pool, lambda: nc.gpsimd.tensor_tensor(out=LBK[:, 0:n, :].rearrange("p t (h q) -> p t h q", h=2),
                                                         in0=BK[:, :, 0:n].rearrange("p q t -> p t q").unsqueeze(2).to_broadcast([128, n, 2, 64]),
                                                         in1=maskh[:].unsqueeze(1).unsqueeze(3).to_broadcast([128, n, 2, 64]), op=ALU.mult),
                 reads=[BK, maskh], writes=[LBK])
            def ypath_a(i_, Msrc):
                k.op(k.act, lambda: nc.scalar.copy(out=Mb[:], in_=Msrc[:]), reads=[Msrc], writes=[Mb])
                for j in range(4):
                    P.mm(PU, PU.t[0:64, j, :], LRb, LRb[:, i_, :], Mb, Mb[:, j * 512:(j + 1) * 512], True, True)
                k.op(k.act, lambda: nc.scalar.copy(out=ytmp[:], in_=PU.t[0:64, :, :].rearrange("p b c -> p (b c)")), reads=[PU], writes=[ytmp])
                k.op(k.pool, lambda: nc.gpsimd.tensor_tensor(out=ytmp[:].rearrange("p (g v) -> p g v", v=64), in0=ytmp[:].rearrange("p (g v) -> p g v", v=64),
                                                             in1=maskgb[0:64, :].unsqueeze(2).to_broadcast([64, 32, 64]), op=ALU.mult), reads=[ytmp, maskgb], writes=[ytmp])

            def ypath_b(i_):
                k.op(k.dve, lambda: nc.vector.tensor_reduce(out=yacc[:, :, i_], in_=ytmp[:].rearrange("p (g v) -> p v g", v=64), axis=AX.X, op=ALU.add),
                     reads=[ytmp], writes=[yacc])

            def vbd(i_):
                RV_ = RVr.next()
                k.op(k.pool, lambda: nc.gpsimd.tensor_tensor(out=RV_[64:128, :].rearrange("p (g v) -> p g v", v=64),
                                                             in0=VH[64:128, :, i_].unsqueeze(1).to_broadcast([64, 32, 64]),
                                                             in1=maskg[64:128, :].unsqueeze(2).to_broadcast([64, 32, 64]), op=ALU.mult),
                     reads=[VH, maskg], writes=[RV_])
                return RV_

            pend_b = None
            RV_next = vbd(0)
            for i in range(n):
                M = MM[cur[0]]; Mn = MM[1 - cur[0]]
                RV = RV_next
                for j in range(4):
                    P.mm(PU, PU.t[0:64, j, :], LU, LU[:, i, :], M, M[:, j * 512:(j + 1) * 512], True, True)
                k.op(k.dve, lambda: nc.vector.tensor_tensor(out=Mw[:].rearrange("p (g v) -> p g v", v=64), in0=M[:].rearrange("p (g v) -> p g v", v=64),
                                                            in1=wF[:, :, i].unsqueeze(2).to_broadcast([128, 32, 64]), op=ALU.mult),
                     reads=[M, wF], writes=[Mw])
                k.op(k.dve, lambda: nc.vector.tensor_tensor(out=RV[0:64, :].rearrange("p (g v) -> p g v", v=64),
                                                            in0=PU.t[0:64, :, :].rearrange("p b (g v) -> p (b g) v", v=64),
                                                            in1=maskg[0:64, :].unsqueeze(2).to_broadcast([64, 32, 64]), op=ALU.mult),
                     reads=[PU, maskg], writes=[RV])
                if pend_b is not None:
                    ypath_b(pend_b)
                    pend_b = None
                for j in range(4):
                    P.mm(PO, PO.t[:, j, :], LBK, LBK[:, i, :], RV, RV[:, j * 512:(j + 1) * 512], True, True)
                k.op(k.dve, lambda: nc.vector.tensor_tensor(out=Mn[:], in0=Mw[:], in1=PO.t[:, :, :].rearrange("p b c -> p (b c)"), op=ALU.add),
                     reads=[Mw, PO], writes=[Mn])
                cur[0] = 1 - cur[0]
                if i + 1 < n:
                    RV_next = vbd(i + 1)
                if i > 0:
                    ypath_a(i - 1, M)
                    pend_b = i - 1
            if pend_b is not None:
                ypath_b(pend_b)
            ypath_a(n - 1, MM[cur[0]])
            ypath_b(n - 1)
            yv = yacc[:, :, 0:n]
            ytv = yv.rearrange("r v t -> r t v")
            k.op(k.dve, lambda: nc.vector.tensor_reduce(out=st[:, 0, 0:n], in_=ytv, axis=AX.X, op=ALU.add), reads=[yacc], writes=[st])
            k.op(k.dve, lambda: nc.vector.tensor_tensor(out=t1[:, :, 0:n], in0=yv, in1=yv, op=ALU.mult), reads=[yacc], writes=[t1])
            k.op(k.dve, lambda: nc.vector.tensor_reduce(out=st[:, 1, 0:n], in_=t1[:, :, 0:n].rearrange("r v t -> r t v"), axis=AX.X, op=ALU.add), reads=[t1], writes=[st])
            k.op(k.dve, lambda: nc.vector.tensor_scalar(out=st[:, 0:2, 0:n], in0=st[:, 0:2, 0:n], scalar1=1.0 / 64, scalar2=None, op0=ALU.mult), reads=[st], writes=[st])
            k.op(k.dve, lambda: nc.vector.tensor_tensor(out=st[:, 2, 0:n], in0=st[:, 0, 0:n], in1=st[:, 0, 0:n], op=ALU.mult), reads=[st], writes=[st])
            k.op(k.dve, lambda: nc.vector.tensor_tensor(out=st[:, 3, 0:n], in0=st[:, 1, 0:n], in1=st[:, 2, 0:n], op=ALU.subtract), reads=[st], writes=[st])
            k.op(k.dve, lambda: nc.vector.tensor_scalar(out=st[:, 3, 0:n], in0=st[:, 3, 0:n], scalar1=GN_EPS, scalar2=None, op0=ALU.add), reads=[st], writes=[st])
            k.op(k.act, lambda: nc.scalar.activation(out=st[:, 3, 0:n], in_=st[:, 3, 0:n], func=AF.Sqrt), reads=[st], writes=[st])
            k.op(k.dve, lambda: nc.vector.reciprocal(out=st[:, 3, 0:n], in_=st[:, 3, 0:n]), reads=[st], writes=[st])
            k.op(k.dve, lambda: nc.vector.tensor_tensor(out=t1[:, :, 0:n], in0=yv, in1=st[:, 0, 0:n].unsqueeze(1).to_broadcast([64, 64, n]), op=ALU.subtract), reads=[yacc, st], writes=[t1])
            k.op(k.dve, lambda: nc.vector.tensor_tensor(out=t1[:, :, 0:n], in0=t1[:, :, 0:n], in1=st[:, 3, 0:n].unsqueeze(1).to_broadcast([64, 64, n]), op=ALU.mult), reads=[t1, st], writes=[t1])
            k.op(k.dve, lambda: nc.vector.tensor_tensor(out=t1[:, :, 0:n], in0=t1[:, :, 0:n], in1=gng[:].unsqueeze(2).to_broadcast([64, 64, n]), op=ALU.mult), reads=[t1, gng], writes=[t1])
            k.op(k.dve, lambda: nc.vector.tensor_tensor(out=t1[:, :, 0:n], in0=t1[:, :, 0:n], in1=gnb[:].unsqueeze(2).to_broadcast([64, 64, n]), op=ALU.add), reads=[t1, gnb], writes=[t1])
            k.op(k.pool, lambda: nc.gpsimd.tensor_tensor(out=t2[:, :, 0:n], in0=RH[:, :, 0:n], in1=KH[:, :, 0:n], op=ALU.mult), reads=[RH, KH], writes=[t2])
            k.op(k.pool, lambda: nc.gpsimd.tensor_tensor(out=t2[:, :, 0:n], in0=t2[:, :, 0:n], in1=rk[:].unsqueeze(2).to_broadcast([64, 64, n]), op=ALU.mult), reads=[t2, rk], writes=[t2])
            k.op(k.dve, lambda: nc.vector.tensor_reduce(out=st[:, 4, 0:n], in_=t2[:, :, 0:n].rearrange("r q t -> r t q"), axis=AX.X, op=ALU.add), reads=[t2], writes=[st])
            k.op(k.dve, lambda: nc.vector.tensor_tensor(out=t2[:, :, 0:n], in0=VH[0:64, :, 0:n], in1=st[:, 4, 0:n].unsqueeze(1).to_broadcast([64, 64, n]), op=ALU.mult), reads=[VH, st], writes=[t2])
            k.op(k.dve, lambda: nc.vector.tensor_tensor(out=t1[:, :, 0:n], in0=t1[:, :, 0:n], in1=t2[:, :, 0:n], op=ALU.add), reads=[t1, t2], writes=[t1])
            P.st(YC, hm(YC, c, n), t1, t1[:, :, 0:n])

        k.op(k.dve, lambda: nc.vector.memset(MM[0][:], 0.0), writes=[MM[0]])
        for c in range(0, SEQ, TCH):
            chunk(c, TCH)
        store_state(P.outs["wkv_p"], P.outs["wkv_p"].t)
        for b in range(4):
            load_state(b)
            chunk(SEQ + 4 * b, 4)
            store_state(P.outs["wkv_s"], P.outs["wkv_s"].t[b])


def phase_rwkv_out(P):
    nc, k = P.nc, P.k
    YC, GC, MIX = P.scr["YC"], P.scr["GC"], P.scr["MIX"]
    with PhaseStack() as es:
        actT = P.sb(es, "actT", [128, KC, T_ALL], BF16)
        with PhaseStack() as es2:
            yr = P.sbring(es2, "yr", [128, T_ALL], F32, 2)
            gr = P.sbring(es2, "gr", [128, T_ALL], F32, 2)
            for g in range(KC):
                yt, gt = yr.next(), gr.next()
                P.ld(yt, yt[:], YC, YC.t[g * 128:(g + 1) * 128, :])
                P.ld(gt, gt[:], GC, GC.t[g * 128:(g + 1) * 128, :])
                k.op(k.dve, lambda: nc.vector.tensor_tensor(out=actT[:, g, :], in0=yt[:], in1=gt[:], op=ALU.mult), reads=[yt, gt], writes=[actT])
        stg = P.sbring(es, "stgB", [128, 256], F32, 4)
        P.wsrc = P.ins["rw_w_o"]

        def emit(c0, cn, t0, tn, pb):
            s_ = stg.next()
            P.evac(s_, s_[0:tn, 0:cn], pb, pb[0:tn, 0:cn])
            P.st(MIX, MIX.t[t0:t0 + tn, c0:c0 + cn], s_, s_[0:tn, 0:cn])
        proj_pass(P, actT, P.ins["rw_w_o"].t, D, "B", emit)


def phase_rwkv(P):
    phase_tables_bf16(P, 1)
    phase_rwkv_prep(P)
    phase_rwkv_scan(P)
    phase_rwkv_out(P)
    X1 = P.scr["X1"]
    phase_res_ln(P, lambda t0, tn: (X1, X1.t[t0:t0 + tn, :]), P.scr["MIX"], 2, P.scr["XM"], P.scr["XMT"], None)
    phase_peer_q(P, 1, P.scr["XMT"], P.scr["QPT"])
    yp, ys = P.outs["y_p"], P.outs["y_s"]
    dst = lambda t0, tn: (yp, yp.t[t0:t0 + tn, :]) if t0 < SEQ else (ys, ys.t[:, :])
    phase_peer_main(P, 1, P.scr["XM"], P.scr["XMB"], P.scr["QPT"], 3, dst, None)

def relpos_bucket_np(dist):
    n = np.maximum(np.asarray(dist, np.int64), 0)
    nf = np.maximum(n, 1).astype(np.float32)
    large = 16 + (np.log(nf / np.float32(16)) / np.float32(np.log(2048 / 16)) * np.float32(16)).astype(np.int32)
    return np.where(n < 16, n, np.minimum(large, 31))


def host_consts():
    c = {"ident": np.eye(128, dtype=np.float32)}
    sel = np.zeros((32, 3, 384), np.float32)
    msk = np.zeros((32, 384), np.float32)
    for di, d in enumerate(DILS):
        for j in range(127, 256):
            s_ = 255 - j
            sel[relpos_bucket_np(d * s_), di, j] = 1.0
    msk[:, 127:256] = 1.0
    c["sel"] = sel
    c["selmask"] = msk
    c["iota16"] = np.tile(np.arange(16, dtype=np.float32)[None, :], (128, 1))
    ohw = np.zeros((128, 255), np.float32); ohw[:, 127] = 1.0
    c["ohw"] = ohw
    ohs = np.zeros((128, 31), np.float32); ohs[:, 15] = 1.0
    c["ohs"] = ohs
    bones = np.zeros((128, 128), np.float32); bones[:64, :64] = 1.0; bones[64:, 64:] = 1.0
    c["bones"] = bones
    p_ = np.arange(128)
    c["maskp"] = (p_[:, None] // 64 == np.arange(2)[None, :]).astype(np.float32)
    c["maskh"] = (((p_ % 64) % 2)[:, None] == np.arange(2)[None, :]).astype(np.float32)
    c["maskg"] = (((p_ % 64) // 2)[:, None] == np.arange(32)[None, :]).astype(np.float32)
    return c


def build(debug=False, phases=("l0", "sample", "l1")):
    P = Prog(debug=debug)
    nc = P.nc
    P.inp("xp", [SEQ, D]); P.inp("xs", [NS, D])
    P.inp("ca_k", [4, 128, 512]); P.inp("ca_v", [4, 128, 512])
    P.inp("cb_k", [4, 2048, 1536]); P.inp("cb_v", [4, 2048, 1536])
    P.inp("st_shift", [4, D]); P.inp("st_wkv", [4, 64, 64, 64])
    P.inp("w_in_att", [D, 8192]); P.inp("w_out_att", [D, D])
    P.inp("att_sinks", [1, 20]); P.inp("rel_bias", [32, 32])
    P.inp("peer_w_q", [2, D, 2048]); P.inp("peer_sub_keys", [2, 8, 2, 128, 128])
    P.inp("peer_u", [2, 16384, D]); P.inp("peer_v", [2, 16384, D])
    P.inp("ln_g", [4, D]); P.inp("ln_b", [4, D])
    P.inp("rw_mu", [6, D]); P.inp("rw_w_rkv", [3, D, D]); P.inp("rw_w0", [1, D]); P.inp("rw_w1", [D, 128]); P.inp("rw_w2", [128, D])
    P.inp("rw_a0", [1, D]); P.inp("rw_a1", [D, 128]); P.inp("rw_a2", [128, D]); P.inp("rw_g1", [D, 480]); P.inp("rw_g2", [480, D])
    P.inp("rw_k_k", [1, D]); P.inp("rw_k_a", [1, D]); P.inp("rw_r_k", [64, 64]); P.inp("rw_gn_g", [1, D]); P.inp("rw_gn_b", [1, D])
    P.inp("rw_w_o", [D, D])
    P.inp("ident", [128, 128]); P.inp("sel", [32, 3, 384]); P.inp("selmask", [32, 384])
    P.inp("iota16", [128, 16]); P.inp("ohw", [128, 255]); P.inp("ohs", [128, 31])
    P.inp("bones", [128, 128]); P.inp("maskp", [128, 2]); P.inp("maskh", [128, 2]); P.inp("maskg", [128, 32])
    P.out("y_p", [SEQ, D]); P.out("y_s", [NS, D])
    P.out("a_k_p", [128, 512]); P.out("a_v_p", [128, 512])
    P.out("b_k_p", [SEQ, 1536]); P.out("b_v_p", [SEQ, 1536])
    P.out("shift_p", [1, D]); P.out("wkv_p", [64, 64, 64])
    P.out("a_k_s", [4, 128, 512]); P.out("a_v_s", [4, 128, 512])
    P.out("b_k_s", [4, 2048, 1536]); P.out("b_v_s", [4, 2048, 1536])
    P.out("shift_s", [4, D]); P.out("wkv_s", [4, 64, 64, 64])
    P.scratch("QK", [48, 128, SEQ], BF16)
    P.scratch("VA", [SEQ, 512], F32)
    P.scratch("PS", [NS, 8192], F32)
    P.scratch("GGD", [3, 32, 384], F32)
    P.scratch("OT", [32, 128, T_ALL], BF16)
    P.scratch("MIX", [T_ALL, D], F32)
    P.scratch("XM", [T_ALL, D], F32)
    P.scratch("XMB", [T_ALL, D], BF16)
    P.scratch("XMT", [KC, 128, T_ALL], BF16)
    P.scratch("QPT", [16, 128, T_ALL], BF16)
    P.scratch("X1", [T_ALL, D], F32, dbg=True)
    P.scratch("X1T", [KC, 128, T_ALL], BF16)
    for l in range(2):
        P.scratch("UB%d" % l, [16384, D], BF16)
        P.scratch("VB%d" % l, [16384, D], BF16)
    for n_ in ("RC", "WC", "AC", "BC", "KFC", "VC", "GC", "A1C", "YC"):
        P.scratch(n_, [D, T_ALL], F32, dbg=True)
    k = P.k
    with ExitStack() as es:
        P.psum_all = es.enter_context(nc.psum_tensor("psall", [128, 8, 512], F32))
        P.psum = [Buf("ps%d" % i, P.psum_all[:, i, :]) for i in range(8)]
        phase_tables_bf16(P, 0)
        phase_l0a(P)
        phase_sample_caches(P)
        phase_bias_setup(P)
        phase_l0b(P)
        phase_sample_attn(P)
        phase_outproj(P, P.scr["OT"], P.ins["w_out_att"], P.scr["MIX"])
        xp, xs = P.ins["xp"], P.ins["xs"]
        xres = lambda t0, tn: (xp, xp.t[t0:t0 + tn, :]) if t0 < SEQ else (xs, xs.t[:, :])
        phase_res_ln(P, xres, P.scr["MIX"], 0, P.scr["XM"], P.scr["XMT"], None)
        phase_peer_q(P, 0, P.scr["XMT"], P.scr["QPT"])
        phase_peer_main(P, 0, P.scr["XM"], P.scr["XMB"], P.scr["QPT"], 1, P.scr["X1"], P.scr["X1T"])
        X1 = P.scr["X1"]
        o = P.outs["shift_p"]
        k.dma(k.sp, lambda: nc.sync.dma_start(out=o.t, in_=X1.t[SEQ - 1:SEQ, :]), reads=[X1], writes=[o], acc=True)
        o2 = P.outs["shift_s"]
        k.dma(k.sp, lambda: nc.sync.dma_start(out=o2.t, in_=X1.t[SEQ + 3:SEQ + NS:4, :]), reads=[X1], writes=[o2], acc=True)
        if "l1" in phases:
            phase_rwkv(P)
        P.k.finish()
    return P


_CACHE = {}


def kernel(**inputs):
    x_prompt = np.ascontiguousarray(inputs["x_prompt"], dtype=np.float32)
    n_cores = 8
    if "P" not in _CACHE:
        _CACHE["P"] = build()
    P = _CACHE["P"]
    consts = host_consts()
    shared = {}
    for n in ["w_in_att", "w_out_att", "rel_bias", "peer_w_q", "peer_sub_keys", "peer_u", "peer_v",
              "rw_mu", "rw_w_rkv", "rw_w1", "rw_w2", "rw_a1", "rw_a2", "rw_g1", "rw_g2", "rw_r_k", "rw_w_o"]:
        shared[n] = np.ascontiguousarray(inputs[n], dtype=np.float32)
    for n in ["rw_w0", "rw_a0", "rw_k_k", "rw_k_a", "rw_gn_g", "rw_gn_b"]:
        shared[n] = np.ascontiguousarray(inputs[n], dtype=np.float32).reshape(1, D)
    shared["att_sinks"] = np.ascontiguousarray(inputs["att_sinks"], dtype=np.float32).reshape(1, 20)
    shared["ln_g"] = np.ascontiguousarray(inputs["ln_g"], dtype=np.float32).reshape(4, D)
    shared["ln_b"] = np.ascontiguousarray(inputs["ln_b"], dtype=np.float32).reshape(4, D)
    shared.update(consts)
    xps = [np.ascontiguousarray(x_prompt[b]) for b in range(4)]
    in_maps = []
    for c in range(n_cores):
        sl = slice(4 * c, 4 * c + 4)
        m = dict(shared)
        m["xp"] = xps[c // 2]
        m["xs"] = np.ascontiguousarray(inputs["x_sample"][sl], dtype=np.float32).reshape(NS, D)
        m["ca_k"] = np.ascontiguousarray(inputs["cache_a_k"][sl], dtype=np.float32).reshape(4, 128, 512)
        m["ca_v"] = np.ascontiguousarray(inputs["cache_a_v"][sl], dtype=np.float32).reshape(4, 128, 512)
        m["cb_k"] = np.ascontiguousarray(inputs["cache_b_k"][sl], dtype=np.float32).reshape(4, 2048, 1536)
        m["cb_v"] = np.ascontiguousarray(inputs["cache_b_v"][sl], dtype=np.float32).reshape(4, 2048, 1536)
        m["st_shift"] = np.ascontiguousarray(inputs["state_shift"][sl], dtype=np.float32)
        m["st_wkv"] = np.ascontiguousarray(inputs["state_wkv"][sl], dtype=np.float32)
        m = {kk: v for kk, v in m.items() if kk in P.ins}
        in_maps.append(m)
    res = run_bass_kernel_spmd(P.nc, in_maps, core_ids=list(range(n_cores)))
    R = res.results
    ev = [R[2 * b] for b in range(4)]
    y_p = np.stack([r["y_p"] for r in ev])
    y_s = np.concatenate([r["y_s"].reshape(4, 4, D) for r in R])
    a_k_p = np.stack([r["a_k_p"].reshape(128, 4, 128) for r in ev])
    a_v_p = np.stack([r["a_v_p"].reshape(128, 4, 128) for r in ev])
    b_k_p = np.stack([r["b_k_p"].reshape(SEQ, 12, 128) for r in ev])
    b_v_p = np.stack([r["b_v_p"].reshape(SEQ, 12, 128) for r in ev])
    shift_p = np.concatenate([r["shift_p"] for r in ev])
    wkv_p = np.stack([r["wkv_p"] for r in ev])
    a_k_s = np.concatenate([r["a_k_s"].reshape(4, 128, 4, 128) for r in R])
    a_v_s = np.concatenate([r["a_v_s"].reshape(4, 128, 4, 128) for r in R])
    b_k_s = np.concatenate([r["b_k_s"].reshape(4, 2048, 12, 128) for r in R])
    b_v_s = np.concatenate([r["b_v_s"].reshape(4, 2048, 12, 128) for r in R])
    shift_s = np.concatenate([r["shift_s"] for r in R])
    wkv_s = np.concatenate([r["wkv_s"] for r in R])
    return (y_p, y_s, a_k_p, a_v_p, b_k_p, b_v_p, shift_p, wkv_p, a_k_s, a_v_s, b_k_s, b_v_s, shift_s, wkv_s)
```
